# Optimizing a Trainium2 kernel written in Bass

```python
import math
import jax
import jax.numpy as jnp
from jax import lax
import numpy as np

D_MODEL = 2048
BATCH = 16
SEQ = 2048
DEPTH = 2

HEAD_DIM = 128
MIX_HEADS = D_MODEL // (2 * HEAD_DIM)

DN_HEADS = MIX_HEADS
DN_DK = HEAD_DIM
DN_DV = HEAD_DIM
DN_CONV = 4
DN_CHUNK = 64

NSA_HEADS = MIX_HEADS
NSA_GROUPS = 2
NSA_HPG = NSA_HEADS // NSA_GROUPS
NSA_DH = HEAD_DIM
CMP_LEN = 32
CMP_STRIDE = 16
CMP_HIDDEN = 256
SEL_BLOCK = 64
SEL_TOPK = 16
SEL_Q_CHUNK = 16
WIN = 512
Q_BLOCK = 128
ROPE_THETA = 10000.0

D_FF = -(-8 * D_MODEL // (3 * 256)) * 256
PLE_DIM = 256
ALPHA = (2.0 * DEPTH) ** 0.25
BETA = (8.0 * DEPTH) ** -0.25
LN_EPS = 1e-5
NORM_EPS = 1e-6
NEG_INF = -1e30

DN_QK = DN_HEADS * DN_DK
DN_VW = DN_HEADS * DN_DV
NSA_QW = NSA_HEADS * NSA_DH
NSA_KVW = NSA_GROUPS * NSA_DH
IN_SPLITS = (DN_QK, DN_QK, DN_VW, DN_VW, DN_HEADS, DN_HEADS,
             NSA_QW, NSA_KVW, NSA_KVW, NSA_KVW, NSA_KVW, NSA_KVW, NSA_KVW,
             3 * NSA_HEADS, D_MODEL, D_MODEL)
N_IN = sum(IN_SPLITS)

kernel_name = 'hybrid_deltanet_nsa_deepnorm_block'


def layer_norm(x, g, b):
    xf = x.astype(jnp.float32)
    mu = jnp.mean(xf, axis=-1, keepdims=True)
    var = jnp.mean(jnp.square(xf - mu), axis=-1, keepdims=True)
    y = (xf - mu) * lax.rsqrt(var + LN_EPS)
    return (y * g.astype(jnp.float32) + b.astype(jnp.float32)).astype(x.dtype)


def l2norm(x):
    xf = x.astype(jnp.float32)
    return xf * lax.rsqrt(jnp.sum(xf * xf, axis=-1, keepdims=True) + NORM_EPS)


def rope(x):
    s_len, dh = x.shape[1], x.shape[-1]
    half = dh // 2
    inv_freq = ROPE_THETA ** (-jnp.arange(half, dtype=jnp.float32) / half)
    ang = jnp.arange(s_len, dtype=jnp.float32)[:, None] * inv_freq[None, :]
    cos = jnp.cos(ang)[:, None, :]
    sin = jnp.sin(ang)[:, None, :]
    xf = x.astype(jnp.float32)
    x1, x2 = xf[..., :half], xf[..., half:]
    return jnp.concatenate([x1 * cos - x2 * sin, x1 * sin + x2 * cos], axis=-1).astype(x.dtype)


def masked_softmax(s, mask):
    p = jax.nn.softmax(jnp.where(mask, s.astype(jnp.float32), NEG_INF), axis=-1)
    return p * mask


def causal_conv_silu(u, w):
    c = u.shape[-1]
    y = lax.conv_general_dilated(
        u, w[:, None, :].astype(u.dtype), window_strides=(1,),
        padding=[(w.shape[0] - 1, 0)], dimension_numbers=('NWC', 'WIO', 'NWC'),
        feature_group_count=c)
    return jax.nn.silu(y)


def chunk_gated_delta_rule(q, k, v, g, beta):
    b_, s_len, h, dk = q.shape
    dv = v.shape[-1]
    c = DN_CHUNK
    n = s_len // c

    def chunks(t):
        return t.reshape(b_, n, c, h, t.shape[-1]).transpose(0, 3, 1, 2, 4)

    q, k, v = chunks(q), chunks(k), chunks(v)
    g = g.reshape(b_, n, c, h).transpose(0, 3, 1, 2)
    beta = beta.reshape(b_, n, c, h).transpose(0, 3, 1, 2)
    g = jnp.cumsum(g, axis=-1)
    idx = jnp.arange(c)
    causal = idx[:, None] >= idx[None, :]
    strict = idx[:, None] > idx[None, :]
    decay = jnp.exp(jnp.where(causal, g[..., :, None] - g[..., None, :], -jnp.inf))
    k_beta = k * beta[..., None]
    lower = jnp.where(strict, jnp.einsum('bhncd,bhnmd->bhncm', k_beta, k) * decay, 0.0)
    eye = jnp.broadcast_to(jnp.eye(c, dtype=jnp.float32), lower.shape)
    t_inv = lax.linalg.triangular_solve(lower + eye, eye, left_side=True, lower=True)
    u = jnp.einsum('bhncm,bhnmv->bhncv', t_inv, v * beta[..., None])
    w = jnp.einsum('bhncm,bhnmk->bhnck', t_inv, k_beta * jnp.exp(g)[..., None])
    qk = jnp.where(causal, jnp.einsum('bhncd,bhnmd->bhncm', q, k) * decay, 0.0)
    g_last = g[..., -1]
    q_dec = q * jnp.exp(g)[..., None]
    k_dec = k * jnp.exp(g_last[..., None] - g)[..., None]

    def step(state, xs):
        u_i, w_i, qk_i, qd_i, kd_i, gl_i = xs
        v_new = u_i - jnp.einsum('bhck,bhkv->bhcv', w_i, state)
        o_i = jnp.einsum('bhck,bhkv->bhcv', qd_i, state) + jnp.einsum('bhcm,bhmv->bhcv', qk_i, v_new)
        state = state * jnp.exp(gl_i)[..., None, None] + jnp.einsum('bhck,bhcv->bhkv', kd_i, v_new)
        return state, o_i

    xs = (jnp.moveaxis(u, 2, 0), jnp.moveaxis(w, 2, 0), jnp.moveaxis(qk, 2, 0),
          jnp.moveaxis(q_dec, 2, 0), jnp.moveaxis(k_dec, 2, 0), jnp.moveaxis(g_last, 2, 0))
    state0 = jnp.zeros((b_, h, dk, dv), jnp.float32)
    _, o = lax.scan(step, state0, xs)
    return o.transpose(1, 0, 3, 2, 4).reshape(b_, s_len, h, dv)


def gated_deltanet(q, k, v, z, b, a, conv_w, a_log, dt_bias, norm_w):
    b_, s_len, _ = q.shape
    f32 = jnp.float32
    qkv = causal_conv_silu(jnp.concatenate([q, k, v], axis=-1), conv_w)
    q, k, v = jnp.split(qkv, [DN_QK, 2 * DN_QK], axis=-1)
    q = l2norm(q.reshape(b_, s_len, DN_HEADS, DN_DK)) * (DN_DK ** -0.5)
    k = l2norm(k.reshape(b_, s_len, DN_HEADS, DN_DK))
    v = v.reshape(b_, s_len, DN_HEADS, DN_DV).astype(f32)
    beta = jax.nn.sigmoid(b.astype(f32))
    g = -jnp.exp(a_log.astype(f32)) * jax.nn.softplus(a.astype(f32) + dt_bias.astype(f32))
    o = chunk_gated_delta_rule(q, k, v, g, beta)
    o = o * lax.rsqrt(jnp.mean(o * o, axis=-1, keepdims=True) + NORM_EPS) * norm_w.astype(f32)
    o = o * jax.nn.silu(z.astype(f32).reshape(b_, s_len, DN_HEADS, DN_DV))
    return o.reshape(b_, s_len, DN_VW).astype(z.dtype)


def compress_blocks(t, pe, w1, w2):
    b_, g_, s_len, dh = t.shape
    r = CMP_LEN // CMP_STRIDE
    nch = s_len // CMP_STRIDE
    nc = nch - r + 1
    ch = t.reshape(b_, g_, nch, CMP_STRIDE, dh)
    blocks = jnp.concatenate([ch[:, :, i:i + nc] for i in range(r)], axis=3)
    blocks = (blocks + pe.astype(t.dtype)).reshape(b_, g_, nc, CMP_LEN * dh)
    return jax.nn.silu(blocks @ w1) @ w2


def native_sparse_attention(q, k_c, v_c, k_s, v_s, k_w, v_w, gate,
                            pe_k, w1_k, w2_k, pe_v, w1_v, w2_v):
    b_, s_len, _ = q.shape
    dt = q.dtype
    f32 = jnp.float32
    t = jnp.arange(s_len)
    q = rope(q.reshape(b_, s_len, NSA_HEADS, NSA_DH)) * (NSA_DH ** -0.5)
    q = q.reshape(b_, s_len, NSA_GROUPS, NSA_HPG, NSA_DH).transpose(0, 2, 3, 1, 4)

    def heads(u, rotate):
        u = u.reshape(b_, s_len, NSA_GROUPS, NSA_DH)
        if rotate:
            u = rope(u)
        return u.transpose(0, 2, 1, 3)

    k_c, k_s, k_w = heads(k_c, True), heads(k_s, True), heads(k_w, True)
    v_c, v_s, v_w = heads(v_c, False), heads(v_s, False), heads(v_w, False)

    kc = compress_blocks(k_c, pe_k, w1_k, w2_k)
    vc = compress_blocks(v_c, pe_v, w1_v, w2_v)
    nc = kc.shape[2]
    cmp_start = jnp.arange(nc) * CMP_STRIDE
    cmp_mask = (cmp_start + CMP_LEN - 1)[None, :] <= t[:, None]
    p_cmp = masked_softmax(jnp.einsum('bghsd,bgnd->bghsn', q, kc), cmp_mask)
    o_cmp = jnp.einsum('bghsn,bgnd->bghsd', p_cmp, vc.astype(f32))

    ns = s_len // SEL_BLOCK
    sel_start = jnp.arange(ns) * SEL_BLOCK
    overlap = ((cmp_start[:, None] <= sel_start[None, :] + SEL_BLOCK - 1)
               & (cmp_start[:, None] + CMP_LEN - 1 >= sel_start[None, :])).astype(f32)
    imp = jnp.einsum('bghsn,nj->bgsj', p_cmp, overlap)
    cur = t // SEL_BLOCK
    j = jnp.arange(ns)
    forced = (j[None, :] == 0) | (j[None, :] == cur[:, None]) | (j[None, :] == cur[:, None] - 1)
    causal_blk = j[None, :] <= cur[:, None]
    imp = jnp.where(forced, jnp.inf, jnp.where(causal_blk, imp, -jnp.inf))
    n_top = min(SEL_TOPK, ns)
    _, sel_idx = lax.top_k(imp, n_top)

    kb = k_s.reshape(b_, NSA_GROUPS, ns, SEL_BLOCK, NSA_DH)
    vb = v_s.reshape(b_, NSA_GROUPS, ns, SEL_BLOCK, NSA_DH)
    nq = s_len // SEL_Q_CHUNK
    q_ch = q.reshape(b_, NSA_GROUPS, NSA_HPG, nq, SEL_Q_CHUNK, NSA_DH).transpose(3, 0, 1, 2, 4, 5)
    idx_ch = sel_idx.reshape(b_, NSA_GROUPS, nq, SEL_Q_CHUNK, n_top).transpose(2, 0, 1, 3, 4)
    pos_ch = t.reshape(nq, SEL_Q_CHUNK)
    bi = jnp.arange(b_)[:, None, None, None]
    gi = jnp.arange(NSA_GROUPS)[None, :, None, None]
    offs = jnp.arange(SEL_BLOCK)

    def sel_chunk(args):
        qc, ic, pc = args
        kg = kb[bi, gi, ic].reshape(b_, NSA_GROUPS, SEL_Q_CHUNK, n_top * SEL_BLOCK, NSA_DH)
        vg = vb[bi, gi, ic].reshape(b_, NSA_GROUPS, SEL_Q_CHUNK, n_top * SEL_BLOCK, NSA_DH)
        kpos = (ic[..., None] * SEL_BLOCK + offs).reshape(b_, NSA_GROUPS, SEL_Q_CHUNK, n_top * SEL_BLOCK)
        mask = (kpos <= pc[None, None, :, None])[:, :, None]
        p = masked_softmax(jnp.einsum('bghqd,bgqmd->bghqm', qc, kg), mask)
        return jnp.einsum('bghqm,bgqmd->bghqd', p, vg.astype(f32))

    o_sel = lax.map(sel_chunk, (q_ch, idx_ch, pos_ch))
    o_sel = o_sel.transpose(1, 2, 3, 0, 4, 5).reshape(b_, NSA_GROUPS, NSA_HPG, s_len, NSA_DH)

    span = WIN + Q_BLOCK
    kp = jnp.pad(k_w, ((0, 0), (0, 0), (WIN, 0), (0, 0)))
    vp = jnp.pad(v_w, ((0, 0), (0, 0), (WIN, 0), (0, 0)))
    nqb = s_len // Q_BLOCK
    q_blk = q.reshape(b_, NSA_GROUPS, NSA_HPG, nqb, Q_BLOCK, NSA_DH).transpose(3, 0, 1, 2, 4, 5)

    def win_block(args):
        qb, jb = args
        start = jb * Q_BLOCK
        kw_ = lax.dynamic_slice_in_dim(kp, start, span, axis=2)
        vw_ = lax.dynamic_slice_in_dim(vp, start, span, axis=2)
        qpos = start + jnp.arange(Q_BLOCK)
        kpos = start - WIN + jnp.arange(span)
        diff = qpos[:, None] - kpos[None, :]
        mask = (diff >= 0) & (diff < WIN) & (kpos[None, :] >= 0)
        p = masked_softmax(jnp.einsum('bghqd,bgkd->bghqk', qb, kw_), mask)
        return jnp.einsum('bghqk,bgkd->bghqd', p, vw_.astype(f32))

    o_win = lax.map(win_block, (q_blk, jnp.arange(nqb)))
    o_win = o_win.transpose(1, 2, 3, 0, 4, 5).reshape(b_, NSA_GROUPS, NSA_HPG, s_len, NSA_DH)

    gt = jax.nn.sigmoid(gate.astype(f32)).reshape(b_, s_len, NSA_GROUPS, NSA_HPG, 3).transpose(0, 2, 3, 1, 4)
    o = gt[..., 0:1] * o_cmp + gt[..., 1:2] * o_sel + gt[..., 2:3] * o_win
    return o.transpose(0, 3, 1, 2, 4).reshape(b_, s_len, NSA_QW).astype(dt)


def hybrid_layer(x, p_i, w_in, conv_w, a_log, dt_bias, dn_norm_w,
                 pe_k, w1_k, w2_k, pe_v, w1_v, w2_v, w_a, w_b, w_out, ln1_g, ln1_b,
                 w_gate, w_up, w_down, w_ple, w_ple_gate, ln2_g, ln2_b):
    points = np.cumsum(IN_SPLITS)[:-1].tolist()
    h = x @ w_in
    (dn_q, dn_k, dn_v, dn_z, dn_b, dn_a, nsa_q, k_c, v_c, k_s, v_s, k_w, v_w,
     nsa_gate, merge_a, merge_b) = jnp.split(h, points, axis=-1)
    o_a = gated_deltanet(dn_q, dn_k, dn_v, dn_z, dn_b, dn_a, conv_w, a_log, dt_bias, dn_norm_w)
    o_b = native_sparse_attention(nsa_q, k_c, v_c, k_s, v_s, k_w, v_w, nsa_gate,
                                  pe_k, w1_k, w2_k, pe_v, w1_v, w2_v)
    mixed = jax.nn.sigmoid(merge_a) * (o_a @ w_a) + jax.nn.sigmoid(merge_b) * (o_b @ w_b)
    x = layer_norm(ALPHA * x + mixed @ w_out, ln1_g, ln1_b)
    ffn = (jax.nn.silu(x @ w_gate) * (x @ w_up)) @ w_down
    ple = (p_i @ w_ple) * jax.nn.sigmoid(x @ w_ple_gate)
    return layer_norm(ALPHA * x + ffn + ple, ln2_g, ln2_b)


def setup_inputs(seed: int = 0) -> dict:
    key = jax.random.key(seed)
    ks = jax.random.split(key, 32)
    f32 = jnp.float32
    L = DEPTH

    def nrm(k, shape, fan_in, scale=1.0):
        return jax.random.normal(k, shape, f32) * (scale * fan_in ** -0.5)

    def gain(k, shape):
        return 1.0 + 0.02 * jax.random.normal(k, shape, f32)

    def small(k, shape):
        return 0.02 * jax.random.normal(k, shape, f32)

    x = jax.random.normal(ks[0], (BATCH, SEQ, D_MODEL), f32)
    p = jax.random.normal(ks[1], (DEPTH, BATCH, SEQ, PLE_DIM), f32)
    w_in = nrm(ks[2], (L, D_MODEL, N_IN), D_MODEL)
    dn_conv_w = nrm(ks[3], (L, DN_CONV, 2 * DN_QK + DN_VW), DN_CONV)
    dn_a_log = jnp.log(jax.random.uniform(ks[4], (L, DN_HEADS), f32, 1.0, 16.0))
    dt = jnp.exp(jax.random.uniform(ks[5], (L, DN_HEADS), f32, math.log(1e-3), math.log(1e-1)))
    dn_dt_bias = dt + jnp.log(-jnp.expm1(-dt))
    dn_norm_w = gain(ks[6], (L, DN_DV))
    cmp_pe_k = small(ks[7], (L, CMP_LEN, NSA_DH))
    cmp_w1_k = nrm(ks[8], (L, CMP_LEN * NSA_DH, CMP_HIDDEN), CMP_LEN * NSA_DH)
    cmp_w2_k = nrm(ks[9], (L, CMP_HIDDEN, NSA_DH), CMP_HIDDEN)
    cmp_pe_v = small(ks[10], (L, CMP_LEN, NSA_DH))
    cmp_w1_v = nrm(ks[11], (L, CMP_LEN * NSA_DH, CMP_HIDDEN), CMP_LEN * NSA_DH)
    cmp_w2_v = nrm(ks[12], (L, CMP_HIDDEN, NSA_DH), CMP_HIDDEN)
    w_branch_a = nrm(ks[13], (L, DN_VW, D_MODEL), DN_VW, BETA)
    w_branch_b = nrm(ks[14], (L, NSA_QW, D_MODEL), NSA_QW, BETA)
    w_out = nrm(ks[15], (L, D_MODEL, D_MODEL), D_MODEL, BETA)
    ln1_g = gain(ks[16], (L, D_MODEL))
    ln1_b = small(ks[17], (L, D_MODEL))
    w_ffn_gate = nrm(ks[18], (L, D_MODEL, D_FF), D_MODEL)
    w_ffn_up = nrm(ks[19], (L, D_MODEL, D_FF), D_MODEL)
    w_ffn_down = nrm(ks[20], (L, D_FF, D_MODEL), D_FF, BETA)
    w_ple = nrm(ks[21], (L, PLE_DIM, D_MODEL), PLE_DIM, BETA)
    w_ple_gate = nrm(ks[22], (L, D_MODEL, D_MODEL), D_MODEL)
    ln2_g = gain(ks[23], (L, D_MODEL))
    ln2_b = small(ks[24], (L, D_MODEL))
    return {'x': x, 'p': p, 'w_in': w_in, 'dn_conv_w': dn_conv_w, 'dn_a_log': dn_a_log,
            'dn_dt_bias': dn_dt_bias, 'dn_norm_w': dn_norm_w,
            'cmp_pe_k': cmp_pe_k, 'cmp_w1_k': cmp_w1_k, 'cmp_w2_k': cmp_w2_k,
            'cmp_pe_v': cmp_pe_v, 'cmp_w1_v': cmp_w1_v, 'cmp_w2_v': cmp_w2_v,
            'w_branch_a': w_branch_a, 'w_branch_b': w_branch_b, 'w_out': w_out,
            'ln1_g': ln1_g, 'ln1_b': ln1_b, 'w_ffn_gate': w_ffn_gate, 'w_ffn_up': w_ffn_up,
            'w_ffn_down': w_ffn_down, 'w_ple': w_ple, 'w_ple_gate': w_ple_gate,
            'ln2_g': ln2_g, 'ln2_b': ln2_b}


def reference(x, p, w_in, dn_conv_w, dn_a_log, dn_dt_bias, dn_norm_w,
              cmp_pe_k, cmp_w1_k, cmp_w2_k, cmp_pe_v, cmp_w1_v, cmp_w2_v,
              w_branch_a, w_branch_b, w_out, ln1_g, ln1_b,
              w_ffn_gate, w_ffn_up, w_ffn_down, w_ple, w_ple_gate, ln2_g, ln2_b):
    for i in range(DEPTH):
        x = hybrid_layer(x, p[i], w_in[i], dn_conv_w[i], dn_a_log[i], dn_dt_bias[i], dn_norm_w[i],
                         cmp_pe_k[i], cmp_w1_k[i], cmp_w2_k[i], cmp_pe_v[i], cmp_w1_v[i], cmp_w2_v[i],
                         w_branch_a[i], w_branch_b[i], w_out[i], ln1_g[i], ln1_b[i],
                         w_ffn_gate[i], w_ffn_up[i], w_ffn_down[i], w_ple[i], w_ple_gate[i],
                         ln2_g[i], ln2_b[i])
    return x
```

```python
import math
from contextlib import ExitStack
import numpy as np
import ml_dtypes
import concourse.bass as bass
import concourse.mybir as mybir
from concourse.bass_utils import run_bass_kernel_spmd

F32 = mybir.dt.float32
BF16 = mybir.dt.bfloat16
I32 = mybir.dt.int32
U32 = mybir.dt.uint32
AF = mybir.ActivationFunctionType
ALU = mybir.AluOpType
AX = mybir.AxisListType

SAME_ENGINE_SYNC = True
NDSEM = 90
NDSEM_SW = 30

D = 2048
T = 2048
DEPTH = 2
NSEQ = 2
NIN = 10792
DFF = 5632
ALPHA = (2.0 * DEPTH) ** 0.25
LN_EPS = 1e-5
NORM_EPS = 1e-6
O_DQ, O_DK, O_DV, O_DZ, O_DB, O_DA = 0, 1024, 2048, 3072, 4096, 4104
O_NQ, O_KC, O_VC, O_KS, O_VS, O_KW, O_VW, O_GT, O_MA, O_MB = 4112, 5136, 5392, 5648, 5904, 6160, 6416, 6672, 6696, 8744


class Buf:
    __slots__ = ("name", "w", "r", "dsem", "dcount", "ps")

    def __init__(self, name, ps=False):
        self.name = name
        self.ps = ps
        self.w = None
        self.r = {}
        self.dsem = None
        self.dcount = 0


class V:
    __slots__ = ("b", "ap")

    def __init__(self, b, ap):
        self.b = b
        self.ap = ap

    def __getitem__(self, idx):
        return V(self.b, self.ap[idx])

    def bc(self, shape):
        return V(self.b, self.ap.to_broadcast(list(shape)))

    def re(self, pat, **kw):
        return V(self.b, self.ap.rearrange(pat, **kw))

    def sub(self, name):
        return V(Buf(name), self.ap)


class Eng:
    def __init__(self, name, eng, sem):
        self.name = name
        self.eng = eng
        self.sem = sem
        self.n = 0
        self.known = {}


class K:
    def __init__(self, nc, stack, kinds=None):
        self.nc = nc
        self.stack = stack
        self.kinds = kinds or {}
        self.engs = {}
        for name, eng in (("pe", nc.tensor), ("dve", nc.vector), ("act", nc.scalar),
                          ("pool", nc.gpsimd), ("sp", nc.sync)):
            sem = stack.enter_context(nc.semaphore("sem_" + name))
            self.engs[name] = Eng(name, eng, sem)
        self.dbufs = []
        self.nins = 0
        self.uid = 0
        self.sempool = {"sw": [], "hw": []}
        for i in range(NDSEM):
            sem = stack.enter_context(nc.semaphore(f"dsem{i}"))
            self.sempool["sw" if i < NDSEM_SW else "hw"].append([f"ds{i}", sem, 0])

    def sb(self, name, shape, dt, stack=None):
        self.uid += 1
        t = (stack or self.stack).enter_context(self.nc.sbuf_tensor(f"{name}_{self.uid}", list(shape), dt))
        return V(Buf(name), t[:])

    def ps(self, name, shape, dt=F32, stack=None):
        self.uid += 1
        t = (stack or self.stack).enter_context(self.nc.psum_tensor(f"{name}_{self.uid}", list(shape), dt))
        return V(Buf(name, ps=True), t[:])

    def dram(self, name, shape, dt, kind="Internal"):
        kind = self.kinds.get(name, kind)
        t = self.nc.dram_tensor(name, list(shape), dt, kind=kind)
        return V(Buf(name), t.ap())

    def _wait(self, E, deps):
        need = {}
        for d in deps:
            if d is None:
                continue
            key, sem, val = d
            if key not in need or need[key][1] < val:
                need[key] = (sem, val)
        for key, (sem, val) in need.items():
            if E.known.get(key, 0) >= val:
                continue
            if key == E.name and (E.name == "pe" or not SAME_ENGINE_SYNC):
                continue
            E.eng.wait_ge(sem, val)
            E.known[key] = val

    def _deps(self, R, W, me=None):
        deps = []
        for v in R:
            deps.append(v.b.w)
            if v.b.ps:
                for key, (sem, val) in v.b.r.items():
                    if key != me:
                        deps.append((key, sem, val))
        for v in W:
            deps.append(v.b.w)
            for key, (sem, val) in v.b.r.items():
                deps.append((key, sem, val))
        return deps

    def _mark(self, R, W, tok):
        key, sem, val = tok
        for v in R:
            v.b.r[key] = (sem, val)
        for v in W:
            v.b.w = tok
            v.b.r = {}

    def op(self, e, fn, R=(), W=()):
        E = self.engs[e]
        R = [v for v in R if isinstance(v, V)]
        self._wait(E, self._deps(R, W, E.name))
        ins = fn(E.eng)
        E.n += 1
        ins.then_inc(E.sem, 1)
        self._mark(R, W, (E.name, E.sem, E.n))
        self.nins += 1
        return ins

    def dma(self, e, out, in_, slot=None, **kw):
        E = self.engs[e]
        self._wait(E, self._deps([in_], [out]))
        if slot is not None:
            sb = slot.b
        elif "DRam" in type(out.ap.tensor).__name__:
            sb = in_.b
        else:
            sb = out.b
        kind = "sw" if e == "pool" else "hw"
        if sb.dsem is None:
            sb.dsem = {}
        if kind not in sb.dsem:
            sb.dsem[kind] = self.sempool[kind].pop()
            self.dbufs.append((sb, kind))
        ent = sb.dsem[kind]
        ent[2] += 16
        E.eng.dma_start(out=out.ap, in_=in_.ap, **kw).then_inc(ent[1], 16)
        tok = (ent[0], ent[1], ent[2])
        self._mark([in_], [out], tok)
        self.nins += 1

    def barrier(self):
        toks = [(E.name, E.sem, E.n) for E in self.engs.values() if E.n > 0]
        toks += [tuple(b.dsem[kind]) for b, kind in self.dbufs]
        for E in self.engs.values():
            self._wait(E, toks)
        for b, kind in self.dbufs:
            self.sempool[kind].append(b.dsem.pop(kind))
            b.w = None
            b.r = {}
        self.dbufs = []

    def mm(self, out, lhsT, rhs, start=True, stop=True, **kw):
        return self.op("pe", lambda e: e.matmul(out.ap, lhsT.ap, rhs.ap, start=start, stop=stop, **kw),
                       R=[lhsT, rhs], W=[out])

    def tr(self, out, in_, ident):
        return self.op("pe", lambda e: e.transpose(out.ap, in_.ap, ident.ap), R=[in_, ident], W=[out])

    def act(self, out, in_, func, bias=None, scale=None, accum_out=None):
        kw = {}
        R = [in_]
        if bias is not None:
            if isinstance(bias, V):
                kw["bias"] = bias.ap
                R.append(bias)
            else:
                kw["bias"] = bias
        if scale is not None:
            if isinstance(scale, V):
                kw["scale"] = scale.ap
                R.append(scale)
            else:
                kw["scale"] = scale
        W = [out]
        if accum_out is not None:
            kw["accum_out"] = accum_out.ap
            W.append(accum_out)
        return self.op("act", lambda g: g.activation(out=out.ap, in_=in_.ap, func=func, **kw), R=R, W=W)

    def tt(self, out, a, b, op, e="dve"):
        return self.op(e, lambda g: g.tensor_tensor(out=out.ap, in0=a.ap, in1=b.ap, op=op), R=[a, b], W=[out])

    def ts(self, out, a, s1, op0, s2=None, op1=None, e="dve"):
        R = [a]
        x1 = s1.ap if isinstance(s1, V) else s1
        x2 = s2.ap if isinstance(s2, V) else s2
        if isinstance(s1, V):
            R.append(s1)
        if isinstance(s2, V):
            R.append(s2)
        if op1 is None:
            return self.op(e, lambda g: g.tensor_scalar(out=out.ap, in0=a.ap, scalar1=x1, scalar2=None, op0=op0),
                           R=R, W=[out])
        return self.op(e, lambda g: g.tensor_scalar(out=out.ap, in0=a.ap, scalar1=x1, scalar2=x2, op0=op0, op1=op1),
                       R=R, W=[out])

    def stt(self, out, a, s, b, op0, op1):
        R = [a, b]
        x = s.ap if isinstance(s, V) else s
        if isinstance(s, V):
            R.append(s)
        return self.op("dve", lambda g: g.scalar_tensor_tensor(out=out.ap, in0=a.ap, scalar=x, in1=b.ap,
                                                               op0=op0, op1=op1), R=R, W=[out])

    def copy(self, out, in_, e="dve"):
        if e == "act":
            return self.op("act", lambda g: g.copy(out=out.ap, in_=in_.ap), R=[in_], W=[out])
        return self.op(e, lambda g: g.tensor_copy(out=out.ap, in_=in_.ap), R=[in_], W=[out])

    def memset(self, out, val, e="dve"):
        return self.op(e, lambda g: g.memset(out.ap, val), R=[], W=[out])

    def recip(self, out, in_):
        return self.op("dve", lambda g: g.reciprocal(out=out.ap, in_=in_.ap), R=[in_], W=[out])


class Rot:
    def __init__(self, items):
        self.items = items
        self.i = 0

    def next(self):
        v = self.items[self.i % len(self.items)]
        self.i += 1
        return v


def make_consts():
    c = {}
    half = 64
    inv_freq = (10000.0 ** (-np.arange(half, dtype=np.float32) / half)).astype(np.float32)
    ang = np.arange(T, dtype=np.float32)[:, None] * inv_freq[None, :]
    cos = np.cos(ang).astype(np.float32).T
    sin = np.sin(ang).astype(np.float32).T
    c["c_cos"] = np.ascontiguousarray(np.concatenate([cos, cos], 0))
    c["c_sin"] = np.ascontiguousarray(np.concatenate([sin, sin], 0))
    c["c_ident"] = np.eye(128, dtype=np.float32)
    c["c_ones"] = np.ones((128, 128), np.float32)
    P = np.zeros((128, 128), np.float32)
    for m in range(64):
        P[m, m + 64] = -1.0
        P[m + 64, m] = 1.0
    c["c_ropeT"] = np.ascontiguousarray(P.T)
    return c


CONST_SHAPES = {"c_cos": (128, T), "c_sin": (128, T), "c_ident": (128, 128), "c_ones": (128, 128),
                "c_ropeT": (128, 128)}


class Prog:
    def __init__(self, nseq=NSEQ, layers=(0, 1), phases=None, kinds=None):
        self.nseq = nseq
        self.layers = layers
        self.phases = phases
        self.nc = bass.Bass("TRN2", target_bir_lowering=False)
        self.stack = ExitStack()
        self.k = K(self.nc, self.stack, kinds)
        self.declare()

    def declare(self):
        L = DEPTH
        ns = self.nseq
        EI = "ExternalInput"
        sp = {}
        sp["xin"] = ([ns, D, T], F32, EI)
        sp["pin"] = ([L, ns, 256, T], F32, EI)
        sp["w_in"] = ([L, D, NIN], F32, EI)
        sp["conv_wT"] = ([L, 3072, 4], F32, EI)
        sp["a_log"] = ([L, 8], F32, EI)
        sp["dt_bias"] = ([L, 8], F32, EI)
        sp["norm_w"] = ([L, 128], F32, EI)
        for n in ("k", "v"):
            sp["peT_" + n] = ([L, 128, 32], F32, EI)
            sp["w1_" + n] = ([L, 4096, 256], F32, EI)
            sp["w2_" + n] = ([L, 256, 128], F32, EI)
        sp["w_a"] = ([L, 1024, D], F32, EI)
        sp["w_b"] = ([L, 1024, D], F32, EI)
        sp["w_out"] = ([L, D, D], F32, EI)
        sp["w_gate"] = ([L, D, DFF], F32, EI)
        sp["w_up"] = ([L, D, DFF], F32, EI)
        sp["w_down"] = ([L, DFF, D], F32, EI)
        sp["w_ple"] = ([L, 256, D], F32, EI)
        sp["w_pg"] = ([L, D, D], F32, EI)
        for n in ("ln1_g", "ln1_b", "ln2_g", "ln2_b"):
            sp[n] = ([L, 128, 16], F32, EI)
        for n, s_ in CONST_SHAPES.items():
            sp[n] = (list(s_), F32, EI)
        sp["out"] = ([ns, D, T], F32, "ExternalOutput")
        IN = "Internal"
        sp["xres"] = ([D, T], F32, IN)
        sp["dnraw"] = ([3072, T], F32, IN)
        sp["nq"] = ([1024, T], BF16, IN)
        for n in ("kc", "vc", "ks", "kw"):
            sp[n] = ([256, T], BF16, IN)
        sp["mga"] = ([D, T], F32, IN)
        sp["mgb"] = ([D, T], F32, IN)
        sp["ztm"] = ([T, 1024], F32, IN)
        sp["batm"] = ([T, 16], F32, IN)
        sp["vstm"] = ([T, 256], BF16, IN)
        sp["vwtm"] = ([T, 256], BF16, IN)
        sp["gttm"] = ([T, 24], F32, IN)
        sp["dqfm"] = ([1024, T], F32, IN)
        sp["dkfm"] = ([1024, T], F32, IN)
        sp["dktm"] = ([T, 1024], F32, IN)
        sp["dvtm"] = ([T, 1024], F32, IN)
        sp["oaT"] = ([1024, T], BF16, IN)
        sp["obT"] = ([1024, T], BF16, IN)
        sp["pre"] = ([D, T], F32, IN)
        self.specs = sp
        prog = self

        class LazyD(dict):
            def __missing__(self, name):
                shape, dt, kind = prog.specs[name]
                v = prog.k.dram(name, shape, dt, kind)
                self[name] = v
                return v
        self.d = LazyD()

    def wload(self, dst, wv, col0, n, kc):
        self.k.dma("pool", dst[:, :kc, :n], wv.re("(kc p) n -> p kc n", p=128)[:, :, col0:col0 + n])

    def phase_inproj(self, l, s):
        k, d = self.k, self.d
        with ExitStack() as st:
            xT = k.sb("p1_xT", [128, 16, T], BF16, st)
            wb = Rot([k.sb(f"p1_w{i}", [128, 16, 512], BF16, st) for i in range(2)])
            cos = k.sb("p1_cos", [128, T], F32, st)
            sin = k.sb("p1_sin", [128, T], F32, st)
            ropeT = k.sb("p1_ropeT", [128, 128], F32, st)
            stg = Rot([k.sb(f"p1_stg{i}", [128, 512], F32, st) for i in range(4)])
            stb = Rot([k.sb(f"p1_stb{i}", [128, 512], BF16, st) for i in range(4)])
            t1 = Rot([k.sb(f"p1_t1{i}", [128, 512], F32, st) for i in range(2)])
            t2 = Rot([k.sb(f"p1_t2{i}", [128, 512], F32, st) for i in range(2)])
            small = Rot([k.sb(f"p1_sm{i}", [128, 64], F32, st) for i in range(4)])
            dtb = k.sb("p1_dtb", [128, 8], F32, st)
            negA = k.sb("p1_negA", [128, 8], F32, st)
            pss = Rot([k.ps(f"p1_ps{i}", [128, 512], F32, st) for i in range(6)])
            ps2 = Rot([k.ps(f"p1_psr{i}", [128, 512], F32, st) for i in range(2)])

            src = d["xin"][s] if l == 0 else d["xres"]
            for c4 in range(4):
                k.dma("pool", xT[:, c4 * 4:(c4 + 1) * 4, :],
                      src.re("(kc p) t -> p kc t", p=128)[:, c4 * 4:(c4 + 1) * 4, :])
            k.dma("sp", cos, d["c_cos"])
            k.dma("sp", sin, d["c_sin"])
            k.dma("sp", ropeT, d["c_ropeT"])
            k.dma("sp", dtb, V(d["dt_bias"].b, d["dt_bias"].ap[l].partition_broadcast(128)))
            k.dma("sp", negA, V(d["a_log"].b, d["a_log"].ap[l].partition_broadcast(128)))
            k.act(negA, negA, AF.Exp)
            k.ts(negA, negA, -1.0, ALU.mult)
            W = d["w_in"][l]

            fm = [(O_DQ, 3072, "raw", d["dnraw"], 0),
                  (O_NQ, 1024, "ropeq", d["nq"], 0),
                  (O_KC, 256, "rope", d["kc"], 0),
                  (O_VC, 256, "bf", d["vc"], 0),
                  (O_KS, 256, "rope", d["ks"], 0),
                  (O_KW, 256, "rope", d["kw"], 0),
                  (O_MA, 2048, "sig", d["mga"], 0),
                  (O_MB, 2048, "sig", d["mgb"], 0)]
            nev = 0
            for (c0, ncols, kind, dst, r0) in fm:
                for cb in range(0, ncols, 512):
                    nb = min(512, ncols - cb)
                    w = wb.next()
                    self.wload(w, W, c0 + cb, nb, 16)
                    for mc in range(nb // 128):
                        row = r0 + cb + mc * 128
                        for tt in range(4):
                            ps = pss.next()
                            tsl = slice(tt * 512, (tt + 1) * 512)
                            for kc in range(16):
                                k.mm(ps, w[:, kc, mc * 128:(mc + 1) * 128], xT[:, kc, tsl],
                                     start=(kc == 0), stop=(kc == 15))
                            nev += 1
                            if kind == "raw":
                                o = stg.next()
                                if nev % 2:
                                    k.act(o, ps, AF.Copy)
                                else:
                                    k.copy(o, ps)
                                k.dma("sp", dst[row:row + 128, tsl], o)
                            elif kind == "sig":
                                o = stg.next()
                                k.act(o, ps, AF.Sigmoid)
                                k.dma("sp", dst[row:row + 128, tsl], o)
                            elif kind == "bf":
                                o = stb.next()
                                k.copy(o, ps)
                                k.dma("sp", dst[row:row + 128, tsl], o)
                            else:
                                sc = (128.0 ** -0.5) if kind == "ropeq" else 1.0
                                xs = stg.next()
                                k.act(xs, ps, AF.Copy)
                                pr = ps2.next()
                                k.mm(pr, ropeT, xs)
                                a = t1.next()
                                b = t2.next()
                                k.stt(a, xs, sc, cos[:, tsl], ALU.mult, ALU.mult)
                                k.stt(b, pr, sc, sin[:, tsl], ALU.mult, ALU.mult)
                                o = stb.next()
                                k.tt(o, a, b, ALU.add)
                                k.dma("sp", dst[row:row + 128, tsl], o)

            tm = [(O_DZ, 512, "silu", d["ztm"], 0), (O_DZ + 512, 512, "silu", d["ztm"], 512),
                  (O_DB, 16, "ba", d["batm"], 0), (O_VS, 256, "bf", d["vstm"], 0),
                  (O_VW, 256, "bf", d["vwtm"], 0), (O_GT, 24, "sig", d["gttm"], 0)]
            for (c0, ncols, kind, dst, dc0) in tm:
                w = wb.next()
                self.wload(w, W, c0, ncols, 16)
                for t128 in range(16):
                    ps = pss.next()
                    rs = slice(t128 * 128, (t128 + 1) * 128)
                    for kc in range(16):
                        k.mm(ps[:, :ncols], xT[:, kc, rs], w[:, kc, :ncols], start=(kc == 0), stop=(kc == 15))
                    if kind == "silu":
                        o = stg.next()
                        k.act(o, ps, AF.Silu)
                        k.dma("sp", dst[rs, dc0:dc0 + ncols], o)
                    elif kind == "bf":
                        o = stb.next()
                        k.copy(o[:, :ncols], ps[:, :ncols])
                        k.dma("sp", dst[rs, dc0:dc0 + ncols], o[:, :ncols])
                    elif kind == "sig":
                        o = small.next()
                        k.act(o[:, :ncols], ps[:, :ncols], AF.Sigmoid)
                        k.dma("sp", dst[rs, dc0:dc0 + ncols], o[:, :ncols])
                    else:
                        o = small.next()
                        k.act(o[:, 0:8], ps[:, 0:8], AF.Sigmoid)
                        k.tt(o[:, 16:24], ps[:, 8:16], dtb, ALU.add)
                        k.act(o[:, 24:32], o[:, 16:24], AF.Exp)
                        k.act(o[:, 32:40], o[:, 24:32], AF.Ln, bias=1.0)
                        k.tt(o[:, 8:16], o[:, 32:40], negA, ALU.mult)
                        k.dma("sp", dst[rs, 0:16], o[:, 0:16])
            k.barrier()


def build(nseq=NSEQ, layers=(0, 1), phases=None, kinds=None):
    p = Prog(nseq, layers, phases, kinds)
    return p


def phase_outproj(self, l, s):
    k, d = self.k, self.d
    with ExitStack() as st:
        mixT = k.sb("p5_mixT", [128, 16, T], BF16, st)
        pss = Rot([k.ps(f"p5_ps{i}", [128, 512], F32, st) for i in range(6)])
        with ExitStack() as st2:
            oaT = k.sb("p5_oaT", [128, 8, T], BF16, st2)
            obT = k.sb("p5_obT", [128, 8, T], BF16, st2)
            wa = Rot([k.sb(f"p5_wa{i}", [128, 8, 512], BF16, st2) for i in range(2)])
            wb = Rot([k.sb(f"p5_wb{i}", [128, 8, 512], BF16, st2) for i in range(2)])
            ga = Rot([k.sb(f"p5_ga{i}", [128, 512], F32, st2) for i in range(2)])
            gb = Rot([k.sb(f"p5_gb{i}", [128, 512], F32, st2) for i in range(2)])
            m1 = Rot([k.sb(f"p5_m1{i}", [128, 512], F32, st2) for i in range(2)])
            m2 = Rot([k.sb(f"p5_m2{i}", [128, 512], F32, st2) for i in range(2)])
            k.dma("sp", oaT, d["oaT"].re("(c p) t -> p c t", p=128))
            k.dma("sp", obT, d["obT"].re("(c p) t -> p c t", p=128))
            for cb in range(4):
                a, b = wa.next(), wb.next()
                self.wload(a, d["w_a"][l], cb * 512, 512, 8)
                self.wload(b, d["w_b"][l], cb * 512, 512, 8)
                for mc in range(4):
                    ch = cb * 4 + mc
                    for tt in range(4):
                        tsl = slice(tt * 512, (tt + 1) * 512)
                        pa, pb = pss.next(), pss.next()
                        for kc in range(8):
                            k.mm(pa, a[:, kc, mc * 128:(mc + 1) * 128], oaT[:, kc, tsl], start=(kc == 0), stop=(kc == 7))
                        for kc in range(8):
                            k.mm(pb, b[:, kc, mc * 128:(mc + 1) * 128], obT[:, kc, tsl], start=(kc == 0), stop=(kc == 7))
                        g1, g2 = ga.next(), gb.next()
                        k.dma("sp", g1, d["mga"][ch * 128:(ch + 1) * 128, tsl])
                        k.dma("act", g2, d["mgb"][ch * 128:(ch + 1) * 128, tsl])
                        x1, x2 = m1.next(), m2.next()
                        k.tt(x1, pa, g1, ALU.mult)
                        k.tt(x2, pb, g2, ALU.mult)
                        k.tt(mixT[:, ch, tsl], x1, x2, ALU.add, e="pool")
            k.barrier()
        with ExitStack() as st2:
            wo = Rot([k.sb(f"p5_wo{i}", [128, 16, 512], BF16, st2) for i in range(2)])
            xr = Rot([k.sb(f"p5_xr{i}", [128, 512], F32, st2) for i in range(2)])
            og = Rot([k.sb(f"p5_og{i}", [128, 512], F32, st2) for i in range(2)])
            src = d["xin"][s] if l == 0 else d["xres"]
            for cb in range(4):
                w = wo.next()
                self.wload(w, d["w_out"][l], cb * 512, 512, 16)
                for mc in range(4):
                    ch = cb * 4 + mc
                    for tt in range(4):
                        tsl = slice(tt * 512, (tt + 1) * 512)
                        ps = pss.next()
                        for kc in range(16):
                            k.mm(ps, w[:, kc, mc * 128:(mc + 1) * 128], mixT[:, kc, tsl], start=(kc == 0), stop=(kc == 15))
                        x = xr.next()
                        k.dma("act", x, src[ch * 128:(ch + 1) * 128, tsl])
                        o = og.next()
                        k.stt(o, x, ALPHA, ps, ALU.mult, ALU.add)
                        k.dma("sp", d["pre"][ch * 128:(ch + 1) * 128, tsl], o)
            k.barrier()


def phase_ln(self, l, gname, bname, dst):
    k, d = self.k, self.d
    with ExitStack() as st:
        xt = Rot([k.sb(f"p6_x{i}", [128, 16, 512], F32, st) for i in range(2)])
        sq = k.sb("p6_sq", [128, 16, 512], F32, st)
        ones = k.sb("p6_ones", [128, 128], F32, st)
        g = k.sb("p6_g", [128, 16], F32, st)
        b = k.sb("p6_b", [128, 16], F32, st)
        mean = k.sb("p6_mean", [128, 512], F32, st)
        var = k.sb("p6_var", [128, 512], F32, st)
        rstd = k.sb("p6_rstd", [128, 512], F32, st)
        ps1 = k.ps("p6_ps1", [128, 512], F32, st)
        ps2 = k.ps("p6_ps2", [128, 512], F32, st)
        k.dma("sp", ones, d["c_ones"])
        k.dma("sp", g, d[gname][l])
        k.dma("sp", b, d[bname][l])
        for tt in range(4):
            tsl = slice(tt * 512, (tt + 1) * 512)
            x = xt.next()
            k.dma("sp", x, d["pre"].re("(c p) t -> p c t", p=128)[:, :, tsl])
            k.act(sq, x, AF.Square)
            for c in range(16):
                k.mm(ps1, ones, x[:, c, :], start=(c == 0), stop=(c == 15))
            for c in range(16):
                k.mm(ps2, ones, sq[:, c, :], start=(c == 0), stop=(c == 15))
            k.ts(mean, ps1, 1.0 / D, ALU.mult)
            k.tt(var, mean, mean, ALU.mult)
            k.stt(var, ps2, 1.0 / D, var, ALU.mult, ALU.subtract)
            k.ts(var, var, LN_EPS, ALU.add)
            k.act(var, var, AF.Sqrt)
            k.recip(rstd, var)
            for c in range(16):
                e = "pool" if c % 2 else "dve"
                k.tt(x[:, c, :], x[:, c, :], mean, ALU.subtract, e=e)
                k.tt(x[:, c, :], x[:, c, :], rstd, ALU.mult, e=e)
                k.ts(x[:, c, :], x[:, c, :], g[:, c:c + 1], ALU.mult, b[:, c:c + 1], ALU.add, e=e)
            k.dma("sp", dst.re("(c p) t -> p c t", p=128)[:, :, tsl], x)
        k.barrier()


def phase_ffn(self, l, s):
    k, d = self.k, self.d
    for half in range(2):
        t0 = half * 1024
        with ExitStack() as st:
            xT = k.sb("p7_xT", [128, 16, 1024], BF16, st)
            pT = k.sb("p7_pT", [128, 2, 1024], BF16, st)
            hT = k.sb("p7_hT", [128, 44, 1024], BF16, st)
            pss = Rot([k.ps(f"p7_ps{i}", [128, 512], F32, st) for i in range(8)])
            k.dma("pool", xT, d["xres"].re("(c p) t -> p c t", p=128)[:, :, t0:t0 + 1024])
            k.dma("pool", pT, d["pin"][l, s].re("(c p) t -> p c t", p=128)[:, :, t0:t0 + 1024])
            with ExitStack() as st2:
                wg = Rot([k.sb(f"p7_wg{i}", [128, 16, 256], BF16, st2) for i in range(2)])
                wu = Rot([k.sb(f"p7_wu{i}", [128, 16, 256], BF16, st2) for i in range(2)])
                sg = Rot([k.sb(f"p7_sg{i}", [128, 512], F32, st2) for i in range(3)])
                for cb in range(22):
                    g_, u_ = wg.next(), wu.next()
                    self.wload(g_, d["w_gate"][l], cb * 256, 256, 16)
                    self.wload(u_, d["w_up"][l], cb * 256, 256, 16)
                    for mc in range(2):
                        j = cb * 2 + mc
                        for tt in range(2):
                            tsl = slice(tt * 512, (tt + 1) * 512)
                            pg, pu = pss.next(), pss.next()
                            for kc in range(16):
                                k.mm(pg, g_[:, kc, mc * 128:(mc + 1) * 128], xT[:, kc, tsl], start=(kc == 0), stop=(kc == 15))
                            for kc in range(16):
                                k.mm(pu, u_[:, kc, mc * 128:(mc + 1) * 128], xT[:, kc, tsl], start=(kc == 0), stop=(kc == 15))
                            s_ = sg.next()
                            k.act(s_, pg, AF.Silu)
                            k.tt(hT[:, j, tsl], s_, pu, ALU.mult)
                k.barrier()
            with ExitStack() as st2:
                wd = Rot([k.sb(f"p7_wd{i}", [128, 44, 128], BF16, st2) for i in range(2)])
                wpg = Rot([k.sb(f"p7_wpg{i}", [128, 16, 128], BF16, st2) for i in range(2)])
                wpl = Rot([k.sb(f"p7_wpl{i}", [128, 2, 128], BF16, st2) for i in range(2)])
                sgm = Rot([k.sb(f"p7_sgm{i}", [128, 512], F32, st2) for i in range(2)])
                ple = Rot([k.sb(f"p7_ple{i}", [128, 512], F32, st2) for i in range(2)])
                xr = Rot([k.sb(f"p7_xr{i}", [128, 512], F32, st2) for i in range(2)])
                acc = Rot([k.sb(f"p7_acc{i}", [128, 512], F32, st2) for i in range(2)])
                for m in range(16):
                    w1, w2, w3 = wd.next(), wpg.next(), wpl.next()
                    self.wload(w1, d["w_down"][l], m * 128, 128, 44)
                    self.wload(w2, d["w_pg"][l], m * 128, 128, 16)
                    self.wload(w3, d["w_ple"][l], m * 128, 128, 2)
                    for tt in range(2):
                        tsl = slice(tt * 512, (tt + 1) * 512)
                        gsl = slice(t0 + tt * 512, t0 + (tt + 1) * 512)
                        pf, pgt, pp = pss.next(), pss.next(), pss.next()
                        for j in range(44):
                            k.mm(pf, w1[:, j, :], hT[:, j, tsl], start=(j == 0), stop=(j == 43))
                        for kc in range(16):
                            k.mm(pgt, w2[:, kc, :], xT[:, kc, tsl], start=(kc == 0), stop=(kc == 15))
                        for c in range(2):
                            k.mm(pp, w3[:, c, :], pT[:, c, tsl], start=(c == 0), stop=(c == 1))
                        s_ = sgm.next()
                        k.act(s_, pgt, AF.Sigmoid)
                        pl = ple.next()
                        k.tt(pl, s_, pp, ALU.mult)
                        x = xr.next()
                        k.dma("act", x, d["xres"][m * 128:(m + 1) * 128, gsl])
                        a = acc.next()
                        k.stt(a, x, ALPHA, pf, ALU.mult, ALU.add)
                        k.tt(a, a, pl, ALU.add, e="pool")
                        k.dma("sp", d["pre"][m * 128:(m + 1) * 128, gsl], a)
                k.barrier()


Prog.phase_outproj = phase_outproj
Prog.phase_ln = phase_ln
Prog.phase_ffn = phase_ffn


NEG = -30000.0


def nsa_consts():
    c = {}
    n = np.arange(127)
    t = np.arange(T)
    c["c_cmpmask"] = np.where((16 * n[:, None] + 31) <= t[None, :], 0.0, NEG).astype(np.float32)
    i = np.arange(T)
    c["c_eexp"] = ((i[None, :] // 64) == np.arange(32)[:, None]).astype(np.float32)
    a = np.arange(128)
    c["c_causb"] = np.where(a[:, None] <= a[None, :], 0.0, NEG).astype(np.float32)
    c["c_winb"] = np.where(a[:, None] > a[None, :], 0.0, NEG).astype(np.float32)
    j = np.arange(32)
    c["c_overlap"] = ((16 * n[:, None] <= 64 * j[None, :] + 63) & (16 * n[:, None] + 31 >= 64 * j[None, :])
                      ).astype(np.float32)
    cur = t // 64
    forced = (j[None, :] == 0) | (j[None, :] == cur[:, None]) | (j[None, :] == cur[:, None] - 1)
    causal = j[None, :] <= cur[:, None]
    c["c_tk_m"] = (causal & ~forced).astype(np.float32)
    c["c_tk_a"] = np.where(forced, 100.0, np.where(causal, 0.0, -100.0)).astype(np.float32)
    return c


CONST_SHAPES.update({"c_cmpmask": (127, T), "c_eexp": (32, T), "c_causb": (128, 128), "c_winb": (128, 128),
                     "c_overlap": (127, 32), "c_tk_m": (T, 32), "c_tk_a": (T, 32)})


def phase_nsa(self, l, s):
    k, d = self.k, self.d
    for g in range(2):
        with ExitStack() as st:
            gsl = slice(g * 128, (g + 1) * 128)
            identb = k.sb("n_identb", [128, 128], BF16, st)
            cmpmask = k.sb("n_cmpmask", [128, T], BF16, st)
            eexp = k.sb("n_eexp", [32, T], BF16, st)
            causb = k.sb("n_causb", [128, 128], BF16, st)
            winb = k.sb("n_winb", [128, 128], BF16, st)
            tkm = k.sb("n_tkm", [128, 16, 32], F32, st)
            tka = k.sb("n_tka", [128, 16, 32], F32, st)
            k.dma("pool", identb, d["c_ident"])
            k.dma("pool", cmpmask[:127, :], d["c_cmpmask"])
            k.dma("pool", eexp, d["c_eexp"])
            k.dma("pool", causb, d["c_causb"])
            k.dma("pool", winb, d["c_winb"])
            k.dma("sp", tkm, d["c_tk_m"].re("(q p) j -> p q j", p=128))
            k.dma("sp", tka, d["c_tk_a"].re("(q p) j -> p q j", p=128))
            q4 = k.sb("n_q4", [128, 4, T], BF16, st)
            ksT = k.sb("n_ksT", [128, T], BF16, st)
            kwT = k.sb("n_kwT", [128, T], BF16, st)
            Vs = k.sb("n_Vs", [128, 16, 132], BF16, st)
            Vw = k.sb("n_Vw", [128, 16, 132], BF16, st)
            gt = k.sb("n_gt", [128, 16, 24], F32, st)
            k.dma("sp", q4, d["nq"].re("(h p) t -> p h t", p=128)[:, g * 4:(g + 1) * 4, :])
            k.dma("sp", ksT, d["ks"][gsl, :])
            k.dma("sp", kwT, d["kw"][gsl, :])
            k.dma("act", Vs[:, :, 0:128], d["vstm"].re("(kb p) c -> p kb c", p=128)[:, :, gsl])
            k.dma("act", Vw[:, :, 0:128], d["vwtm"].re("(kb p) c -> p kb c", p=128)[:, :, gsl])
            k.memset(Vs[:, :, 128:129], 1.0)
            k.memset(Vw[:, :, 128:129], 1.0)
            k.dma("sp", gt, d["gttm"].re("(q p) c -> p q c", p=128))
            import os as _os
            stop = int(_os.environ.get("NSA_STOP", "99"))
            if stop == 1:
                k.barrier()
                return
            kcmpT = k.sb("n_kcmpT", [128, 128], BF16, st)
            rcmp = k.sb("n_rcmp", [128, 161], BF16, st)
            k.memset(rcmp[:, 128:129], 1.0)
            k.dma("pool", rcmp[:127, 129:161], d["c_overlap"])
            ST = Rot([k.ps(f"n_st{i}", [128, 512], F32, st) for i in range(3)])
            PV = Rot([k.ps(f"n_pv{i}", [128, 2, 512], F32, st) for i in range(2)])
            psT = k.ps("n_psT", [128, 1024], BF16, st)

            with ExitStack() as st2:
                xT = k.sb("n_cx", [128, T], BF16, st2)
                w1 = k.sb("n_w1", [128, 32, 256], BF16, st2)
                w2 = k.sb("n_w2", [128, 2, 128], BF16, st2)
                peT = k.sb("n_peT", [128, 32], BF16, st2)
                hT = k.sb("n_hT", [128, 2, 128], BF16, st2)
                b1 = k.sb("n_b1", [128, 2], F32, st2)
                for nm in ("k", "v"):
                    k.dma("sp", xT, d["kc" if nm == "k" else "vc"][gsl, :])
                    k.dma("pool", w1, d["w1_" + nm][l].re("(l d) h -> d l h", d=128))
                    k.dma("pool", w2, d["w2_" + nm][l].re("(c p) d -> p c d", p=128))
                    k.dma("pool", peT, d["peT_" + nm][l])
                    pb = ST.next()
                    for mh in range(2):
                        for li in range(32):
                            k.mm(pb[:, mh:mh + 1], w1[:, li, mh * 128:(mh + 1) * 128], peT[:, li:li + 1],
                                 start=(li == 0), stop=(li == 31))
                    k.copy(b1, pb[:, 0:2])
                    for mh in range(2):
                        ph = ST.next()
                        for li in range(32):
                            k.mm(ph[:, :127], w1[:, li, mh * 128:(mh + 1) * 128], xT[:, li:li + 16 * 126 + 1:16],
                                 start=(li == 0), stop=(li == 31))
                        k.act(hT[:, mh, :127], ph[:, :127], AF.Silu, bias=b1[:, mh:mh + 1])
                    po = ST.next()
                    if nm == "k":
                        for mh in range(2):
                            k.mm(po[:, :127], w2[:, mh, :], hT[:, mh, :127], start=(mh == 0), stop=(mh == 1))
                        k.copy(kcmpT[:, :127], po[:, :127])
                    else:
                        for mh in range(2):
                            k.mm(po[:127, :128], hT[:, mh, :127], w2[:, mh, :], start=(mh == 0), stop=(mh == 1))
                        k.copy(rcmp[:127, 0:128], po[:127, :128])

            if stop == 2:
                k.barrier()
                return
            Ec = k.sb("n_Ec", [128, 512], BF16, st)
            Es = k.sb("n_Es", [128, 16, 512], BF16, st)
            Ew = k.sb("n_Ew", [128, 5, 512], BF16, st)
            rden = k.sb("n_rden", [128, 4], F32, st)
            cf = k.sb("n_cf", [128, 4], F32, st)
            imp = k.sb("n_imp", [128, 32], F32, st)
            vv = k.sb("n_vv", [128, 32], F32, st)
            v2 = k.sb("n_v2", [128, 32], F32, st)
            m8 = k.sb("n_m8", [128, 16], F32, st)
            nsel = k.sb("n_nsel", [128, 32], BF16, st)
            nsT = k.sb("n_nsT", [32, 128], BF16, st)
            acc = k.sb("n_acc", [128, 4, 128], F32, st)
            accb = k.sb("n_accb", [128, 4, 128], BF16, st)
            obs = Rot([k.sb(f"n_obs{i}", [128, 4, 128], BF16, st) for i in range(2)])

            def finish(pv, width, branch, first):
                pv4 = pv.re("p b (x c) -> p (b x) c", c=256)
                k.ts(rden.re("p (h o) -> p h o", o=1), pv4[:, :, 128:129], 1e-30, ALU.max)
                k.recip(rden, rden)
                k.tt(cf.re("p (h o) -> p h o", o=1), rden.re("p (h o) -> p h o", o=1),
                     gt[:, qb, g * 12 + branch:g * 12 + 12:3].re("p (h o) -> p h o", o=1), ALU.mult)
                for h in range(4):
                    if first:
                        k.act(acc[:, h, :], pv4[:, h, 0:128], AF.Copy, scale=cf[:, h:h + 1])
                    else:
                        k.stt(acc[:, h, :], pv4[:, h, 0:128], cf[:, h:h + 1], acc[:, h, :], ALU.mult, ALU.add)
                return pv4

            for qb in range(16):
                qsl = slice(qb * 128, (qb + 1) * 128)
                qr = q4[:, :, qsl]
                ps = ST.next()
                k.mm(ps[:127, :], kcmpT[:, :127], qr, start=True, stop=False)
                k.mm(ps[:127, :], identb[:127, :127], cmpmask[:127, qsl].re("p (o t) -> p o t", o=1).bc([127, 4, 128]),
                     start=False, stop=True)
                k.act(Ec[:127, :], ps[:127, :], AF.Exp)
                pv = PV.next()
                pv4 = pv.re("p b (x c) -> p (b x) c", c=256)
                for h in range(4):
                    k.mm(pv4[:, h, 0:161], Ec[:127, h * 128:(h + 1) * 128], rcmp[:127, :], start=True, stop=True)
                finish(pv, 161, 0, True)
                if stop == 3 or (stop == 16 and qb == 8):
                    k.barrier()
                    return
                sel_on = qb >= 8
                if sel_on:
                    def chk(n):
                        if stop == n:
                            k.barrier()
                            return True
                        return False
                    k.ts(imp, pv4[:, 0, 129:161], rden[:, 0:1], ALU.mult)
                    if chk(17): return
                    for h in range(1, 4):
                        k.stt(imp, pv4[:, h, 129:161], rden[:, h:h + 1], imp, ALU.mult, ALU.add)
                    if chk(10): return
                    k.tt(vv, imp, tkm[:, qb, :], ALU.mult)
                    k.tt(vv, vv, tka[:, qb, :], ALU.add)
                    if chk(11): return
                    k.op("dve", lambda e: e.max(out=m8[:, 0:8].ap, in_=vv.ap), R=[vv], W=[m8])
                    if chk(12): return
                    k.op("dve", lambda e: e.match_replace(out=v2.ap, in_to_replace=m8[:, 0:8].ap, in_values=vv.ap,
                                                          imm_value=-1e9), R=[vv, m8], W=[v2])
                    if chk(13): return
                    k.op("dve", lambda e: e.max(out=m8[:, 8:16].ap, in_=v2.ap), R=[v2], W=[m8])
                    k.ts(v2, vv, m8[:, 15:16], ALU.is_ge)
                    if chk(14): return
                    k.ts(nsel, v2, 1.0, ALU.subtract, -NEG, ALU.mult)
                    if chk(15): return
                    k.tr(psT[:32, 0:128], nsel, identb)
                    k.copy(nsT, psT[:32, 0:128])
                    if stop == 8:
                        k.barrier()
                        return
                for kb in range(qb + 1):
                    ps = ST.next()
                    ksl = slice(kb * 128, (kb + 1) * 128)
                    last = (not sel_on) and kb != qb
                    k.mm(ps, ksT[:, ksl], qr, start=True, stop=last)
                    if sel_on:
                        k.mm(ps, eexp[:, ksl], nsT.re("p (o t) -> p o t", o=1).bc([32, 4, 128]),
                             start=False, stop=(kb != qb))
                    if kb == qb:
                        k.mm(ps, identb, causb.re("p (o t) -> p o t", o=1).bc([128, 4, 128]), start=False, stop=True)
                    k.act(Es[:, kb, :], ps, AF.Exp)
                pv = PV.next()
                pv4 = pv.re("p b (x c) -> p (b x) c", c=256)
                for h in range(4):
                    for kb in range(qb + 1):
                        k.mm(pv4[:, h, 0:129], Es[:, kb, h * 128:(h + 1) * 128], Vs[:, kb, 0:129],
                             start=(kb == 0), stop=(kb == qb))
                finish(pv, 129, 1, False)
                if stop == 4 or (stop == 6 and qb == 8):
                    k.barrier()
                    return
                kb0 = max(0, qb - 4)
                for kb in range(kb0, qb + 1):
                    ps = ST.next()
                    ksl = slice(kb * 128, (kb + 1) * 128)
                    mk = causb if kb == qb else (winb if kb == qb - 4 else None)
                    k.mm(ps, kwT[:, ksl], qr, start=True, stop=(mk is None))
                    if mk is not None:
                        k.mm(ps, identb, mk.re("p (o t) -> p o t", o=1).bc([128, 4, 128]), start=False, stop=True)
                    k.act(Ew[:, kb - kb0, :], ps, AF.Exp)
                pv = PV.next()
                pv4 = pv.re("p b (x c) -> p (b x) c", c=256)
                for h in range(4):
                    for kb in range(kb0, qb + 1):
                        k.mm(pv4[:, h, 0:129], Ew[:, kb - kb0, h * 128:(h + 1) * 128], Vw[:, kb, 0:129],
                             start=(kb == kb0), stop=(kb == qb))
                finish(pv, 129, 2, False)
                if stop == 5:
                    k.barrier()
                    return
                k.copy(accb, acc)
                for h in range(4):
                    k.tr(psT[:, h * 128:(h + 1) * 128], accb[:, h, :], identb)
                ob = obs.next()
                k.copy(ob, psT[:, 0:512].re("p (h t) -> p h t", h=4))
                k.dma("sp", d["obT"].re("(h p) t -> p h t", p=128)[:, g * 4:(g + 1) * 4, qsl], ob)
                if stop == 7 or (stop == 9 and qb == 7):
                    k.barrier()
                    return
            k.barrier()


Prog.phase_nsa = phase_nsa


def dn_consts():
    c = {}
    a = np.arange(128)
    c["c_tri_le"] = (a[:, None] <= a[None, :]).astype(np.float32)
    c["c_tri_gt"] = (a[:, None] > a[None, :]).astype(np.float32)
    return c


CONST_SHAPES.update({"c_tri_le": (128, 128), "c_tri_gt": (128, 128)})


def phase_dnprep(self, l, s):
    k, d = self.k, self.d
    with ExitStack() as st:
        cw = k.sb("d2_cw", [128, 24, 4], F32, st)
        ones = k.sb("d2_ones", [128, 128], F32, st)
        ident = k.sb("d2_ident", [128, 128], F32, st)
        up = Rot([k.sb(f"d2_up{i}", [128, T + 3], F32, st) for i in range(2)])
        ys = Rot([k.sb(f"d2_y{i}", [128, T], F32, st) for i in range(2)])
        sq = k.sb("d2_sq", [128, T], F32, st)
        rs = Rot([k.sb(f"d2_rs{i}", [128, 512], F32, st) for i in range(2)])
        tms = Rot([k.sb(f"d2_tm{i}", [128, 4, 128], F32, st) for i in range(2)])
        pss = Rot([k.ps(f"d2_ps{i}", [128, 512], F32, st) for i in range(4)])
        k.dma("sp", cw, d["conv_wT"][l].re("(c p) j -> p c j", p=128))
        k.dma("sp", ones, d["c_ones"])
        k.dma("sp", ident, d["c_ident"])
        for u in up.items:
            k.memset(u[:, 0:3], 0.0)
        for which in range(3):
            for h in range(8):
                c = which * 8 + h
                u = up.next()
                k.dma("sp" if c % 2 else "act", u[:, 3:3 + T], d["dnraw"][c * 128:(c + 1) * 128, :])
                y = ys.next()
                k.ts(y, u[:, 3:3 + T], cw[:, c, 3:4], ALU.mult)
                for j in range(3):
                    k.stt(y, u[:, j:j + T], cw[:, c, j:j + 1], y, ALU.mult, ALU.add)
                k.act(y, y, AF.Silu)
                if which < 2:
                    k.act(sq, y, AF.Square)
                    for tt in range(4):
                        tsl = slice(tt * 512, (tt + 1) * 512)
                        ps = pss.next()
                        k.mm(ps, ones, sq[:, tsl])
                        r = rs.next()
                        k.ts(r, ps, NORM_EPS, ALU.add)
                        k.act(r, r, AF.Sqrt)
                        k.recip(r, r)
                        if which == 0:
                            k.stt(y[:, tsl], y[:, tsl], 128.0 ** -0.5, r, ALU.mult, ALU.mult)
                        else:
                            k.tt(y[:, tsl], y[:, tsl], r, ALU.mult)
                    k.dma("sp", d["dqfm" if which == 0 else "dkfm"][h * 128:(h + 1) * 128, :], y)
                if which >= 1:
                    dst = d["dktm" if which == 1 else "dvtm"]
                    for t4 in range(4):
                        ps = pss.next()
                        for i in range(4):
                            tb = t4 * 4 + i
                            k.tr(ps[:, i * 128:(i + 1) * 128], y[:, tb * 128:(tb + 1) * 128], ident)
                        o = tms.next()
                        k.copy(o, ps.re("p (i c) -> p i c", i=4), e="act" if t4 % 2 else "dve")
                        k.dma("sp", dst.re("(tb p) c -> p tb c", p=128)[:, t4 * 4:(t4 + 1) * 4, h * 128:(h + 1) * 128], o)
        k.barrier()


def phase_dnscan(self, l, s):
    k, d = self.k, self.d
    with ExitStack() as st:
        SH = [128, 8, 128]
        ident = k.sb("d3_ident", [128, 128], F32, st)
        identb = k.sb("d3_identb", [128, 128], BF16, st)
        ones = k.sb("d3_ones", [128, 128], F32, st)
        trile = k.sb("d3_trile", [128, 128], F32, st)
        trigt = k.sb("d3_trigt", [128, 128], F32, st)
        nw = k.sb("d3_nw", [128, 128], F32, st)
        k.dma("sp", ident, d["c_ident"])
        k.dma("pool", identb, d["c_ident"])
        k.dma("sp", ones, d["c_ones"])
        k.dma("sp", trile, d["c_tri_le"])
        k.dma("sp", trigt, d["c_tri_gt"])
        k.dma("sp", nw, V(d["norm_w"].b, d["norm_w"].ap[l].partition_broadcast(128)))
        qT = Rot([k.sb(f"d3_qT{i}", SH, F32, st) for i in range(2)])
        kT = Rot([k.sb(f"d3_kT{i}", SH, F32, st) for i in range(2)])
        ktm = Rot([k.sb(f"d3_ktm{i}", SH, F32, st) for i in range(2)])
        vtm = Rot([k.sb(f"d3_vtm{i}", SH, F32, st) for i in range(2)])
        zt = Rot([k.sb(f"d3_z{i}", SH, F32, st) for i in range(2)])
        ba = Rot([k.sb(f"d3_ba{i}", [128, 16], F32, st) for i in range(2)])
        S = k.sb("d3_S", SH, F32, st)
        S1 = k.sb("d3_S1", SH, F32, st)
        gbc = k.sb("d3_gbc", SH, F32, st)
        E = k.sb("d3_E", SH, F32, st)
        Dm = k.sb("d3_D", SH, F32, st)
        DT = k.sb("d3_DT", SH, F32, st)
        X = Rot([k.sb(f"d3_X{i}", SH, F32, st) for i in range(2)])
        Y = Rot([k.sb(f"d3_Y{i}", SH, F32, st) for i in range(2)])
        RT = k.sb("d3_RT", SH, F32, st)
        qkT = k.sb("d3_qkT", SH, F32, st)
        vb = k.sb("d3_vb", SH, F32, st)
        kbg = k.sb("d3_kbg", SH, F32, st)
        kd = k.sb("d3_kd", SH, F32, st)
        uu = k.sb("d3_u", SH, F32, st)
        wT = k.sb("d3_wT", SH, F32, st)
        vnew = k.sb("d3_vnew", SH, F32, st)
        o1 = k.sb("d3_o1", SH, F32, st)
        oo = k.sb("d3_o", SH, F32, st)
        ob = k.sb("d3_ob", SH, BF16, st)
        oT = Rot([k.sb(f"d3_oT{i}", SH, BF16, st) for i in range(2)])
        sm = k.sb("d3_sm", [128, 64], F32, st)
        Gs, eG, eGr, eGl, bg, rn = (sm[:, i * 8:(i + 1) * 8] for i in range(6))
        P = Rot([k.ps(f"d3_P{i}", [128, 8, 128], F32, st) for i in range(3)])
        pm = k.ps("d3_pm", [128, 512], F32, st)
        pT = k.ps("d3_pT", [128, 1024], BF16, st)
        k.memset(S, 0.0)

        def bc8(v):
            return v.re("p (h o) -> p h o", o=1).bc(SH)

        def bcm(v):
            return v.re("p (o c) -> p o c", o=1).bc(SH)

        def mm8(ps, lhs, rhs):
            for h in range(8):
                k.mm(ps[:, h, :], lhs[:, h, :], rhs[:, h, :])

        for ci in range(16):
            csl = slice(ci * 128, (ci + 1) * 128)
            q_, k_, kt_, vt_, z_, ba_ = qT.next(), kT.next(), ktm.next(), vtm.next(), zt.next(), ba.next()
            k.dma("sp", q_, d["dqfm"].re("(h p) t -> p h t", p=128)[:, :, csl])
            k.dma("act", k_, d["dkfm"].re("(h p) t -> p h t", p=128)[:, :, csl])
            k.dma("sp", kt_, d["dktm"][csl, :].re("p (h c) -> p h c", h=8))
            k.dma("act", vt_, d["dvtm"][csl, :].re("p (h c) -> p h c", h=8))
            k.dma("sp", z_, d["ztm"][csl, :].re("p (h c) -> p h c", h=8))
            k.dma("act", ba_, d["batm"][csl, :])
            beta, g = ba_[:, 0:8], ba_[:, 8:16]
            k.mm(pm[:, 0:8], trile, g)
            k.mm(pm[:, 8:16], trigt, g)
            k.mm(pm[:, 16:24], ones, g)
            k.copy(Gs, pm[:, 0:8])
            k.act(sm[:, 8:32], pm[:, 0:24], AF.Exp)
            k.tt(bg, beta, eG, ALU.mult)
            k.copy(gbc, bc8(g))
            pg = P.next()
            for h in range(8):
                k.mm(pg[:, h, :], gbc[:, h, :], trile)
            for h in range(8):
                k.ts(E[:, h, :], pg[:, h, :], Gs[:, h:h + 1], ALU.subtract)
            k.ts(DT, E, 0.0, ALU.min)
            k.ts(Dm, E, -1.0, ALU.mult, 0.0, ALU.min)
            k.act(DT, DT, AF.Exp)
            k.act(Dm, Dm, AF.Exp)
            k.tt(DT, DT, bcm(trile), ALU.mult, e="pool")
            k.tt(Dm, Dm, bcm(trigt), ALU.mult, e="pool")
            pk = P.next()
            mm8(pk, k_, k_)
            X0 = X.next()
            for h in range(8):
                k.stt(X0[:, h, :], pk[:, h, :], beta[:, h:h + 1], Dm[:, h, :], ALU.mult, ALU.mult)
            pq = P.next()
            mm8(pq, k_, q_)
            k.tt(qkT, pq, DT, ALU.mult)
            py = P.next()
            for h in range(8):
                k.tr(py[:, h, :], X0[:, h, :], ident)
            Y0 = Y.next()
            k.copy(Y0, py, e="act")
            k.tt(RT, bcm(ident), Y0, ALU.subtract)
            Xp, Yp = X0, Y0
            for i in range(1, 7):
                px = P.next()
                mm8(px, Yp, Xp)
                Xn = X.next()
                k.copy(Xn, px, e="act")
                if i < 6:
                    pyy = P.next()
                    mm8(pyy, Xp, Yp)
                    Yn = Y.next()
                    k.copy(Yn, pyy)
                pr = P.next()
                mm8(pr, Xn, RT)
                k.tt(RT, RT, pr, ALU.add)
                Xp = Xn
                if i < 6:
                    Yp = Yn
            k.tt(vb, vt_, bc8(beta), ALU.mult, e="pool")
            k.tt(kbg, kt_, bc8(bg), ALU.mult, e="pool")
            k.tt(kd, kt_, bc8(eGr), ALU.mult, e="pool")
            pu = P.next()
            mm8(pu, RT, vb)
            k.copy(uu, pu, e="act")
            pw = P.next()
            mm8(pw, kbg, RT)
            k.copy(wT, pw)
            pv = P.next()
            mm8(pv, wT, S)
            k.tt(vnew, uu, pv, ALU.subtract)
            po1 = P.next()
            mm8(po1, q_, S)
            k.tt(o1, po1, bc8(eG), ALU.mult)
            po2 = P.next()
            mm8(po2, qkT, vnew)
            k.tt(oo, o1, po2, ALU.add)
            pS = P.next()
            mm8(pS, kd, vnew)
            k.tt(S1, S, bc8(eGl), ALU.mult, e="pool")
            k.tt(S, S1, pS, ALU.add)
            k.tt(o1, oo, oo, ALU.mult, e="pool")
            k.op("dve", lambda e: e.tensor_reduce(out=rn.ap, in_=o1.ap, axis=AX.X, op=ALU.add), R=[o1], W=[rn])
            k.ts(rn, rn, 1.0 / 128, ALU.mult, NORM_EPS, ALU.add)
            k.act(rn, rn, AF.Sqrt)
            k.recip(rn, rn)
            k.tt(oo, oo, bc8(rn), ALU.mult)
            k.tt(oo, oo, bcm(nw), ALU.mult, e="pool")
            k.tt(ob, oo, z_, ALU.mult)
            for h in range(8):
                k.tr(pT[:, h * 128:(h + 1) * 128], ob[:, h, :], identb)
            ot = oT.next()
            k.copy(ot, pT.re("p (h t) -> p h t", h=8), e="act")
            k.dma("sp", d["oaT"].re("(h p) t -> p h t", p=128)[:, :, csl], ot)
        k.barrier()


Prog.phase_dnprep = phase_dnprep
Prog.phase_dnscan = phase_dnscan


def emit_all(p):
    for s in range(p.nseq):
        for l in p.layers:
            p.phase_inproj(l, s)
            p.phase_dnprep(l, s)
            p.phase_dnscan(l, s)
            p.phase_nsa(l, s)
            p.phase_outproj(l, s)
            p.phase_ln(l, "ln1_g", "ln1_b", p.d["xres"])
            p.phase_ffn(l, s)
            last = (l == p.layers[-1])
            p.phase_ln(l, "ln2_g", "ln2_b", p.d["out"][s] if last else p.d["xres"])
    p.k.barrier()


def all_consts():
    c = make_consts()
    c.update(nsa_consts())
    c.update(dn_consts())
    return c


def host_params(inp):
    f = lambda a: np.ascontiguousarray(np.asarray(a, dtype=np.float32))
    lnl = lambda a: f(np.asarray(a).reshape(DEPTH, 16, 128).transpose(0, 2, 1))
    m = {
        "w_in": f(inp["w_in"]), "conv_wT": f(np.asarray(inp["dn_conv_w"]).transpose(0, 2, 1)),
        "a_log": f(inp["dn_a_log"]), "dt_bias": f(inp["dn_dt_bias"]), "norm_w": f(inp["dn_norm_w"]),
        "peT_k": f(np.asarray(inp["cmp_pe_k"]).transpose(0, 2, 1)), "w1_k": f(inp["cmp_w1_k"]), "w2_k": f(inp["cmp_w2_k"]),
        "peT_v": f(np.asarray(inp["cmp_pe_v"]).transpose(0, 2, 1)), "w1_v": f(inp["cmp_w1_v"]), "w2_v": f(inp["cmp_w2_v"]),
        "w_a": f(inp["w_branch_a"]), "w_b": f(inp["w_branch_b"]), "w_out": f(inp["w_out"]),
        "w_gate": f(inp["w_ffn_gate"]), "w_up": f(inp["w_ffn_up"]), "w_down": f(inp["w_ffn_down"]),
        "w_ple": f(inp["w_ple"]), "w_pg": f(inp["w_ple_gate"]),
        "ln1_g": lnl(inp["ln1_g"]), "ln1_b": lnl(inp["ln1_b"]), "ln2_g": lnl(inp["ln2_g"]), "ln2_b": lnl(inp["ln2_b"]),
    }
    m.update(all_consts())
    return m


def run(inp, seq_ids_per_core, trace=False):
    nseq = len(seq_ids_per_core[0])
    p = Prog(nseq=nseq)
    emit_all(p)
    shared = host_params(inp)
    x = np.asarray(inp["x"], dtype=np.float32)
    pp = np.asarray(inp["p"], dtype=np.float32)
    in_maps = []
    for ids in seq_ids_per_core:
        m = dict(shared)
        m["xin"] = np.ascontiguousarray(x[ids].transpose(0, 2, 1))
        m["pin"] = np.ascontiguousarray(pp[:, ids].transpose(0, 1, 3, 2))
        in_maps.append({n: v for n, v in m.items() if n in p.d})
    res = run_bass_kernel_spmd(p.nc, in_maps, core_ids=list(range(len(in_maps))), trace=trace)
    outs = [np.ascontiguousarray(r["out"].transpose(0, 2, 1)) for r in res.results]
    return outs, res


def kernel(**inputs):
    B = np.asarray(inputs["x"]).shape[0]
    ids = [[2 * c, 2 * c + 1] for c in range(8)]
    outs, _ = run(inputs, ids)
    out = np.empty((B, T, D), np.float32)
    for c, o in enumerate(outs):
        out[ids[c]] = o
    return out
```

```python
import math
from contextlib import ExitStack
import numpy as np
import ml_dtypes
import concourse.bass as bass
import concourse.mybir as mybir
from concourse.bass_utils import run_bass_kernel_spmd

F32 = mybir.dt.float32
BF16 = mybir.dt.bfloat16
F32R = mybir.dt.float32r
DN_R = False
I32 = mybir.dt.int32
U32 = mybir.dt.uint32
AF = mybir.ActivationFunctionType
ALU = mybir.AluOpType
AX = mybir.AxisListType

SAME_ENGINE_SYNC = True
NDSEM = 90
NDSEM_SW = 30

D = 2048
T = 2048
DEPTH = 2
NSEQ = 2
NIN = 10792
DFF = 5632
ALPHA = (2.0 * DEPTH) ** 0.25
LN_EPS = 1e-5
NORM_EPS = 1e-6
O_DQ, O_DK, O_DV, O_DZ, O_DB, O_DA = 0, 1024, 2048, 3072, 4096, 4104
O_NQ, O_KC, O_VC, O_KS, O_VS, O_KW, O_VW, O_GT, O_MA, O_MB = 4112, 5136, 5392, 5648, 5904, 6160, 6416, 6672, 6696, 8744


class Buf:
    __slots__ = ("name", "w", "r", "dsem", "dcount", "ps")

    def __init__(self, name, ps=False):
        self.name = name
        self.ps = ps
        self.w = None
        self.r = {}
        self.dsem = None
        self.dcount = 0


class V:
    __slots__ = ("b", "ap")

    def __init__(self, b, ap):
        self.b = b
        self.ap = ap

    def __getitem__(self, idx):
        return V(self.b, self.ap[idx])

    def bc(self, shape):
        return V(self.b, self.ap.to_broadcast(list(shape)))

    def re(self, pat, **kw):
        return V(self.b, self.ap.rearrange(pat, **kw))

    def bt(self, dt):
        return V(self.b, self.ap.bitcast(dt))

    def sub(self, name):
        return V(Buf(name), self.ap)


class Eng:
    def __init__(self, name, eng, sem):
        self.name = name
        self.eng = eng
        self.sem = sem
        self.n = 0
        self.known = {}


class K:
    def __init__(self, nc, stack, kinds=None):
        self.nc = nc
        self.stack = stack
        self.kinds = kinds or {}
        self.engs = {}
        for name, eng in (("pe", nc.tensor), ("dve", nc.vector), ("act", nc.scalar),
                          ("pool", nc.gpsimd), ("sp", nc.sync)):
            sem = stack.enter_context(nc.semaphore("sem_" + name))
            self.engs[name] = Eng(name, eng, sem)
        self.dbufs = []
        self.nins = 0
        self.uid = 0
        self.sempool = {"sw": [], "hw": []}
        for i in range(NDSEM):
            sem = stack.enter_context(nc.semaphore(f"dsem{i}"))
            self.sempool["sw" if i < NDSEM_SW else "hw"].append([f"ds{i}", sem, 0])

    def sb(self, name, shape, dt, stack=None):
        self.uid += 1
        t = (stack or self.stack).enter_context(self.nc.sbuf_tensor(f"{name}_{self.uid}", list(shape), dt))
        return V(Buf(name), t[:])

    def ps(self, name, shape, dt=F32, stack=None):
        self.uid += 1
        t = (stack or self.stack).enter_context(self.nc.psum_tensor(f"{name}_{self.uid}", list(shape), dt))
        return V(Buf(name, ps=True), t[:])

    def dram(self, name, shape, dt, kind="Internal"):
        kind = self.kinds.get(name, kind)
        t = self.nc.dram_tensor(name, list(shape), dt, kind=kind)
        return V(Buf(name), t.ap())

    def _wait(self, E, deps):
        need = {}
        for d in deps:
            if d is None:
                continue
            key, sem, val = d
            if key not in need or need[key][1] < val:
                need[key] = (sem, val)
        for key, (sem, val) in need.items():
            if E.known.get(key, 0) >= val:
                continue
            if key == E.name and (E.name == "pe" or not SAME_ENGINE_SYNC):
                continue
            E.eng.wait_ge(sem, val)
            E.known[key] = val

    def _deps(self, R, W, me=None):
        deps = []
        for v in R:
            deps.append(v.b.w)
            if v.b.ps:
                for key, (sem, val) in v.b.r.items():
                    if key != me:
                        deps.append((key, sem, val))
        for v in W:
            deps.append(v.b.w)
            for key, (sem, val) in v.b.r.items():
                deps.append((key, sem, val))
        return deps

    def _mark(self, R, W, tok):
        key, sem, val = tok
        for v in R:
            v.b.r[key] = (sem, val)
        for v in W:
            v.b.w = tok
            v.b.r = {}

    def op(self, e, fn, R=(), W=()):
        E = self.engs[e]
        R = [v for v in R if isinstance(v, V)]
        self._wait(E, self._deps(R, W, E.name))
        ins = fn(E.eng)
        E.n += 1
        ins.then_inc(E.sem, 1)
        self._mark(R, W, (E.name, E.sem, E.n))
        self.nins += 1
        return ins

    def dma(self, e, out, in_, slot=None, **kw):
        E = self.engs[e]
        self._wait(E, self._deps([in_], [out]))
        if slot is not None:
            sb = slot.b
        elif "DRam" in type(out.ap.tensor).__name__:
            sb = in_.b
        else:
            sb = out.b
        kind = "sw" if e == "pool" else "hw"
        if sb.dsem is None:
            sb.dsem = {}
        if kind not in sb.dsem:
            sb.dsem[kind] = self.sempool[kind].pop()
            self.dbufs.append((sb, kind))
        ent = sb.dsem[kind]
        ent[2] += 16
        E.eng.dma_start(out=out.ap, in_=in_.ap, **kw).then_inc(ent[1], 16)
        tok = (ent[0], ent[1], ent[2])
        self._mark([in_], [out], tok)
        self.nins += 1

    def barrier(self):
        toks = [(E.name, E.sem, E.n) for E in self.engs.values() if E.n > 0]
        toks += [tuple(b.dsem[kind]) for b, kind in self.dbufs]
        for E in self.engs.values():
            self._wait(E, toks)
        for b, kind in self.dbufs:
            self.sempool[kind].append(b.dsem.pop(kind))
            b.w = None
            b.r = {}
        self.dbufs = []

    def mm(self, out, lhsT, rhs, start=True, stop=True, **kw):
        return self.op("pe", lambda e: e.matmul(out.ap, lhsT.ap, rhs.ap, start=start, stop=stop, **kw),
                       R=[lhsT, rhs], W=[out])

    def tr(self, out, in_, ident):
        return self.op("pe", lambda e: e.transpose(out.ap, in_.ap, ident.ap), R=[in_, ident], W=[out])

    def act(self, out, in_, func, bias=None, scale=None, accum_out=None):
        kw = {}
        R = [in_]
        if bias is not None:
            if isinstance(bias, V):
                kw["bias"] = bias.ap
                R.append(bias)
            else:
                kw["bias"] = bias
        if scale is not None:
            if isinstance(scale, V):
                kw["scale"] = scale.ap
                R.append(scale)
            else:
                kw["scale"] = scale
        W = [out]
        if accum_out is not None:
            kw["accum_out"] = accum_out.ap
            W.append(accum_out)
        return self.op("act", lambda g: g.activation(out=out.ap, in_=in_.ap, func=func, **kw), R=R, W=W)

    def tt(self, out, a, b, op, e="dve"):
        return self.op(e, lambda g: g.tensor_tensor(out=out.ap, in0=a.ap, in1=b.ap, op=op), R=[a, b], W=[out])

    def ts(self, out, a, s1, op0, s2=None, op1=None, e="dve"):
        R = [a]
        x1 = s1.ap if isinstance(s1, V) else s1
        x2 = s2.ap if isinstance(s2, V) else s2
        if isinstance(s1, V):
            R.append(s1)
        if isinstance(s2, V):
            R.append(s2)
        if op1 is None:
            return self.op(e, lambda g: g.tensor_scalar(out=out.ap, in0=a.ap, scalar1=x1, scalar2=None, op0=op0),
                           R=R, W=[out])
        return self.op(e, lambda g: g.tensor_scalar(out=out.ap, in0=a.ap, scalar1=x1, scalar2=x2, op0=op0, op1=op1),
                       R=R, W=[out])

    def stt(self, out, a, s, b, op0, op1):
        R = [a, b]
        x = s.ap if isinstance(s, V) else s
        if isinstance(s, V):
            R.append(s)
        return self.op("dve", lambda g: g.scalar_tensor_tensor(out=out.ap, in0=a.ap, scalar=x, in1=b.ap,
                                                               op0=op0, op1=op1), R=R, W=[out])

    def copy(self, out, in_, e="dve"):
        if e == "act":
            return self.op("act", lambda g: g.copy(out=out.ap, in_=in_.ap), R=[in_], W=[out])
        return self.op(e, lambda g: g.tensor_copy(out=out.ap, in_=in_.ap), R=[in_], W=[out])

    def memset(self, out, val, e="dve"):
        return self.op(e, lambda g: g.memset(out.ap, val), R=[], W=[out])

    def recip(self, out, in_):
        return self.op("dve", lambda g: g.reciprocal(out=out.ap, in_=in_.ap), R=[in_], W=[out])


def rr(v):
    return v.bt(F32R) if DN_R else v


class Rot:
    def __init__(self, items):
        self.items = items
        self.i = 0

    def next(self):
        v = self.items[self.i % len(self.items)]
        self.i += 1
        return v


def make_consts():
    c = {}
    half = 64
    inv_freq = (10000.0 ** (-np.arange(half, dtype=np.float32) / half)).astype(np.float32)
    ang = np.arange(T, dtype=np.float32)[:, None] * inv_freq[None, :]
    cos = np.cos(ang).astype(np.float32).T
    sin = np.sin(ang).astype(np.float32).T
    c["c_cos"] = np.ascontiguousarray(np.concatenate([cos, cos], 0))
    c["c_sin"] = np.ascontiguousarray(np.concatenate([sin, sin], 0))
    c["c_ident"] = np.eye(128, dtype=np.float32)
    c["c_ones"] = np.ones((128, 128), np.float32)
    P = np.zeros((128, 128), np.float32)
    for m in range(64):
        P[m, m + 64] = -1.0
        P[m + 64, m] = 1.0
    c["c_ropeT"] = np.ascontiguousarray(P.T)
    return c


CONST_SHAPES = {"c_cos": (128, T), "c_sin": (128, T), "c_ident": (128, 128), "c_ones": (128, 128),
                "c_ropeT": (128, 128)}


class Prog:
    def __init__(self, nseq=NSEQ, layers=(0, 1), phases=None, kinds=None):
        self.nseq = nseq
        self.layers = layers
        self.phases = phases
        self.nc = bass.Bass("TRN2", target_bir_lowering=False)
        self.stack = ExitStack()
        self.k = K(self.nc, self.stack, kinds)
        self.declare()

    def declare(self):
        L = DEPTH
        ns = self.nseq
        EI = "ExternalInput"
        sp = {}
        sp["xin"] = ([ns, D, T], F32, EI)
        sp["pin"] = ([L, ns, 256, T], F32, EI)
        sp["w_in"] = ([L, D, NIN], F32, EI)
        sp["conv_wT"] = ([L, 3072, 4], F32, EI)
        sp["a_log"] = ([L, 8], F32, EI)
        sp["dt_bias"] = ([L, 8], F32, EI)
        sp["norm_w"] = ([L, 128], F32, EI)
        for n in ("k", "v"):
            sp["peT_" + n] = ([L, 128, 32], F32, EI)
            sp["w1_" + n] = ([L, 4096, 256], F32, EI)
            sp["w2_" + n] = ([L, 256, 128], F32, EI)
        sp["w_a"] = ([L, 1024, D], F32, EI)
        sp["w_b"] = ([L, 1024, D], F32, EI)
        sp["w_out"] = ([L, D, D], F32, EI)
        sp["w_gate"] = ([L, D, DFF], F32, EI)
        sp["w_up"] = ([L, D, DFF], F32, EI)
        sp["w_down"] = ([L, DFF, D], F32, EI)
        sp["w_ple"] = ([L, 256, D], F32, EI)
        sp["w_pg"] = ([L, D, D], F32, EI)
        for n in ("ln1_g", "ln1_b", "ln2_g", "ln2_b"):
            sp[n] = ([L, 128, 16], F32, EI)
        for n, s_ in CONST_SHAPES.items():
            sp[n] = (list(s_), F32, EI)
        sp["out"] = ([ns, D, T], F32, "ExternalOutput")
        IN = "Internal"
        sp["xres"] = ([D, T], F32, IN)
        sp["dnraw"] = ([3072, T], F32, IN)
        sp["nq"] = ([1024, T], BF16, IN)
        for n in ("kc", "vc", "ks", "kw"):
            sp[n] = ([256, T], BF16, IN)
        sp["mga"] = ([D, T], F32, IN)
        sp["mgb"] = ([D, T], F32, IN)
        sp["ztm"] = ([T, 1024], F32, IN)
        sp["batm"] = ([T, 16], F32, IN)
        sp["vstm"] = ([T, 256], BF16, IN)
        sp["vwtm"] = ([T, 256], BF16, IN)
        sp["gttm"] = ([T, 24], F32, IN)
        sp["dqfm"] = ([1024, T], F32, IN)
        sp["dkfm"] = ([1024, T], F32, IN)
        sp["dktm"] = ([T, 1024], F32, IN)
        sp["dvtm"] = ([T, 1024], F32, IN)
        sp["oaT"] = ([1024, T], BF16, IN)
        sp["obT"] = ([1024, T], BF16, IN)
        sp["pre"] = ([D, T], F32, IN)
        self.specs = sp
        prog = self

        class LazyD(dict):
            def __missing__(self, name):
                shape, dt, kind = prog.specs[name]
                v = prog.k.dram(name, shape, dt, kind)
                self[name] = v
                return v
        self.d = LazyD()

    def wload(self, dst, wv, col0, n, kc):
        self.k.dma("pool", dst[:, :kc, :n], wv.re("(kc p) n -> p kc n", p=128)[:, :, col0:col0 + n])

    def phase_inproj(self, l, s):
        k, d = self.k, self.d
        with ExitStack() as st:
            xT = k.sb("p1_xT", [128, 16, T], BF16, st)
            wb = Rot([k.sb(f"p1_w{i}", [128, 16, 512], BF16, st) for i in range(2)])
            cos = k.sb("p1_cos", [128, T], F32, st)
            sin = k.sb("p1_sin", [128, T], F32, st)
            ropeT = k.sb("p1_ropeT", [128, 128], F32, st)
            stg = Rot([k.sb(f"p1_stg{i}", [128, 512], F32, st) for i in range(4)])
            stb = Rot([k.sb(f"p1_stb{i}", [128, 512], BF16, st) for i in range(4)])
            t1 = Rot([k.sb(f"p1_t1{i}", [128, 512], F32, st) for i in range(2)])
            t2 = Rot([k.sb(f"p1_t2{i}", [128, 512], F32, st) for i in range(2)])
            small = Rot([k.sb(f"p1_sm{i}", [128, 64], F32, st) for i in range(4)])
            dtb = k.sb("p1_dtb", [128, 8], F32, st)
            negA = k.sb("p1_negA", [128, 8], F32, st)
            pss = Rot([k.ps(f"p1_ps{i}", [128, 512], F32, st) for i in range(6)])
            ps2 = Rot([k.ps(f"p1_psr{i}", [128, 512], F32, st) for i in range(2)])

            src = d["xin"][s] if l == 0 else d["xres"]
            for c4 in range(4):
                k.dma("pool", xT[:, c4 * 4:(c4 + 1) * 4, :],
                      src.re("(kc p) t -> p kc t", p=128)[:, c4 * 4:(c4 + 1) * 4, :])
            k.dma("sp", cos, d["c_cos"])
            k.dma("sp", sin, d["c_sin"])
            k.dma("sp", ropeT, d["c_ropeT"])
            k.dma("sp", dtb, V(d["dt_bias"].b, d["dt_bias"].ap[l].partition_broadcast(128)))
            k.dma("sp", negA, V(d["a_log"].b, d["a_log"].ap[l].partition_broadcast(128)))
            k.act(negA, negA, AF.Exp)
            k.ts(negA, negA, -1.0, ALU.mult)
            W = d["w_in"][l]

            fm = [(O_DQ, 3072, "raw", d["dnraw"], 0),
                  (O_NQ, 1024, "ropeq", d["nq"], 0),
                  (O_KC, 256, "rope", d["kc"], 0),
                  (O_VC, 256, "bf", d["vc"], 0),
                  (O_KS, 256, "rope", d["ks"], 0),
                  (O_KW, 256, "rope", d["kw"], 0),
                  (O_MA, 2048, "sig", d["mga"], 0),
                  (O_MB, 2048, "sig", d["mgb"], 0)]
            nev = 0
            for (c0, ncols, kind, dst, r0) in fm:
                for cb in range(0, ncols, 512):
                    nb = min(512, ncols - cb)
                    w = wb.next()
                    self.wload(w, W, c0 + cb, nb, 16)
                    for mc in range(nb // 128):
                        row = r0 + cb + mc * 128
                        for tt in range(4):
                            ps = pss.next()
                            tsl = slice(tt * 512, (tt + 1) * 512)
                            for kc in range(16):
                                k.mm(ps, w[:, kc, mc * 128:(mc + 1) * 128], xT[:, kc, tsl],
                                     start=(kc == 0), stop=(kc == 15))
                            nev += 1
                            if kind == "raw":
                                o = stg.next()
                                if nev % 2:
                                    k.act(o, ps, AF.Copy)
                                else:
                                    k.copy(o, ps)
                                k.dma("sp", dst[row:row + 128, tsl], o)
                            elif kind == "sig":
                                o = stg.next()
                                k.act(o, ps, AF.Sigmoid)
                                k.dma("sp", dst[row:row + 128, tsl], o)
                            elif kind == "bf":
                                o = stb.next()
                                k.copy(o, ps)
                                k.dma("sp", dst[row:row + 128, tsl], o)
                            else:
                                sc = (128.0 ** -0.5) if kind == "ropeq" else 1.0
                                xs = stg.next()
                                k.act(xs, ps, AF.Copy)
                                pr = ps2.next()
                                k.mm(pr, ropeT, xs)
                                a = t1.next()
                                b = t2.next()
                                k.stt(a, xs, sc, cos[:, tsl], ALU.mult, ALU.mult)
                                k.stt(b, pr, sc, sin[:, tsl], ALU.mult, ALU.mult)
                                o = stb.next()
                                k.tt(o, a, b, ALU.add)
                                k.dma("sp", dst[row:row + 128, tsl], o)

            tm = [(O_DZ, 512, "silu", d["ztm"], 0), (O_DZ + 512, 512, "silu", d["ztm"], 512),
                  (O_DB, 16, "ba", d["batm"], 0), (O_VS, 256, "bf", d["vstm"], 0),
                  (O_VW, 256, "bf", d["vwtm"], 0), (O_GT, 24, "sig", d["gttm"], 0)]
            for (c0, ncols, kind, dst, dc0) in tm:
                w = wb.next()
                self.wload(w, W, c0, ncols, 16)
                for t128 in range(16):
                    ps = pss.next()
                    rs = slice(t128 * 128, (t128 + 1) * 128)
                    for kc in range(16):
                        k.mm(ps[:, :ncols], xT[:, kc, rs], w[:, kc, :ncols], start=(kc == 0), stop=(kc == 15))
                    if kind == "silu":
                        o = stg.next()
                        k.act(o, ps, AF.Silu)
                        k.dma("sp", dst[rs, dc0:dc0 + ncols], o)
                    elif kind == "bf":
                        o = stb.next()
                        k.copy(o[:, :ncols], ps[:, :ncols])
                        k.dma("sp", dst[rs, dc0:dc0 + ncols], o[:, :ncols])
                    elif kind == "sig":
                        o = small.next()
                        k.act(o[:, :ncols], ps[:, :ncols], AF.Sigmoid)
                        k.dma("sp", dst[rs, dc0:dc0 + ncols], o[:, :ncols])
                    else:
                        o = small.next()
                        k.act(o[:, 0:8], ps[:, 0:8], AF.Sigmoid)
                        k.tt(o[:, 16:24], ps[:, 8:16], dtb, ALU.add)
                        k.act(o[:, 24:32], o[:, 16:24], AF.Exp)
                        k.act(o[:, 32:40], o[:, 24:32], AF.Ln, bias=1.0)
                        k.tt(o[:, 8:16], o[:, 32:40], negA, ALU.mult)
                        k.dma("sp", dst[rs, 0:16], o[:, 0:16])
            k.barrier()


def build(nseq=NSEQ, layers=(0, 1), phases=None, kinds=None):
    p = Prog(nseq, layers, phases, kinds)
    return p


def phase_outproj(self, l, s):
    k, d = self.k, self.d
    with ExitStack() as st:
        mixT = k.sb("p5_mixT", [128, 16, T], BF16, st)
        pss = Rot([k.ps(f"p5_ps{i}", [128, 512], F32, st) for i in range(6)])
        with ExitStack() as st2:
            oaT = k.sb("p5_oaT", [128, 8, T], BF16, st2)
            obT = k.sb("p5_obT", [128, 8, T], BF16, st2)
            wa = Rot([k.sb(f"p5_wa{i}", [128, 8, 512], BF16, st2) for i in range(2)])
            wb = Rot([k.sb(f"p5_wb{i}", [128, 8, 512], BF16, st2) for i in range(2)])
            ga = Rot([k.sb(f"p5_ga{i}", [128, 512], F32, st2) for i in range(2)])
            gb = Rot([k.sb(f"p5_gb{i}", [128, 512], F32, st2) for i in range(2)])
            m1 = Rot([k.sb(f"p5_m1{i}", [128, 512], F32, st2) for i in range(2)])
            m2 = Rot([k.sb(f"p5_m2{i}", [128, 512], F32, st2) for i in range(2)])
            k.dma("sp", oaT, d["oaT"].re("(c p) t -> p c t", p=128))
            k.dma("sp", obT, d["obT"].re("(c p) t -> p c t", p=128))
            for cb in range(4):
                a, b = wa.next(), wb.next()
                self.wload(a, d["w_a"][l], cb * 512, 512, 8)
                self.wload(b, d["w_b"][l], cb * 512, 512, 8)
                for mc in range(4):
                    ch = cb * 4 + mc
                    for tt in range(4):
                        tsl = slice(tt * 512, (tt + 1) * 512)
                        pa, pb = pss.next(), pss.next()
                        for kc in range(8):
                            k.mm(pa, a[:, kc, mc * 128:(mc + 1) * 128], oaT[:, kc, tsl], start=(kc == 0), stop=(kc == 7))
                        for kc in range(8):
                            k.mm(pb, b[:, kc, mc * 128:(mc + 1) * 128], obT[:, kc, tsl], start=(kc == 0), stop=(kc == 7))
                        g1, g2 = ga.next(), gb.next()
                        k.dma("sp", g1, d["mga"][ch * 128:(ch + 1) * 128, tsl])
                        k.dma("act", g2, d["mgb"][ch * 128:(ch + 1) * 128, tsl])
                        x1, x2 = m1.next(), m2.next()
                        k.tt(x1, pa, g1, ALU.mult)
                        k.tt(x2, pb, g2, ALU.mult)
                        k.tt(mixT[:, ch, tsl], x1, x2, ALU.add, e="pool")
            k.barrier()
        with ExitStack() as st2:
            wo = Rot([k.sb(f"p5_wo{i}", [128, 16, 512], BF16, st2) for i in range(2)])
            xr = Rot([k.sb(f"p5_xr{i}", [128, 512], F32, st2) for i in range(2)])
            og = Rot([k.sb(f"p5_og{i}", [128, 512], F32, st2) for i in range(2)])
            src = d["xin"][s] if l == 0 else d["xres"]
            for cb in range(4):
                w = wo.next()
                self.wload(w, d["w_out"][l], cb * 512, 512, 16)
                for mc in range(4):
                    ch = cb * 4 + mc
                    for tt in range(4):
                        tsl = slice(tt * 512, (tt + 1) * 512)
                        ps = pss.next()
                        for kc in range(16):
                            k.mm(ps, w[:, kc, mc * 128:(mc + 1) * 128], mixT[:, kc, tsl], start=(kc == 0), stop=(kc == 15))
                        x = xr.next()
                        k.dma("act", x, src[ch * 128:(ch + 1) * 128, tsl])
                        o = og.next()
                        k.stt(o, x, ALPHA, ps, ALU.mult, ALU.add)
                        k.dma("sp", d["pre"][ch * 128:(ch + 1) * 128, tsl], o)
            k.barrier()


def phase_ln(self, l, gname, bname, dst):
    k, d = self.k, self.d
    with ExitStack() as st:
        xt = Rot([k.sb(f"p6_x{i}", [128, 16, 512], F32, st) for i in range(3)])
        sqs = Rot([k.sb(f"p6_sq{i}", [128, 16, 512], F32, st) for i in range(2)])
        ones = k.sb("p6_ones", [128, 128], F32, st)
        g = k.sb("p6_g", [128, 16], F32, st)
        b = k.sb("p6_b", [128, 16], F32, st)
        means = Rot([k.sb(f"p6_mean{i}", [128, 512], F32, st) for i in range(2)])
        vars_ = Rot([k.sb(f"p6_var{i}", [128, 512], F32, st) for i in range(2)])
        rstds = Rot([k.sb(f"p6_rstd{i}", [128, 512], F32, st) for i in range(2)])
        ps1s = Rot([k.ps(f"p6_ps1{i}", [128, 512], F32, st) for i in range(2)])
        ps2s = Rot([k.ps(f"p6_ps2{i}", [128, 512], F32, st) for i in range(2)])
        k.dma("sp", ones, d["c_ones"])
        k.dma("sp", g, d[gname][l])
        k.dma("sp", b, d[bname][l])
        for tt in range(4):
            tsl = slice(tt * 512, (tt + 1) * 512)
            x, sq, mean, var, rstd, ps1, ps2 = (xt.next(), sqs.next(), means.next(), vars_.next(), rstds.next(),
                                                ps1s.next(), ps2s.next())
            k.dma("sp" if tt % 2 == 0 else "act", x, d["pre"].re("(c p) t -> p c t", p=128)[:, :, tsl])
            k.act(sq, x, AF.Square)
            for c in range(16):
                k.mm(ps1, rr(ones), rr(x[:, c, :]), start=(c == 0), stop=(c == 15))
            for c in range(16):
                k.mm(ps2, rr(ones), rr(sq[:, c, :]), start=(c == 0), stop=(c == 15))
            k.ts(mean, ps1, 1.0 / D, ALU.mult)
            k.tt(var, mean, mean, ALU.mult)
            k.stt(var, ps2, 1.0 / D, var, ALU.mult, ALU.subtract)
            k.ts(var, var, LN_EPS, ALU.add)
            k.act(var, var, AF.Sqrt)
            k.recip(rstd, var)
            for c in range(16):
                e = "pool" if c % 3 == 2 else "dve"
                k.tt(x[:, c, :], x[:, c, :], mean, ALU.subtract, e=e)
                k.tt(x[:, c, :], x[:, c, :], rstd, ALU.mult, e=e)
                k.act(x[:, c, :], x[:, c, :], AF.Identity, scale=g[:, c:c + 1], bias=b[:, c:c + 1])
            k.dma("sp", dst.re("(c p) t -> p c t", p=128)[:, :, tsl], x)
        k.barrier()


def phase_ffn(self, l, s):
    k, d = self.k, self.d
    for half in range(2):
        t0 = half * 1024
        with ExitStack() as st:
            xT = k.sb("p7_xT", [128, 16, 1024], BF16, st)
            pT = k.sb("p7_pT", [128, 2, 1024], BF16, st)
            hT = k.sb("p7_hT", [128, 44, 1024], BF16, st)
            pss = Rot([k.ps(f"p7_ps{i}", [128, 512], F32, st) for i in range(8)])
            k.dma("pool", xT, d["xres"].re("(c p) t -> p c t", p=128)[:, :, t0:t0 + 1024])
            k.dma("pool", pT, d["pin"][l, s].re("(c p) t -> p c t", p=128)[:, :, t0:t0 + 1024])
            with ExitStack() as st2:
                wg = Rot([k.sb(f"p7_wg{i}", [128, 16, 256], BF16, st2) for i in range(2)])
                wu = Rot([k.sb(f"p7_wu{i}", [128, 16, 256], BF16, st2) for i in range(2)])
                sg = Rot([k.sb(f"p7_sg{i}", [128, 512], F32, st2) for i in range(3)])
                for cb in range(22):
                    g_, u_ = wg.next(), wu.next()
                    self.wload(g_, d["w_gate"][l], cb * 256, 256, 16)
                    self.wload(u_, d["w_up"][l], cb * 256, 256, 16)
                    for mc in range(2):
                        j = cb * 2 + mc
                        for tt in range(2):
                            tsl = slice(tt * 512, (tt + 1) * 512)
                            pg, pu = pss.next(), pss.next()
                            for kc in range(16):
                                k.mm(pg, g_[:, kc, mc * 128:(mc + 1) * 128], xT[:, kc, tsl], start=(kc == 0), stop=(kc == 15))
                            for kc in range(16):
                                k.mm(pu, u_[:, kc, mc * 128:(mc + 1) * 128], xT[:, kc, tsl], start=(kc == 0), stop=(kc == 15))
                            s_ = sg.next()
                            k.act(s_, pg, AF.Silu)
                            k.tt(hT[:, j, tsl], s_, pu, ALU.mult)
                k.barrier()
            with ExitStack() as st2:
                wd = Rot([k.sb(f"p7_wd{i}", [128, 44, 128], BF16, st2) for i in range(2)])
                wpg = Rot([k.sb(f"p7_wpg{i}", [128, 16, 128], BF16, st2) for i in range(2)])
                wpl = Rot([k.sb(f"p7_wpl{i}", [128, 2, 128], BF16, st2) for i in range(2)])
                sgm = Rot([k.sb(f"p7_sgm{i}", [128, 512], F32, st2) for i in range(2)])
                ple = Rot([k.sb(f"p7_ple{i}", [128, 512], F32, st2) for i in range(2)])
                xr = Rot([k.sb(f"p7_xr{i}", [128, 512], F32, st2) for i in range(2)])
                acc = Rot([k.sb(f"p7_acc{i}", [128, 512], F32, st2) for i in range(2)])
                for m in range(16):
                    w1, w2, w3 = wd.next(), wpg.next(), wpl.next()
                    self.wload(w1, d["w_down"][l], m * 128, 128, 44)
                    self.wload(w2, d["w_pg"][l], m * 128, 128, 16)
                    self.wload(w3, d["w_ple"][l], m * 128, 128, 2)
                    for tt in range(2):
                        tsl = slice(tt * 512, (tt + 1) * 512)
                        gsl = slice(t0 + tt * 512, t0 + (tt + 1) * 512)
                        pf, pgt, pp = pss.next(), pss.next(), pss.next()
                        for j in range(44):
                            k.mm(pf, w1[:, j, :], hT[:, j, tsl], start=(j == 0), stop=(j == 43))
                        for kc in range(16):
                            k.mm(pgt, w2[:, kc, :], xT[:, kc, tsl], start=(kc == 0), stop=(kc == 15))
                        for c in range(2):
                            k.mm(pp, w3[:, c, :], pT[:, c, tsl], start=(c == 0), stop=(c == 1))
                        s_ = sgm.next()
                        k.act(s_, pgt, AF.Sigmoid)
                        pl = ple.next()
                        k.tt(pl, s_, pp, ALU.mult)
                        x = xr.next()
                        k.dma("act", x, d["xres"][m * 128:(m + 1) * 128, gsl])
                        a = acc.next()
                        k.stt(a, x, ALPHA, pf, ALU.mult, ALU.add)
                        k.tt(a, a, pl, ALU.add, e="pool")
                        k.dma("sp", d["pre"][m * 128:(m + 1) * 128, gsl], a)
                k.barrier()


Prog.phase_outproj = phase_outproj
Prog.phase_ln = phase_ln
Prog.phase_ffn = phase_ffn


NEG = -30000.0


def nsa_consts():
    c = {}
    n = np.arange(127)
    t = np.arange(T)
    c["c_cmpmask"] = np.where((16 * n[:, None] + 31) <= t[None, :], 0.0, NEG).astype(np.float32)
    i = np.arange(T)
    c["c_eexp"] = ((i[None, :] // 64) == np.arange(32)[:, None]).astype(np.float32)
    a = np.arange(128)
    c["c_causb"] = np.where(a[:, None] <= a[None, :], 0.0, NEG).astype(np.float32)
    c["c_winb"] = np.where(a[:, None] > a[None, :], 0.0, NEG).astype(np.float32)
    j = np.arange(32)
    c["c_overlap"] = ((16 * n[:, None] <= 64 * j[None, :] + 63) & (16 * n[:, None] + 31 >= 64 * j[None, :])
                      ).astype(np.float32)
    cur = t // 64
    forced = (j[None, :] == 0) | (j[None, :] == cur[:, None]) | (j[None, :] == cur[:, None] - 1)
    causal = j[None, :] <= cur[:, None]
    c["c_tk_m"] = (causal & ~forced).astype(np.float32)
    c["c_tk_a"] = np.where(forced, 100.0, np.where(causal, 0.0, -100.0)).astype(np.float32)
    return c


CONST_SHAPES.update({"c_cmpmask": (127, T), "c_eexp": (32, T), "c_causb": (128, 128), "c_winb": (128, 128),
                     "c_overlap": (127, 32), "c_tk_m": (T, 32), "c_tk_a": (T, 32)})


def phase_nsa(self, l, s):
    k, d = self.k, self.d
    for g in range(2):
        with ExitStack() as st:
            gsl = slice(g * 128, (g + 1) * 128)
            identb = k.sb("n_identb", [128, 128], BF16, st)
            cmpmask = k.sb("n_cmpmask", [128, T], BF16, st)
            eexp = k.sb("n_eexp", [32, T], BF16, st)
            causb = k.sb("n_causb", [128, 128], BF16, st)
            winb = k.sb("n_winb", [128, 128], BF16, st)
            tkm = k.sb("n_tkm", [128, 16, 32], F32, st)
            tka = k.sb("n_tka", [128, 16, 32], F32, st)
            k.dma("pool", identb, d["c_ident"])
            k.dma("pool", cmpmask[:127, :], d["c_cmpmask"])
            k.dma("pool", eexp, d["c_eexp"])
            k.dma("pool", causb, d["c_causb"])
            k.dma("pool", winb, d["c_winb"])
            k.dma("sp", tkm, d["c_tk_m"].re("(q p) j -> p q j", p=128))
            k.dma("sp", tka, d["c_tk_a"].re("(q p) j -> p q j", p=128))
            q4 = k.sb("n_q4", [128, 4, T], BF16, st)
            ksT = k.sb("n_ksT", [128, T], BF16, st)
            kwT = k.sb("n_kwT", [128, T], BF16, st)
            Vs = k.sb("n_Vs", [128, 16, 132], BF16, st)
            Vw = k.sb("n_Vw", [128, 16, 132], BF16, st)
            gt = k.sb("n_gt", [128, 16, 24], F32, st)
            k.dma("sp", q4, d["nq"].re("(h p) t -> p h t", p=128)[:, g * 4:(g + 1) * 4, :])
            k.dma("sp", ksT, d["ks"][gsl, :])
            k.dma("sp", kwT, d["kw"][gsl, :])
            k.dma("act", Vs[:, :, 0:128], d["vstm"].re("(kb p) c -> p kb c", p=128)[:, :, gsl])
            k.dma("act", Vw[:, :, 0:128], d["vwtm"].re("(kb p) c -> p kb c", p=128)[:, :, gsl])
            k.memset(Vs[:, :, 128:129], 1.0)
            k.memset(Vw[:, :, 128:129], 1.0)
            k.dma("sp", gt, d["gttm"].re("(q p) c -> p q c", p=128))
            import os as _os
            stop = int(_os.environ.get("NSA_STOP", "99"))
            if stop == 1:
                k.barrier()
                return
            kcmpT = k.sb("n_kcmpT", [128, 128], BF16, st)
            rcmp = k.sb("n_rcmp", [128, 161], BF16, st)
            k.memset(rcmp[:, 128:129], 1.0)
            k.dma("pool", rcmp[:127, 129:161], d["c_overlap"])
            ST = Rot([k.ps(f"n_st{i}", [128, 512], F32, st) for i in range(3)])
            PV = Rot([k.ps(f"n_pv{i}", [128, 2, 512], F32, st) for i in range(2)])
            psT = k.ps("n_psT", [128, 1024], BF16, st)

            with ExitStack() as st2:
                xT = k.sb("n_cx", [128, T], BF16, st2)
                w1 = k.sb("n_w1", [128, 32, 256], BF16, st2)
                w2 = k.sb("n_w2", [128, 2, 128], BF16, st2)
                peT = k.sb("n_peT", [128, 32], BF16, st2)
                hT = k.sb("n_hT", [128, 2, 128], BF16, st2)
                b1 = k.sb("n_b1", [128, 2], F32, st2)
                for nm in ("k", "v"):
                    k.dma("sp", xT, d["kc" if nm == "k" else "vc"][gsl, :])
                    k.dma("pool", w1, d["w1_" + nm][l].re("(l d) h -> d l h", d=128))
                    k.dma("pool", w2, d["w2_" + nm][l].re("(c p) d -> p c d", p=128))
                    k.dma("pool", peT, d["peT_" + nm][l])
                    pb = ST.next()
                    for mh in range(2):
                        for li in range(32):
                            k.mm(pb[:, mh:mh + 1], w1[:, li, mh * 128:(mh + 1) * 128], peT[:, li:li + 1],
                                 start=(li == 0), stop=(li == 31))
                    k.copy(b1, pb[:, 0:2])
                    for mh in range(2):
                        ph = ST.next()
                        for li in range(32):
                            k.mm(ph[:, :127], w1[:, li, mh * 128:(mh + 1) * 128], xT[:, li:li + 16 * 126 + 1:16],
                                 start=(li == 0), stop=(li == 31))
                        k.act(hT[:, mh, :127], ph[:, :127], AF.Silu, bias=b1[:, mh:mh + 1])
                    po = ST.next()
                    if nm == "k":
                        for mh in range(2):
                            k.mm(po[:, :127], w2[:, mh, :], hT[:, mh, :127], start=(mh == 0), stop=(mh == 1))
                        k.copy(kcmpT[:, :127], po[:, :127])
                    else:
                        for mh in range(2):
                            k.mm(po[:127, :128], hT[:, mh, :127], w2[:, mh, :], start=(mh == 0), stop=(mh == 1))
                        k.copy(rcmp[:127, 0:128], po[:127, :128])

            if stop == 2:
                k.barrier()
                return
            Ec = k.sb("n_Ec", [128, 512], BF16, st)
            Es = k.sb("n_Es", [128, 16, 512], BF16, st)
            Ew = k.sb("n_Ew", [128, 5, 512], BF16, st)
            rden = k.sb("n_rden", [128, 4], F32, st)
            cf = k.sb("n_cf", [128, 4], F32, st)
            imp = k.sb("n_imp", [128, 32], F32, st)
            vv = k.sb("n_vv", [128, 32], F32, st)
            v2 = k.sb("n_v2", [128, 32], F32, st)
            m8 = k.sb("n_m8", [128, 16], F32, st)
            nsel = k.sb("n_nsel", [128, 32], BF16, st)
            nsT = k.sb("n_nsT", [32, 128], BF16, st)
            acc = k.sb("n_acc", [128, 4, 128], F32, st)
            accb = k.sb("n_accb", [128, 4, 128], BF16, st)
            obs = Rot([k.sb(f"n_obs{i}", [128, 4, 128], BF16, st) for i in range(2)])

            def finish(pv, width, branch, first):
                pv4 = pv.re("p b (x c) -> p (b x) c", c=256)
                k.ts(rden.re("p (h o) -> p h o", o=1), pv4[:, :, 128:129], 1e-30, ALU.max)
                k.recip(rden, rden)
                k.tt(cf.re("p (h o) -> p h o", o=1), rden.re("p (h o) -> p h o", o=1),
                     gt[:, qb, g * 12 + branch:g * 12 + 12:3].re("p (h o) -> p h o", o=1), ALU.mult)
                for h in range(4):
                    if first:
                        k.act(acc[:, h, :], pv4[:, h, 0:128], AF.Copy, scale=cf[:, h:h + 1])
                    else:
                        k.stt(acc[:, h, :], pv4[:, h, 0:128], cf[:, h:h + 1], acc[:, h, :], ALU.mult, ALU.add)
                return pv4

            for qb in range(16):
                qsl = slice(qb * 128, (qb + 1) * 128)
                qr = q4[:, :, qsl]
                ps = ST.next()
                k.mm(ps[:127, :], kcmpT[:, :127], qr, start=True, stop=False)
                k.mm(ps[:127, :], identb[:127, :127], cmpmask[:127, qsl].re("p (o t) -> p o t", o=1).bc([127, 4, 128]),
                     start=False, stop=True)
                k.act(Ec[:127, :], ps[:127, :], AF.Exp)
                pv = PV.next()
                pv4 = pv.re("p b (x c) -> p (b x) c", c=256)
                for h in range(4):
                    k.mm(pv4[:, h, 0:161], Ec[:127, h * 128:(h + 1) * 128], rcmp[:127, :], start=True, stop=True)
                finish(pv, 161, 0, True)
                if stop == 3 or (stop == 16 and qb == 8):
                    k.barrier()
                    return
                sel_on = qb >= 8
                if sel_on:
                    def chk(n):
                        if stop == n:
                            k.barrier()
                            return True
                        return False
                    k.ts(imp, pv4[:, 0, 129:161], rden[:, 0:1], ALU.mult)
                    if chk(17): return
                    for h in range(1, 4):
                        k.stt(imp, pv4[:, h, 129:161], rden[:, h:h + 1], imp, ALU.mult, ALU.add)
                    if chk(10): return
                    k.tt(vv, imp, tkm[:, qb, :], ALU.mult)
                    k.tt(vv, vv, tka[:, qb, :], ALU.add)
                    if chk(11): return
                    k.op("dve", lambda e: e.max(out=m8[:, 0:8].ap, in_=vv.ap), R=[vv], W=[m8])
                    if chk(12): return
                    k.op("dve", lambda e: e.match_replace(out=v2.ap, in_to_replace=m8[:, 0:8].ap, in_values=vv.ap,
                                                          imm_value=-1e9), R=[vv, m8], W=[v2])
                    if chk(13): return
                    k.op("dve", lambda e: e.max(out=m8[:, 8:16].ap, in_=v2.ap), R=[v2], W=[m8])
                    k.ts(v2, vv, m8[:, 15:16], ALU.is_ge)
                    if chk(14): return
                    k.ts(nsel, v2, 1.0, ALU.subtract, -NEG, ALU.mult)
                    if chk(15): return
                    k.tr(psT[:32, 0:128], nsel, identb)
                    k.copy(nsT, psT[:32, 0:128])
                    if stop == 8:
                        k.barrier()
                        return
                for kb in range(qb + 1):
                    ps = ST.next()
                    ksl = slice(kb * 128, (kb + 1) * 128)
                    last = (not sel_on) and kb != qb
                    k.mm(ps, ksT[:, ksl], qr, start=True, stop=last)
                    if sel_on:
                        k.mm(ps, eexp[:, ksl], nsT.re("p (o t) -> p o t", o=1).bc([32, 4, 128]),
                             start=False, stop=(kb != qb))
                    if kb == qb:
                        k.mm(ps, identb, causb.re("p (o t) -> p o t", o=1).bc([128, 4, 128]), start=False, stop=True)
                    k.act(Es[:, kb, :], ps, AF.Exp)
                pv = PV.next()
                pv4 = pv.re("p b (x c) -> p (b x) c", c=256)
                for h in range(4):
                    for kb in range(qb + 1):
                        k.mm(pv4[:, h, 0:129], Es[:, kb, h * 128:(h + 1) * 128], Vs[:, kb, 0:129],
                             start=(kb == 0), stop=(kb == qb))
                finish(pv, 129, 1, False)
                if stop == 4 or (stop == 6 and qb == 8):
                    k.barrier()
                    return
                kb0 = max(0, qb - 4)
                for kb in range(kb0, qb + 1):
                    ps = ST.next()
                    ksl = slice(kb * 128, (kb + 1) * 128)
                    mk = causb if kb == qb else (winb if kb == qb - 4 else None)
                    k.mm(ps, kwT[:, ksl], qr, start=True, stop=(mk is None))
                    if mk is not None:
                        k.mm(ps, identb, mk.re("p (o t) -> p o t", o=1).bc([128, 4, 128]), start=False, stop=True)
                    k.act(Ew[:, kb - kb0, :], ps, AF.Exp)
                pv = PV.next()
                pv4 = pv.re("p b (x c) -> p (b x) c", c=256)
                for h in range(4):
                    for kb in range(kb0, qb + 1):
                        k.mm(pv4[:, h, 0:129], Ew[:, kb - kb0, h * 128:(h + 1) * 128], Vw[:, kb, 0:129],
                             start=(kb == kb0), stop=(kb == qb))
                finish(pv, 129, 2, False)
                if stop == 5:
                    k.barrier()
                    return
                k.copy(accb, acc)
                for h in range(4):
                    k.tr(psT[:, h * 128:(h + 1) * 128], accb[:, h, :], identb)
                ob = obs.next()
                k.copy(ob, psT[:, 0:512].re("p (h t) -> p h t", h=4))
                k.dma("sp", d["obT"].re("(h p) t -> p h t", p=128)[:, g * 4:(g + 1) * 4, qsl], ob)
                if stop == 7 or (stop == 9 and qb == 7):
                    k.barrier()
                    return
            k.barrier()


Prog.phase_nsa = phase_nsa


def dn_consts():
    c = {}
    a = np.arange(128)
    c["c_tri_le"] = (a[:, None] <= a[None, :]).astype(np.float32)
    c["c_tri_gt"] = (a[:, None] > a[None, :]).astype(np.float32)
    return c


CONST_SHAPES.update({"c_tri_le": (128, 128), "c_tri_gt": (128, 128)})


def phase_dnprep(self, l, s):
    k, d = self.k, self.d
    with ExitStack() as st:
        cw = k.sb("d2_cw", [128, 24, 4], F32, st)
        ones = k.sb("d2_ones", [128, 128], F32, st)
        ident = k.sb("d2_ident", [128, 128], F32, st)
        up = Rot([k.sb(f"d2_up{i}", [128, T + 3], F32, st) for i in range(2)])
        ys = Rot([k.sb(f"d2_y{i}", [128, T], F32, st) for i in range(2)])
        sq = k.sb("d2_sq", [128, T], F32, st)
        rs = Rot([k.sb(f"d2_rs{i}", [128, 512], F32, st) for i in range(2)])
        tms = Rot([k.sb(f"d2_tm{i}", [128, 4, 128], F32, st) for i in range(2)])
        pss = Rot([k.ps(f"d2_ps{i}", [128, 512], F32, st) for i in range(4)])
        k.dma("sp", cw, d["conv_wT"][l].re("(c p) j -> p c j", p=128))
        k.dma("sp", ones, d["c_ones"])
        k.dma("sp", ident, d["c_ident"])
        for u in up.items:
            k.memset(u[:, 0:3], 0.0)
        for which in range(3):
            for h in range(8):
                c = which * 8 + h
                u = up.next()
                k.dma("sp" if c % 2 else "act", u[:, 3:3 + T], d["dnraw"][c * 128:(c + 1) * 128, :])
                y = ys.next()
                k.ts(y, u[:, 3:3 + T], cw[:, c, 3:4], ALU.mult)
                for j in range(3):
                    k.stt(y, u[:, j:j + T], cw[:, c, j:j + 1], y, ALU.mult, ALU.add)
                k.act(y, y, AF.Silu)
                if which < 2:
                    k.act(sq, y, AF.Square)
                    for tt in range(4):
                        tsl = slice(tt * 512, (tt + 1) * 512)
                        ps = pss.next()
                        k.mm(ps, rr(ones), rr(sq[:, tsl]))
                        r = rs.next()
                        k.ts(r, ps, NORM_EPS, ALU.add)
                        k.act(r, r, AF.Sqrt)
                        k.recip(r, r)
                        if which == 0:
                            k.stt(y[:, tsl], y[:, tsl], 128.0 ** -0.5, r, ALU.mult, ALU.mult)
                        else:
                            k.tt(y[:, tsl], y[:, tsl], r, ALU.mult)
                    k.dma("sp", d["dqfm" if which == 0 else "dkfm"][h * 128:(h + 1) * 128, :], y)
                if which >= 1:
                    dst = d["dktm" if which == 1 else "dvtm"]
                    for t4 in range(4):
                        ps = pss.next()
                        for i in range(4):
                            tb = t4 * 4 + i
                            k.tr(ps[:, i * 128:(i + 1) * 128], y[:, tb * 128:(tb + 1) * 128], ident)
                        o = tms.next()
                        k.copy(o, ps.re("p (i c) -> p i c", i=4), e="act" if t4 % 2 else "dve")
                        k.dma("sp", dst.re("(tb p) c -> p tb c", p=128)[:, t4 * 4:(t4 + 1) * 4, h * 128:(h + 1) * 128], o)
        k.barrier()


def phase_dnscan(self, l, s):
    k, d = self.k, self.d
    with ExitStack() as st:
        SH = [128, 8, 128]
        GH = [128, 4, 128]
        ident = k.sb("d3_ident", [128, 128], F32, st)
        identb = k.sb("d3_identb", [128, 128], BF16, st)
        ones = k.sb("d3_ones", [128, 128], F32, st)
        trile = k.sb("d3_trile", [128, 128], F32, st)
        trigt = k.sb("d3_trigt", [128, 128], F32, st)
        nw = k.sb("d3_nw", [128, 128], F32, st)
        k.dma("sp", ident, d["c_ident"])
        k.dma("pool", identb, d["c_ident"])
        k.dma("sp", ones, d["c_ones"])
        k.dma("sp", trile, d["c_tri_le"])
        k.dma("sp", trigt, d["c_tri_gt"])
        k.dma("sp", nw, V(d["norm_w"].b, d["norm_w"].ap[l].partition_broadcast(128)))
        qT = Rot([k.sb(f"d3_qT{i}", SH, F32, st) for i in range(2)])
        kT = Rot([k.sb(f"d3_kT{i}", SH, F32, st) for i in range(2)])
        ktm = Rot([k.sb(f"d3_ktm{i}", SH, F32, st) for i in range(2)])
        vtm = Rot([k.sb(f"d3_vtm{i}", SH, F32, st) for i in range(2)])
        zt = Rot([k.sb(f"d3_z{i}", SH, F32, st) for i in range(2)])
        ba = Rot([k.sb(f"d3_ba{i}", [128, 16], F32, st) for i in range(2)])
        gbc = k.sb("d3_gbc", SH, F32, st)
        sm = k.sb("d3_sm", [128, 48], F32, st)
        Gs, eG, eGr, eGl, bg = (sm[:, i * 8:(i + 1) * 8] for i in range(5))
        pm = k.ps("d3_pm", [128, 512], F32, st)
        pT = k.ps("d3_pT", [128, 1024], BF16, st)

        class G:
            pass
        grp = []
        for hg in range(2):
            o = G()
            for nm in ("S", "S1", "E", "Dm", "DT", "RT", "qkT", "vb", "kbg", "kd", "uu", "wT", "vnew", "o1", "oo"):
                setattr(o, nm, k.sb(f"d3_{nm}{hg}", GH, F32, st))
            o.ob = k.sb(f"d3_ob{hg}", GH, BF16, st)
            o.X = Rot([k.sb(f"d3_X{hg}{i}", GH, F32, st) for i in range(2)])
            o.Y = Rot([k.sb(f"d3_Y{hg}{i}", GH, F32, st) for i in range(2)])
            o.oT = Rot([k.sb(f"d3_oT{hg}{i}", GH, BF16, st) for i in range(2)])
            o.rn = k.sb(f"d3_rn{hg}", [128, 4], F32, st)
            o.P = Rot([k.ps(f"d3_P{hg}{i}", GH, F32, st) for i in range(3)])
            k.memset(o.S, 0.0)
            grp.append(o)

        def bc4(v):
            return v.re("p (h o) -> p h o", o=1).bc(GH)

        def bc8(v):
            return v.re("p (h o) -> p h o", o=1).bc(SH)

        def bcm(v):
            return v.re("p (o c) -> p o c", o=1).bc(GH)

        def mm4(ps, lhs, rhs):
            for h in range(4):
                k.mm(ps[:, h, :], lhs[:, h, :], rhs[:, h, :])

        def chain(o, hg, q_, k_, kt_, vt_, z_, beta, csl):
            hs = slice(hg * 4, hg * 4 + 4)
            q4, k4, kt4, vt4, z4 = q_[:, hs, :], k_[:, hs, :], kt_[:, hs, :], vt_[:, hs, :], z_[:, hs, :]
            P = o.P
            pg = P.next()
            for h in range(4):
                k.mm(pg[:, h, :], gbc[:, hg * 4 + h, :], trile)
            yield
            for h in range(4):
                k.ts(o.E[:, h, :], pg[:, h, :], Gs[:, hg * 4 + h:hg * 4 + h + 1], ALU.subtract)
            k.ts(o.DT, o.E, 0.0, ALU.min)
            k.ts(o.Dm, o.E, -1.0, ALU.mult, 0.0, ALU.min)
            k.act(o.DT, o.DT, AF.Exp)
            k.act(o.Dm, o.Dm, AF.Exp)
            k.tt(o.DT, o.DT, bcm(trile), ALU.mult, e="pool")
            k.tt(o.Dm, o.Dm, bcm(trigt), ALU.mult, e="pool")
            pk = P.next()
            mm4(pk, k4, k4)
            yield
            X0 = o.X.next()
            for h in range(4):
                k.stt(X0[:, h, :], pk[:, h, :], beta[:, hg * 4 + h:hg * 4 + h + 1], o.Dm[:, h, :], ALU.mult, ALU.mult)
            pq = P.next()
            mm4(pq, k4, q4)
            py = P.next()
            for h in range(4):
                k.tr(py[:, h, :], X0[:, h, :], ident)
            yield
            k.tt(o.qkT, pq, o.DT, ALU.mult)
            Y0 = o.Y.next()
            k.copy(Y0, py, e="act")
            k.tt(o.RT, bcm(ident), Y0, ALU.subtract)
            Xp, Yp = X0, Y0
            for i in range(1, 7):
                px = P.next()
                mm4(px, Yp, Xp)
                if i < 6:
                    pyy = P.next()
                    mm4(pyy, Xp, Yp)
                yield
                Xn = o.X.next()
                k.copy(Xn, px, e="act")
                if i < 6:
                    Yn = o.Y.next()
                    k.copy(Yn, pyy)
                pr = P.next()
                mm4(pr, Xn, o.RT)
                yield
                k.tt(o.RT, o.RT, pr, ALU.add)
                Xp = Xn
                if i < 6:
                    Yp = Yn
            k.tt(o.vb, vt4, bc4(beta[:, hs]), ALU.mult, e="pool")
            k.tt(o.kbg, kt4, bc4(bg[:, hs]), ALU.mult, e="pool")
            k.tt(o.kd, kt4, bc4(eGr[:, hs]), ALU.mult, e="pool")
            pu = P.next()
            mm4(pu, o.RT, o.vb)
            pw = P.next()
            mm4(pw, o.kbg, o.RT)
            yield
            k.copy(o.uu, pu, e="act")
            k.copy(o.wT, pw)
            pv = P.next()
            mm4(pv, o.wT, o.S)
            po1 = P.next()
            mm4(po1, q4, o.S)
            yield
            k.tt(o.vnew, o.uu, pv, ALU.subtract)
            k.tt(o.o1, po1, bc4(eG[:, hs]), ALU.mult)
            po2 = P.next()
            mm4(po2, o.qkT, o.vnew)
            pS = P.next()
            mm4(pS, o.kd, o.vnew)
            k.tt(o.S1, o.S, bc4(eGl[:, hs]), ALU.mult, e="pool")
            yield
            k.tt(o.oo, o.o1, po2, ALU.add)
            k.tt(o.S, o.S1, pS, ALU.add)
            k.tt(o.o1, o.oo, o.oo, ALU.mult, e="pool")
            k.op("dve", lambda e: e.tensor_reduce(out=o.rn.ap, in_=o.o1.ap, axis=AX.X, op=ALU.add), R=[o.o1], W=[o.rn])
            k.ts(o.rn, o.rn, 1.0 / 128, ALU.mult, NORM_EPS, ALU.add)
            k.act(o.rn, o.rn, AF.Sqrt)
            k.recip(o.rn, o.rn)
            yield
            k.tt(o.oo, o.oo, bc4(o.rn), ALU.mult)
            k.tt(o.oo, o.oo, bcm(nw), ALU.mult, e="pool")
            k.tt(o.ob, o.oo, z4, ALU.mult)
            yield
            for h in range(4):
                k.tr(pT[:, (hg * 4 + h) * 128:(hg * 4 + h + 1) * 128], o.ob[:, h, :], identb)
            ot = o.oT.next()
            k.copy(ot, pT[:, hg * 512:(hg + 1) * 512].re("p (h t) -> p h t", h=4), e="act")
            k.dma("sp", d["oaT"].re("(h p) t -> p h t", p=128)[:, hs, csl], ot)

        for ci in range(16):
            csl = slice(ci * 128, (ci + 1) * 128)
            q_, k_, kt_, vt_, z_, ba_ = qT.next(), kT.next(), ktm.next(), vtm.next(), zt.next(), ba.next()
            k.dma("sp", q_, d["dqfm"].re("(h p) t -> p h t", p=128)[:, :, csl])
            k.dma("act", k_, d["dkfm"].re("(h p) t -> p h t", p=128)[:, :, csl])
            k.dma("sp", kt_, d["dktm"][csl, :].re("p (h c) -> p h c", h=8))
            k.dma("act", vt_, d["dvtm"][csl, :].re("p (h c) -> p h c", h=8))
            k.dma("sp", z_, d["ztm"][csl, :].re("p (h c) -> p h c", h=8))
            k.dma("act", ba_, d["batm"][csl, :])
            beta, g = ba_[:, 0:8], ba_[:, 8:16]
            k.mm(pm[:, 0:8], trile, g)
            k.mm(pm[:, 8:16], trigt, g)
            k.mm(pm[:, 16:24], ones, g)
            k.copy(Gs, pm[:, 0:8])
            k.act(sm[:, 8:32], pm[:, 0:24], AF.Exp)
            k.tt(bg, beta, eG, ALU.mult)
            k.copy(gbc, bc8(g))
            gens = [chain(grp[hg], hg, q_, k_, kt_, vt_, z_, beta, csl) for hg in range(2)]
            while gens:
                for gn in list(gens):
                    try:
                        next(gn)
                    except StopIteration:
                        gens.remove(gn)
        k.barrier()


Prog.phase_dnprep = phase_dnprep
Prog.phase_dnscan = phase_dnscan


def emit_all(p):
    for s in range(p.nseq):
        for l in p.layers:
            p.phase_inproj(l, s)
            p.phase_dnprep(l, s)
            p.phase_dnscan(l, s)
            p.phase_nsa(l, s)
            p.phase_outproj(l, s)
            p.phase_ln(l, "ln1_g", "ln1_b", p.d["xres"])
            p.phase_ffn(l, s)
            last = (l == p.layers[-1])
            p.phase_ln(l, "ln2_g", "ln2_b", p.d["out"][s] if last else p.d["xres"])
    p.k.barrier()


def all_consts():
    c = make_consts()
    c.update(nsa_consts())
    c.update(dn_consts())
    return c


def host_params(inp):
    f = lambda a: np.ascontiguousarray(np.asarray(a, dtype=np.float32))
    lnl = lambda a: f(np.asarray(a).reshape(DEPTH, 16, 128).transpose(0, 2, 1))
    m = {
        "w_in": f(inp["w_in"]), "conv_wT": f(np.asarray(inp["dn_conv_w"]).transpose(0, 2, 1)),
        "a_log": f(inp["dn_a_log"]), "dt_bias": f(inp["dn_dt_bias"]), "norm_w": f(inp["dn_norm_w"]),
        "peT_k": f(np.asarray(inp["cmp_pe_k"]).transpose(0, 2, 1)), "w1_k": f(inp["cmp_w1_k"]), "w2_k": f(inp["cmp_w2_k"]),
        "peT_v": f(np.asarray(inp["cmp_pe_v"]).transpose(0, 2, 1)), "w1_v": f(inp["cmp_w1_v"]), "w2_v": f(inp["cmp_w2_v"]),
        "w_a": f(inp["w_branch_a"]), "w_b": f(inp["w_branch_b"]), "w_out": f(inp["w_out"]),
        "w_gate": f(inp["w_ffn_gate"]), "w_up": f(inp["w_ffn_up"]), "w_down": f(inp["w_ffn_down"]),
        "w_ple": f(inp["w_ple"]), "w_pg": f(inp["w_ple_gate"]),
        "ln1_g": lnl(inp["ln1_g"]), "ln1_b": lnl(inp["ln1_b"]), "ln2_g": lnl(inp["ln2_g"]), "ln2_b": lnl(inp["ln2_b"]),
    }
    m.update(all_consts())
    return m


def run(inp, seq_ids_per_core, trace=False):
    nseq = len(seq_ids_per_core[0])
    p = Prog(nseq=nseq)
    emit_all(p)
    shared = host_params(inp)
    x = np.asarray(inp["x"], dtype=np.float32)
    pp = np.asarray(inp["p"], dtype=np.float32)
    in_maps = []
    for ids in seq_ids_per_core:
        m = dict(shared)
        m["xin"] = np.ascontiguousarray(x[ids].transpose(0, 2, 1))
        m["pin"] = np.ascontiguousarray(pp[:, ids].transpose(0, 1, 3, 2))
        in_maps.append({n: v for n, v in m.items() if n in p.d})
    res = run_bass_kernel_spmd(p.nc, in_maps, core_ids=list(range(len(in_maps))), trace=trace)
    outs = [np.ascontiguousarray(r["out"].transpose(0, 2, 1)) for r in res.results]
    return outs, res


def kernel(**inputs):
    B = np.asarray(inputs["x"]).shape[0]
    ids = [[2 * c, 2 * c + 1] for c in range(8)]
    outs, _ = run(inputs, ids)
    out = np.empty((B, T, D), np.float32)
    for c, o in enumerate(outs):
        out[ids[c]] = o
    return out
```

```python
import math
from contextlib import ExitStack
import numpy as np
import ml_dtypes
import concourse.bass as bass
import concourse.mybir as mybir
from concourse.bass_utils import run_bass_kernel_spmd

F32 = mybir.dt.float32
BF16 = mybir.dt.bfloat16
F32R = mybir.dt.float32r
DN_R = False
I32 = mybir.dt.int32
U32 = mybir.dt.uint32
AF = mybir.ActivationFunctionType
ALU = mybir.AluOpType
AX = mybir.AxisListType

SAME_ENGINE_SYNC = True
NDSEM = 90
NDSEM_SW = 30

D = 2048
T = 2048
DEPTH = 2
NSEQ = 2
NIN = 10792
DFF = 5632
ALPHA = (2.0 * DEPTH) ** 0.25
LN_EPS = 1e-5
NORM_EPS = 1e-6
O_DQ, O_DK, O_DV, O_DZ, O_DB, O_DA = 0, 1024, 2048, 3072, 4096, 4104
O_NQ, O_KC, O_VC, O_KS, O_VS, O_KW, O_VW, O_GT, O_MA, O_MB = 4112, 5136, 5392, 5648, 5904, 6160, 6416, 6672, 6696, 8744


class Buf:
    __slots__ = ("name", "w", "r", "dsem", "dcount", "ps")

    def __init__(self, name, ps=False):
        self.name = name
        self.ps = ps
        self.w = None
        self.r = {}
        self.dsem = None
        self.dcount = 0


class V:
    __slots__ = ("b", "ap")

    def __init__(self, b, ap):
        self.b = b
        self.ap = ap

    def __getitem__(self, idx):
        return V(self.b, self.ap[idx])

    def bc(self, shape):
        return V(self.b, self.ap.to_broadcast(list(shape)))

    def re(self, pat, **kw):
        return V(self.b, self.ap.rearrange(pat, **kw))

    def bt(self, dt):
        return V(self.b, self.ap.bitcast(dt))

    def sub(self, name):
        return V(Buf(name), self.ap)


class Eng:
    def __init__(self, name, eng, sem):
        self.name = name
        self.eng = eng
        self.sem = sem
        self.n = 0
        self.known = {}


class K:
    def __init__(self, nc, stack, kinds=None):
        self.nc = nc
        self.stack = stack
        self.kinds = kinds or {}
        self.engs = {}
        for name, eng in (("pe", nc.tensor), ("dve", nc.vector), ("act", nc.scalar),
                          ("pool", nc.gpsimd), ("sp", nc.sync)):
            sem = stack.enter_context(nc.semaphore("sem_" + name))
            self.engs[name] = Eng(name, eng, sem)
        self.dbufs = []
        self.nins = 0
        self.uid = 0
        self.sempool = {"sw": [], "hw": []}
        for i in range(NDSEM):
            sem = stack.enter_context(nc.semaphore(f"dsem{i}"))
            self.sempool["sw" if i < NDSEM_SW else "hw"].append([f"ds{i}", sem, 0])

    def sb(self, name, shape, dt, stack=None):
        self.uid += 1
        t = (stack or self.stack).enter_context(self.nc.sbuf_tensor(f"{name}_{self.uid}", list(shape), dt))
        return V(Buf(name), t[:])

    def ps(self, name, shape, dt=F32, stack=None):
        self.uid += 1
        t = (stack or self.stack).enter_context(self.nc.psum_tensor(f"{name}_{self.uid}", list(shape), dt))
        return V(Buf(name, ps=True), t[:])

    def dram(self, name, shape, dt, kind="Internal"):
        kind = self.kinds.get(name, kind)
        t = self.nc.dram_tensor(name, list(shape), dt, kind=kind)
        return V(Buf(name), t.ap())

    def _wait(self, E, deps):
        need = {}
        for d in deps:
            if d is None:
                continue
            key, sem, val = d
            if key not in need or need[key][1] < val:
                need[key] = (sem, val)
        for key, (sem, val) in need.items():
            if E.known.get(key, 0) >= val:
                continue
            if key == E.name and (E.name == "pe" or not SAME_ENGINE_SYNC):
                continue
            E.eng.wait_ge(sem, val)
            E.known[key] = val

    def _deps(self, R, W, me=None):
        deps = []
        for v in R:
            deps.append(v.b.w)
            if v.b.ps:
                for key, (sem, val) in v.b.r.items():
                    if key != me:
                        deps.append((key, sem, val))
        for v in W:
            deps.append(v.b.w)
            for key, (sem, val) in v.b.r.items():
                deps.append((key, sem, val))
        return deps

    def _mark(self, R, W, tok):
        key, sem, val = tok
        for v in R:
            v.b.r[key] = (sem, val)
        for v in W:
            v.b.w = tok
            v.b.r = {}

    def op(self, e, fn, R=(), W=()):
        E = self.engs[e]
        R = [v for v in R if isinstance(v, V)]
        self._wait(E, self._deps(R, W, E.name))
        ins = fn(E.eng)
        E.n += 1
        ins.then_inc(E.sem, 1)
        self._mark(R, W, (E.name, E.sem, E.n))
        self.nins += 1
        return ins

    def dma(self, e, out, in_, slot=None, **kw):
        E = self.engs[e]
        self._wait(E, self._deps([in_], [out]))
        if slot is not None:
            sb = slot.b
        elif "DRam" in type(out.ap.tensor).__name__:
            sb = in_.b
        else:
            sb = out.b
        kind = "sw" if e == "pool" else "hw"
        if sb.dsem is None:
            sb.dsem = {}
        if kind not in sb.dsem:
            sb.dsem[kind] = self.sempool[kind].pop()
            self.dbufs.append((sb, kind))
        ent = sb.dsem[kind]
        ent[2] += 16
        E.eng.dma_start(out=out.ap, in_=in_.ap, **kw).then_inc(ent[1], 16)
        tok = (ent[0], ent[1], ent[2])
        self._mark([in_], [out], tok)
        self.nins += 1

    def barrier(self):
        toks = [(E.name, E.sem, E.n) for E in self.engs.values() if E.n > 0]
        toks += [tuple(b.dsem[kind]) for b, kind in self.dbufs]
        for E in self.engs.values():
            self._wait(E, toks)
        for b, kind in self.dbufs:
            self.sempool[kind].append(b.dsem.pop(kind))
            b.w = None
            b.r = {}
        self.dbufs = []

    def mm(self, out, lhsT, rhs, start=True, stop=True, **kw):
        return self.op("pe", lambda e: e.matmul(out.ap, lhsT.ap, rhs.ap, start=start, stop=stop, **kw),
                       R=[lhsT, rhs], W=[out])

    def tr(self, out, in_, ident):
        return self.op("pe", lambda e: e.transpose(out.ap, in_.ap, ident.ap), R=[in_, ident], W=[out])

    def act(self, out, in_, func, bias=None, scale=None, accum_out=None):
        kw = {}
        R = [in_]
        if bias is not None:
            if isinstance(bias, V):
                kw["bias"] = bias.ap
                R.append(bias)
            else:
                kw["bias"] = bias
        if scale is not None:
            if isinstance(scale, V):
                kw["scale"] = scale.ap
                R.append(scale)
            else:
                kw["scale"] = scale
        W = [out]
        if accum_out is not None:
            kw["accum_out"] = accum_out.ap
            W.append(accum_out)
        return self.op("act", lambda g: g.activation(out=out.ap, in_=in_.ap, func=func, **kw), R=R, W=W)

    def tt(self, out, a, b, op, e="dve"):
        return self.op(e, lambda g: g.tensor_tensor(out=out.ap, in0=a.ap, in1=b.ap, op=op), R=[a, b], W=[out])

    def ts(self, out, a, s1, op0, s2=None, op1=None, e="dve"):
        R = [a]
        x1 = s1.ap if isinstance(s1, V) else s1
        x2 = s2.ap if isinstance(s2, V) else s2
        if isinstance(s1, V):
            R.append(s1)
        if isinstance(s2, V):
            R.append(s2)
        if op1 is None:
            return self.op(e, lambda g: g.tensor_scalar(out=out.ap, in0=a.ap, scalar1=x1, scalar2=None, op0=op0),
                           R=R, W=[out])
        return self.op(e, lambda g: g.tensor_scalar(out=out.ap, in0=a.ap, scalar1=x1, scalar2=x2, op0=op0, op1=op1),
                       R=R, W=[out])

    def stt(self, out, a, s, b, op0, op1):
        R = [a, b]
        x = s.ap if isinstance(s, V) else s
        if isinstance(s, V):
            R.append(s)
        return self.op("dve", lambda g: g.scalar_tensor_tensor(out=out.ap, in0=a.ap, scalar=x, in1=b.ap,
                                                               op0=op0, op1=op1), R=R, W=[out])

    def copy(self, out, in_, e="dve"):
        if e == "act":
            return self.op("act", lambda g: g.copy(out=out.ap, in_=in_.ap), R=[in_], W=[out])
        return self.op(e, lambda g: g.tensor_copy(out=out.ap, in_=in_.ap), R=[in_], W=[out])

    def memset(self, out, val, e="dve"):
        return self.op(e, lambda g: g.memset(out.ap, val), R=[], W=[out])

    def recip(self, out, in_):
        return self.op("dve", lambda g: g.reciprocal(out=out.ap, in_=in_.ap), R=[in_], W=[out])


def rr(v):
    return v.bt(F32R) if DN_R else v


class Rot:
    def __init__(self, items):
        self.items = items
        self.i = 0

    def next(self):
        v = self.items[self.i % len(self.items)]
        self.i += 1
        return v


def run_window(gens, W):
    pending = list(gens)
    active = []
    while pending or active:
        while len(active) < W and pending:
            active.append(pending.pop(0))
        for g in list(active):
            try:
                next(g)
            except StopIteration:
                active.remove(g)


def make_consts():
    c = {}
    half = 64
    inv_freq = (10000.0 ** (-np.arange(half, dtype=np.float32) / half)).astype(np.float32)
    ang = np.arange(T, dtype=np.float32)[:, None] * inv_freq[None, :]
    cos = np.cos(ang).astype(np.float32).T
    sin = np.sin(ang).astype(np.float32).T
    c["c_cos"] = np.ascontiguousarray(np.concatenate([cos, cos], 0))
    c["c_sin"] = np.ascontiguousarray(np.concatenate([sin, sin], 0))
    c["c_ident"] = np.eye(128, dtype=np.float32)
    c["c_ones"] = np.ones((128, 128), np.float32)
    P = np.zeros((128, 128), np.float32)
    for m in range(64):
        P[m, m + 64] = -1.0
        P[m + 64, m] = 1.0
    c["c_ropeT"] = np.ascontiguousarray(P.T)
    return c


CONST_SHAPES = {"c_cos": (128, T), "c_sin": (128, T), "c_ident": (128, 128), "c_ones": (128, 128),
                "c_ropeT": (128, 128)}


class Prog:
    def __init__(self, nseq=NSEQ, layers=(0, 1), phases=None, kinds=None):
        self.nseq = nseq
        self.layers = layers
        self.phases = phases
        self.nc = bass.Bass("TRN2", target_bir_lowering=False)
        self.stack = ExitStack()
        self.k = K(self.nc, self.stack, kinds)
        self.declare()

    def declare(self):
        L = DEPTH
        ns = self.nseq
        EI = "ExternalInput"
        sp = {}
        sp["xin"] = ([ns, D, T], F32, EI)
        sp["pin"] = ([L, ns, 256, T], F32, EI)
        sp["w_in"] = ([L, D, NIN], F32, EI)
        sp["conv_wT"] = ([L, 3072, 4], F32, EI)
        sp["a_log"] = ([L, 8], F32, EI)
        sp["dt_bias"] = ([L, 8], F32, EI)
        sp["norm_w"] = ([L, 128], F32, EI)
        for n in ("k", "v"):
            sp["peT_" + n] = ([L, 128, 32], F32, EI)
            sp["w1_" + n] = ([L, 4096, 256], F32, EI)
            sp["w2_" + n] = ([L, 256, 128], F32, EI)
        sp["w_a"] = ([L, 1024, D], F32, EI)
        sp["w_b"] = ([L, 1024, D], F32, EI)
        sp["w_out"] = ([L, D, D], F32, EI)
        sp["w_gate"] = ([L, D, DFF], F32, EI)
        sp["w_up"] = ([L, D, DFF], F32, EI)
        sp["w_down"] = ([L, DFF, D], F32, EI)
        sp["w_ple"] = ([L, 256, D], F32, EI)
        sp["w_pg"] = ([L, D, D], F32, EI)
        for n in ("ln1_g", "ln1_b", "ln2_g", "ln2_b"):
            sp[n] = ([L, 128, 16], F32, EI)
        for n, s_ in CONST_SHAPES.items():
            sp[n] = (list(s_), F32, EI)
        sp["out"] = ([ns, D, T], F32, "ExternalOutput")
        IN = "Internal"
        sp["xres"] = ([D, T], F32, IN)
        sp["dnraw"] = ([3072, T], F32, IN)
        sp["nq"] = ([1024, T], BF16, IN)
        for n in ("kc", "vc", "ks", "kw"):
            sp[n] = ([256, T], BF16, IN)
        sp["mga"] = ([D, T], F32, IN)
        sp["mgb"] = ([D, T], F32, IN)
        sp["ztm"] = ([T, 1024], F32, IN)
        sp["batm"] = ([T, 16], F32, IN)
        sp["vstm"] = ([T, 256], BF16, IN)
        sp["vwtm"] = ([T, 256], BF16, IN)
        sp["gttm"] = ([T, 24], F32, IN)
        sp["dqfm"] = ([1024, T], F32, IN)
        sp["dkfm"] = ([1024, T], F32, IN)
        sp["dktm"] = ([T, 1024], F32, IN)
        sp["dvtm"] = ([T, 1024], F32, IN)
        sp["oaT"] = ([1024, T], BF16, IN)
        sp["obT"] = ([1024, T], BF16, IN)
        sp["pre"] = ([D, T], F32, IN)
        self.specs = sp
        prog = self

        class LazyD(dict):
            def __missing__(self, name):
                shape, dt, kind = prog.specs[name]
                v = prog.k.dram(name, shape, dt, kind)
                self[name] = v
                return v
        self.d = LazyD()

    def wload(self, dst, wv, col0, n, kc):
        self.k.dma("pool", dst[:, :kc, :n], wv.re("(kc p) n -> p kc n", p=128)[:, :, col0:col0 + n])

    def phase_inproj(self, l, s):
        k, d = self.k, self.d
        with ExitStack() as st:
            xT = k.sb("p1_xT", [128, 16, T], BF16, st)
            wb = Rot([k.sb(f"p1_w{i}", [128, 16, 512], BF16, st) for i in range(2)])
            cos = k.sb("p1_cos", [128, T], F32, st)
            sin = k.sb("p1_sin", [128, T], F32, st)
            ropeT = k.sb("p1_ropeT", [128, 128], F32, st)
            stg = Rot([k.sb(f"p1_stg{i}", [128, 512], F32, st) for i in range(4)])
            stb = Rot([k.sb(f"p1_stb{i}", [128, 512], BF16, st) for i in range(4)])
            t1 = Rot([k.sb(f"p1_t1{i}", [128, 512], F32, st) for i in range(2)])
            t2 = Rot([k.sb(f"p1_t2{i}", [128, 512], F32, st) for i in range(2)])
            small = Rot([k.sb(f"p1_sm{i}", [128, 64], F32, st) for i in range(4)])
            dtb = k.sb("p1_dtb", [128, 8], F32, st)
            negA = k.sb("p1_negA", [128, 8], F32, st)
            pss = Rot([k.ps(f"p1_ps{i}", [128, 512], F32, st) for i in range(6)])
            ps2 = Rot([k.ps(f"p1_psr{i}", [128, 512], F32, st) for i in range(2)])

            src = d["xin"][s] if l == 0 else d["xres"]
            for c4 in range(4):
                k.dma("pool", xT[:, c4 * 4:(c4 + 1) * 4, :],
                      src.re("(kc p) t -> p kc t", p=128)[:, c4 * 4:(c4 + 1) * 4, :])
            k.dma("sp", cos, d["c_cos"])
            k.dma("sp", sin, d["c_sin"])
            k.dma("sp", ropeT, d["c_ropeT"])
            k.dma("sp", dtb, V(d["dt_bias"].b, d["dt_bias"].ap[l].partition_broadcast(128)))
            k.dma("sp", negA, V(d["a_log"].b, d["a_log"].ap[l].partition_broadcast(128)))
            k.act(negA, negA, AF.Exp)
            k.ts(negA, negA, -1.0, ALU.mult)
            W = d["w_in"][l]

            fm = [(O_DQ, 3072, "raw", d["dnraw"], 0),
                  (O_NQ, 1024, "ropeq", d["nq"], 0),
                  (O_KC, 256, "rope", d["kc"], 0),
                  (O_VC, 256, "bf", d["vc"], 0),
                  (O_KS, 256, "rope", d["ks"], 0),
                  (O_KW, 256, "rope", d["kw"], 0),
                  (O_MA, 2048, "sig", d["mga"], 0),
                  (O_MB, 2048, "sig", d["mgb"], 0)]
            nev = 0
            for (c0, ncols, kind, dst, r0) in fm:
                for cb in range(0, ncols, 512):
                    nb = min(512, ncols - cb)
                    w = wb.next()
                    self.wload(w, W, c0 + cb, nb, 16)
                    for mc in range(nb // 128):
                        row = r0 + cb + mc * 128
                        for tt in range(4):
                            ps = pss.next()
                            tsl = slice(tt * 512, (tt + 1) * 512)
                            for kc in range(16):
                                k.mm(ps, w[:, kc, mc * 128:(mc + 1) * 128], xT[:, kc, tsl],
                                     start=(kc == 0), stop=(kc == 15))
                            nev += 1
                            if kind == "raw":
                                o = stg.next()
                                if nev % 2:
                                    k.act(o, ps, AF.Copy)
                                else:
                                    k.copy(o, ps)
                                k.dma("sp", dst[row:row + 128, tsl], o)
                            elif kind == "sig":
                                o = stg.next()
                                k.act(o, ps, AF.Sigmoid)
                                k.dma("sp", dst[row:row + 128, tsl], o)
                            elif kind == "bf":
                                o = stb.next()
                                k.copy(o, ps)
                                k.dma("sp", dst[row:row + 128, tsl], o)
                            else:
                                sc = (128.0 ** -0.5) if kind == "ropeq" else 1.0
                                xs = stg.next()
                                k.act(xs, ps, AF.Copy)
                                pr = ps2.next()
                                k.mm(pr, ropeT, xs)
                                a = t1.next()
                                b = t2.next()
                                k.stt(a, xs, sc, cos[:, tsl], ALU.mult, ALU.mult)
                                k.stt(b, pr, sc, sin[:, tsl], ALU.mult, ALU.mult)
                                o = stb.next()
                                k.tt(o, a, b, ALU.add)
                                k.dma("sp", dst[row:row + 128, tsl], o)

            tm = [(O_DZ, 512, "silu", d["ztm"], 0), (O_DZ + 512, 512, "silu", d["ztm"], 512),
                  (O_DB, 16, "ba", d["batm"], 0), (O_VS, 256, "bf", d["vstm"], 0),
                  (O_VW, 256, "bf", d["vwtm"], 0), (O_GT, 24, "sig", d["gttm"], 0)]
            for (c0, ncols, kind, dst, dc0) in tm:
                w = wb.next()
                self.wload(w, W, c0, ncols, 16)
                for t128 in range(16):
                    ps = pss.next()
                    rs = slice(t128 * 128, (t128 + 1) * 128)
                    for kc in range(16):
                        k.mm(ps[:, :ncols], xT[:, kc, rs], w[:, kc, :ncols], start=(kc == 0), stop=(kc == 15))
                    if kind == "silu":
                        o = stg.next()
                        k.act(o, ps, AF.Silu)
                        k.dma("sp", dst[rs, dc0:dc0 + ncols], o)
                    elif kind == "bf":
                        o = stb.next()
                        k.copy(o[:, :ncols], ps[:, :ncols])
                        k.dma("sp", dst[rs, dc0:dc0 + ncols], o[:, :ncols])
                    elif kind == "sig":
                        o = small.next()
                        k.act(o[:, :ncols], ps[:, :ncols], AF.Sigmoid)
                        k.dma("sp", dst[rs, dc0:dc0 + ncols], o[:, :ncols])
                    else:
                        o = small.next()
                        k.act(o[:, 0:8], ps[:, 0:8], AF.Sigmoid)
                        k.tt(o[:, 16:24], ps[:, 8:16], dtb, ALU.add)
                        k.act(o[:, 24:32], o[:, 16:24], AF.Exp)
                        k.act(o[:, 32:40], o[:, 24:32], AF.Ln, bias=1.0)
                        k.tt(o[:, 8:16], o[:, 32:40], negA, ALU.mult)
                        k.dma("sp", dst[rs, 0:16], o[:, 0:16])
            k.barrier()


def build(nseq=NSEQ, layers=(0, 1), phases=None, kinds=None):
    p = Prog(nseq, layers, phases, kinds)
    return p


def phase_outproj(self, l, s):
    k, d = self.k, self.d
    with ExitStack() as st:
        mixT = k.sb("p5_mixT", [128, 16, T], BF16, st)
        pss = Rot([k.ps(f"p5_ps{i}", [128, 512], F32, st) for i in range(6)])
        with ExitStack() as st2:
            oaT = k.sb("p5_oaT", [128, 8, T], BF16, st2)
            obT = k.sb("p5_obT", [128, 8, T], BF16, st2)
            wa = Rot([k.sb(f"p5_wa{i}", [128, 8, 512], BF16, st2) for i in range(2)])
            wb = Rot([k.sb(f"p5_wb{i}", [128, 8, 512], BF16, st2) for i in range(2)])
            ga = Rot([k.sb(f"p5_ga{i}", [128, 512], F32, st2) for i in range(2)])
            gb = Rot([k.sb(f"p5_gb{i}", [128, 512], F32, st2) for i in range(2)])
            m1 = Rot([k.sb(f"p5_m1{i}", [128, 512], F32, st2) for i in range(2)])
            m2 = Rot([k.sb(f"p5_m2{i}", [128, 512], F32, st2) for i in range(2)])
            k.dma("sp", oaT, d["oaT"].re("(c p) t -> p c t", p=128))
            k.dma("sp", obT, d["obT"].re("(c p) t -> p c t", p=128))
            for cb in range(4):
                a, b = wa.next(), wb.next()
                self.wload(a, d["w_a"][l], cb * 512, 512, 8)
                self.wload(b, d["w_b"][l], cb * 512, 512, 8)
                for mc in range(4):
                    ch = cb * 4 + mc
                    for tt in range(4):
                        tsl = slice(tt * 512, (tt + 1) * 512)
                        pa, pb = pss.next(), pss.next()
                        for kc in range(8):
                            k.mm(pa, a[:, kc, mc * 128:(mc + 1) * 128], oaT[:, kc, tsl], start=(kc == 0), stop=(kc == 7))
                        for kc in range(8):
                            k.mm(pb, b[:, kc, mc * 128:(mc + 1) * 128], obT[:, kc, tsl], start=(kc == 0), stop=(kc == 7))
                        g1, g2 = ga.next(), gb.next()
                        k.dma("sp", g1, d["mga"][ch * 128:(ch + 1) * 128, tsl])
                        k.dma("act", g2, d["mgb"][ch * 128:(ch + 1) * 128, tsl])
                        x1, x2 = m1.next(), m2.next()
                        k.tt(x1, pa, g1, ALU.mult)
                        k.tt(x2, pb, g2, ALU.mult)
                        k.tt(mixT[:, ch, tsl], x1, x2, ALU.add, e="pool")
            k.barrier()
        with ExitStack() as st2:
            wo = Rot([k.sb(f"p5_wo{i}", [128, 16, 512], BF16, st2) for i in range(2)])
            xr = Rot([k.sb(f"p5_xr{i}", [128, 512], F32, st2) for i in range(2)])
            og = Rot([k.sb(f"p5_og{i}", [128, 512], F32, st2) for i in range(2)])
            src = d["xin"][s] if l == 0 else d["xres"]
            for cb in range(4):
                w = wo.next()
                self.wload(w, d["w_out"][l], cb * 512, 512, 16)
                for mc in range(4):
                    ch = cb * 4 + mc
                    for tt in range(4):
                        tsl = slice(tt * 512, (tt + 1) * 512)
                        ps = pss.next()
                        for kc in range(16):
                            k.mm(ps, w[:, kc, mc * 128:(mc + 1) * 128], mixT[:, kc, tsl], start=(kc == 0), stop=(kc == 15))
                        x = xr.next()
                        k.dma("act", x, src[ch * 128:(ch + 1) * 128, tsl])
                        o = og.next()
                        k.stt(o, x, ALPHA, ps, ALU.mult, ALU.add)
                        k.dma("sp", d["pre"][ch * 128:(ch + 1) * 128, tsl], o)
            k.barrier()


def phase_ln(self, l, gname, bname, dst):
    k, d = self.k, self.d
    with ExitStack() as st:
        xt = Rot([k.sb(f"p6_x{i}", [128, 16, 512], F32, st) for i in range(3)])
        sqs = Rot([k.sb(f"p6_sq{i}", [128, 16, 512], F32, st) for i in range(2)])
        ones = k.sb("p6_ones", [128, 128], F32, st)
        g = k.sb("p6_g", [128, 16], F32, st)
        b = k.sb("p6_b", [128, 16], F32, st)
        means = Rot([k.sb(f"p6_mean{i}", [128, 512], F32, st) for i in range(2)])
        vars_ = Rot([k.sb(f"p6_var{i}", [128, 512], F32, st) for i in range(2)])
        rstds = Rot([k.sb(f"p6_rstd{i}", [128, 512], F32, st) for i in range(2)])
        ps1s = Rot([k.ps(f"p6_ps1{i}", [128, 512], F32, st) for i in range(2)])
        ps2s = Rot([k.ps(f"p6_ps2{i}", [128, 512], F32, st) for i in range(2)])
        k.dma("sp", ones, d["c_ones"])
        k.dma("sp", g, d[gname][l])
        k.dma("sp", b, d[bname][l])
        for tt in range(4):
            tsl = slice(tt * 512, (tt + 1) * 512)
            x, sq, mean, var, rstd, ps1, ps2 = (xt.next(), sqs.next(), means.next(), vars_.next(), rstds.next(),
                                                ps1s.next(), ps2s.next())
            k.dma("sp" if tt % 2 == 0 else "act", x, d["pre"].re("(c p) t -> p c t", p=128)[:, :, tsl])
            k.act(sq, x, AF.Square)
            for c in range(16):
                k.mm(ps1, rr(ones), rr(x[:, c, :]), start=(c == 0), stop=(c == 15))
            for c in range(16):
                k.mm(ps2, rr(ones), rr(sq[:, c, :]), start=(c == 0), stop=(c == 15))
            k.ts(mean, ps1, 1.0 / D, ALU.mult)
            k.tt(var, mean, mean, ALU.mult)
            k.stt(var, ps2, 1.0 / D, var, ALU.mult, ALU.subtract)
            k.ts(var, var, LN_EPS, ALU.add)
            k.act(var, var, AF.Sqrt)
            k.recip(rstd, var)
            for c in range(16):
                e = "pool" if c % 3 == 2 else "dve"
                k.tt(x[:, c, :], x[:, c, :], mean, ALU.subtract, e=e)
                k.tt(x[:, c, :], x[:, c, :], rstd, ALU.mult, e=e)
                k.act(x[:, c, :], x[:, c, :], AF.Identity, scale=g[:, c:c + 1], bias=b[:, c:c + 1])
            k.dma("sp", dst.re("(c p) t -> p c t", p=128)[:, :, tsl], x)
        k.barrier()


def phase_ffn(self, l, s):
    k, d = self.k, self.d
    for half in range(2):
        t0 = half * 1024
        with ExitStack() as st:
            xT = k.sb("p7_xT", [128, 16, 1024], BF16, st)
            pT = k.sb("p7_pT", [128, 2, 1024], BF16, st)
            hT = k.sb("p7_hT", [128, 44, 1024], BF16, st)
            pss = Rot([k.ps(f"p7_ps{i}", [128, 512], F32, st) for i in range(8)])
            k.dma("pool", xT, d["xres"].re("(c p) t -> p c t", p=128)[:, :, t0:t0 + 1024])
            k.dma("pool", pT, d["pin"][l, s].re("(c p) t -> p c t", p=128)[:, :, t0:t0 + 1024])
            with ExitStack() as st2:
                wg = Rot([k.sb(f"p7_wg{i}", [128, 16, 256], BF16, st2) for i in range(2)])
                wu = Rot([k.sb(f"p7_wu{i}", [128, 16, 256], BF16, st2) for i in range(2)])
                sg = Rot([k.sb(f"p7_sg{i}", [128, 512], F32, st2) for i in range(3)])
                for cb in range(22):
                    g_, u_ = wg.next(), wu.next()
                    self.wload(g_, d["w_gate"][l], cb * 256, 256, 16)
                    self.wload(u_, d["w_up"][l], cb * 256, 256, 16)
                    for mc in range(2):
                        j = cb * 2 + mc
                        for tt in range(2):
                            tsl = slice(tt * 512, (tt + 1) * 512)
                            pg, pu = pss.next(), pss.next()
                            for kc in range(16):
                                k.mm(pg, g_[:, kc, mc * 128:(mc + 1) * 128], xT[:, kc, tsl], start=(kc == 0), stop=(kc == 15))
                            for kc in range(16):
                                k.mm(pu, u_[:, kc, mc * 128:(mc + 1) * 128], xT[:, kc, tsl], start=(kc == 0), stop=(kc == 15))
                            s_ = sg.next()
                            k.act(s_, pg, AF.Silu)
                            k.tt(hT[:, j, tsl], s_, pu, ALU.mult)
                k.barrier()
            with ExitStack() as st2:
                wd = Rot([k.sb(f"p7_wd{i}", [128, 44, 128], BF16, st2) for i in range(2)])
                wpg = Rot([k.sb(f"p7_wpg{i}", [128, 16, 128], BF16, st2) for i in range(2)])
                wpl = Rot([k.sb(f"p7_wpl{i}", [128, 2, 128], BF16, st2) for i in range(2)])
                sgm = Rot([k.sb(f"p7_sgm{i}", [128, 512], F32, st2) for i in range(2)])
                ple = Rot([k.sb(f"p7_ple{i}", [128, 512], F32, st2) for i in range(2)])
                xr = Rot([k.sb(f"p7_xr{i}", [128, 512], F32, st2) for i in range(2)])
                acc = Rot([k.sb(f"p7_acc{i}", [128, 512], F32, st2) for i in range(2)])
                for m in range(16):
                    w1, w2, w3 = wd.next(), wpg.next(), wpl.next()
                    self.wload(w1, d["w_down"][l], m * 128, 128, 44)
                    self.wload(w2, d["w_pg"][l], m * 128, 128, 16)
                    self.wload(w3, d["w_ple"][l], m * 128, 128, 2)
                    for tt in range(2):
                        tsl = slice(tt * 512, (tt + 1) * 512)
                        gsl = slice(t0 + tt * 512, t0 + (tt + 1) * 512)
                        pf, pgt, pp = pss.next(), pss.next(), pss.next()
                        for j in range(44):
                            k.mm(pf, w1[:, j, :], hT[:, j, tsl], start=(j == 0), stop=(j == 43))
                        for kc in range(16):
                            k.mm(pgt, w2[:, kc, :], xT[:, kc, tsl], start=(kc == 0), stop=(kc == 15))
                        for c in range(2):
                            k.mm(pp, w3[:, c, :], pT[:, c, tsl], start=(c == 0), stop=(c == 1))
                        s_ = sgm.next()
                        k.act(s_, pgt, AF.Sigmoid)
                        pl = ple.next()
                        k.tt(pl, s_, pp, ALU.mult)
                        x = xr.next()
                        k.dma("act", x, d["xres"][m * 128:(m + 1) * 128, gsl])
                        a = acc.next()
                        k.stt(a, x, ALPHA, pf, ALU.mult, ALU.add)
                        k.tt(a, a, pl, ALU.add, e="pool")
                        k.dma("sp", d["pre"][m * 128:(m + 1) * 128, gsl], a)
                k.barrier()


Prog.phase_outproj = phase_outproj
Prog.phase_ln = phase_ln
Prog.phase_ffn = phase_ffn


NEG = -30000.0


def nsa_consts():
    c = {}
    n = np.arange(127)
    t = np.arange(T)
    c["c_cmpmask"] = np.where((16 * n[:, None] + 31) <= t[None, :], 0.0, NEG).astype(np.float32)
    i = np.arange(T)
    c["c_eexp"] = ((i[None, :] // 64) == np.arange(32)[:, None]).astype(np.float32)
    a = np.arange(128)
    c["c_causb"] = np.where(a[:, None] <= a[None, :], 0.0, NEG).astype(np.float32)
    c["c_winb"] = np.where(a[:, None] > a[None, :], 0.0, NEG).astype(np.float32)
    j = np.arange(32)
    c["c_overlap"] = ((16 * n[:, None] <= 64 * j[None, :] + 63) & (16 * n[:, None] + 31 >= 64 * j[None, :])
                      ).astype(np.float32)
    cur = t // 64
    forced = (j[None, :] == 0) | (j[None, :] == cur[:, None]) | (j[None, :] == cur[:, None] - 1)
    causal = j[None, :] <= cur[:, None]
    c["c_tk_m"] = (causal & ~forced).astype(np.float32)
    c["c_tk_a"] = np.where(forced, 100.0, np.where(causal, 0.0, -100.0)).astype(np.float32)
    return c


CONST_SHAPES.update({"c_cmpmask": (127, T), "c_eexp": (32, T), "c_causb": (128, 128), "c_winb": (128, 128),
                     "c_overlap": (127, 32), "c_tk_m": (T, 32), "c_tk_a": (T, 32)})


def phase_nsa(self, l, s):
    k, d = self.k, self.d
    for g in range(2):
        with ExitStack() as st:
            gsl = slice(g * 128, (g + 1) * 128)
            identb = k.sb("n_identb", [128, 128], BF16, st)
            cmpmask = k.sb("n_cmpmask", [128, T], BF16, st)
            eexp = k.sb("n_eexp", [32, T], BF16, st)
            causb = k.sb("n_causb", [128, 128], BF16, st)
            winb = k.sb("n_winb", [128, 128], BF16, st)
            tkm = k.sb("n_tkm", [128, 16, 32], F32, st)
            tka = k.sb("n_tka", [128, 16, 32], F32, st)
            k.dma("pool", identb, d["c_ident"])
            k.dma("pool", cmpmask[:127, :], d["c_cmpmask"])
            k.dma("pool", eexp, d["c_eexp"])
            k.dma("pool", causb, d["c_causb"])
            k.dma("pool", winb, d["c_winb"])
            k.dma("sp", tkm, d["c_tk_m"].re("(q p) j -> p q j", p=128))
            k.dma("sp", tka, d["c_tk_a"].re("(q p) j -> p q j", p=128))
            q4 = k.sb("n_q4", [128, 4, T], BF16, st)
            ksT = k.sb("n_ksT", [128, T], BF16, st)
            kwT = k.sb("n_kwT", [128, T], BF16, st)
            Vs = k.sb("n_Vs", [128, 16, 132], BF16, st)
            Vw = k.sb("n_Vw", [128, 16, 132], BF16, st)
            gt = k.sb("n_gt", [128, 16, 24], F32, st)
            k.dma("sp", q4, d["nq"].re("(h p) t -> p h t", p=128)[:, g * 4:(g + 1) * 4, :])
            k.dma("sp", ksT, d["ks"][gsl, :])
            k.dma("sp", kwT, d["kw"][gsl, :])
            k.dma("act", Vs[:, :, 0:128], d["vstm"].re("(kb p) c -> p kb c", p=128)[:, :, gsl])
            k.dma("act", Vw[:, :, 0:128], d["vwtm"].re("(kb p) c -> p kb c", p=128)[:, :, gsl])
            k.memset(Vs[:, :, 128:129], 1.0)
            k.memset(Vw[:, :, 128:129], 1.0)
            k.dma("sp", gt, d["gttm"].re("(q p) c -> p q c", p=128))
            import os as _os
            stop = int(_os.environ.get("NSA_STOP", "99"))
            if stop == 1:
                k.barrier()
                return
            kcmpT = k.sb("n_kcmpT", [128, 128], BF16, st)
            rcmp = k.sb("n_rcmp", [128, 161], BF16, st)
            k.memset(rcmp[:, 128:129], 1.0)
            k.dma("pool", rcmp[:127, 129:161], d["c_overlap"])
            ST = Rot([k.ps(f"n_st{i}", [128, 512], F32, st) for i in range(3)])
            PV = Rot([k.ps(f"n_pv{i}", [128, 2, 512], F32, st) for i in range(2)])
            psT = k.ps("n_psT", [128, 1024], BF16, st)

            with ExitStack() as st2:
                xT = k.sb("n_cx", [128, T], BF16, st2)
                w1 = k.sb("n_w1", [128, 32, 256], BF16, st2)
                w2 = k.sb("n_w2", [128, 2, 128], BF16, st2)
                peT = k.sb("n_peT", [128, 32], BF16, st2)
                hT = k.sb("n_hT", [128, 2, 128], BF16, st2)
                b1 = k.sb("n_b1", [128, 2], F32, st2)
                for nm in ("k", "v"):
                    k.dma("sp", xT, d["kc" if nm == "k" else "vc"][gsl, :])
                    k.dma("pool", w1, d["w1_" + nm][l].re("(l d) h -> d l h", d=128))
                    k.dma("pool", w2, d["w2_" + nm][l].re("(c p) d -> p c d", p=128))
                    k.dma("pool", peT, d["peT_" + nm][l])
                    pb = ST.next()
                    for mh in range(2):
                        for li in range(32):
                            k.mm(pb[:, mh:mh + 1], w1[:, li, mh * 128:(mh + 1) * 128], peT[:, li:li + 1],
                                 start=(li == 0), stop=(li == 31))
                    k.copy(b1, pb[:, 0:2])
                    for mh in range(2):
                        ph = ST.next()
                        for li in range(32):
                            k.mm(ph[:, :127], w1[:, li, mh * 128:(mh + 1) * 128], xT[:, li:li + 16 * 126 + 1:16],
                                 start=(li == 0), stop=(li == 31))
                        k.act(hT[:, mh, :127], ph[:, :127], AF.Silu, bias=b1[:, mh:mh + 1])
                    po = ST.next()
                    if nm == "k":
                        for mh in range(2):
                            k.mm(po[:, :127], w2[:, mh, :], hT[:, mh, :127], start=(mh == 0), stop=(mh == 1))
                        k.copy(kcmpT[:, :127], po[:, :127])
                    else:
                        for mh in range(2):
                            k.mm(po[:127, :128], hT[:, mh, :127], w2[:, mh, :], start=(mh == 0), stop=(mh == 1))
                        k.copy(rcmp[:127, 0:128], po[:127, :128])

            if stop == 2:
                k.barrier()
                return
            class BS:
                pass
            bufsets = []
            for i in range(2):
                B = BS()
                B.Ec = k.sb(f"n_Ec{i}", [128, 512], BF16, st)
                B.Es = k.sb(f"n_Es{i}", [128, 16, 512], BF16, st)
                B.Ew = k.sb(f"n_Ew{i}", [128, 5, 512], BF16, st)
                B.rden = k.sb(f"n_rden{i}", [128, 4], F32, st)
                B.cf = k.sb(f"n_cf{i}", [128, 4], F32, st)
                B.imp = k.sb(f"n_imp{i}", [128, 32], F32, st)
                B.vv = k.sb(f"n_vv{i}", [128, 32], F32, st)
                B.v2 = k.sb(f"n_v2{i}", [128, 32], F32, st)
                B.m8 = k.sb(f"n_m8{i}", [128, 16], F32, st)
                B.nsel = k.sb(f"n_nsel{i}", [128, 32], BF16, st)
                B.nsT = k.sb(f"n_nsT{i}", [32, 128], BF16, st)
                B.acc = k.sb(f"n_acc{i}", [128, 4, 128], F32, st)
                B.accb = k.sb(f"n_accb{i}", [128, 4, 128], BF16, st)
                B.ob = k.sb(f"n_obs{i}", [128, 4, 128], BF16, st)
                bufsets.append(B)

            def finish(B, qb, pv, branch, first):
                pv4 = pv.re("p b (x c) -> p (b x) c", c=256)
                k.ts(B.rden.re("p (h o) -> p h o", o=1), pv4[:, :, 128:129], 1e-30, ALU.max)
                k.recip(B.rden, B.rden)
                k.tt(B.cf.re("p (h o) -> p h o", o=1), B.rden.re("p (h o) -> p h o", o=1),
                     gt[:, qb, g * 12 + branch:g * 12 + 12:3].re("p (h o) -> p h o", o=1), ALU.mult)
                for h in range(4):
                    if first:
                        k.act(B.acc[:, h, :], pv4[:, h, 0:128], AF.Copy, scale=B.cf[:, h:h + 1])
                    else:
                        k.stt(B.acc[:, h, :], pv4[:, h, 0:128], B.cf[:, h:h + 1], B.acc[:, h, :], ALU.mult, ALU.add)
                return pv4

            def qbgen(qb):
                B = bufsets[qb % 2]
                qsl = slice(qb * 128, (qb + 1) * 128)
                qr = q4[:, :, qsl]
                ps = ST.next()
                k.mm(ps[:127, :], kcmpT[:, :127], qr, start=True, stop=False)
                k.mm(ps[:127, :], identb[:127, :127], cmpmask[:127, qsl].re("p (o t) -> p o t", o=1).bc([127, 4, 128]),
                     start=False, stop=True)
                k.act(B.Ec[:127, :], ps[:127, :], AF.Exp)
                yield
                pv = PV.next()
                pv4 = pv.re("p b (x c) -> p (b x) c", c=256)
                for h in range(4):
                    k.mm(pv4[:, h, 0:161], B.Ec[:127, h * 128:(h + 1) * 128], rcmp[:127, :], start=True, stop=True)
                yield
                finish(B, qb, pv, 0, True)
                sel_on = qb >= 8
                if sel_on:
                    k.ts(B.imp, pv4[:, 0, 129:161], B.rden[:, 0:1], ALU.mult)
                    for h in range(1, 4):
                        k.stt(B.imp, pv4[:, h, 129:161], B.rden[:, h:h + 1], B.imp, ALU.mult, ALU.add)
                    k.tt(B.vv, B.imp, tkm[:, qb, :], ALU.mult)
                    k.tt(B.vv, B.vv, tka[:, qb, :], ALU.add)
                    k.op("dve", lambda e: e.max(out=B.m8[:, 0:8].ap, in_=B.vv.ap), R=[B.vv], W=[B.m8])
                    k.op("dve", lambda e: e.match_replace(out=B.v2.ap, in_to_replace=B.m8[:, 0:8].ap, in_values=B.vv.ap,
                                                          imm_value=-1e9), R=[B.vv, B.m8], W=[B.v2])
                    k.op("dve", lambda e: e.max(out=B.m8[:, 8:16].ap, in_=B.v2.ap), R=[B.v2], W=[B.m8])
                    k.ts(B.v2, B.vv, B.m8[:, 15:16], ALU.is_ge)
                    k.ts(B.nsel, B.v2, 1.0, ALU.subtract, -NEG, ALU.mult)
                    yield
                    k.tr(psT[:32, 0:128], B.nsel, identb)
                    k.copy(B.nsT, psT[:32, 0:128])
                yield
                for kb in range(qb + 1):
                    ps = ST.next()
                    ksl = slice(kb * 128, (kb + 1) * 128)
                    last = (not sel_on) and kb != qb
                    k.mm(ps, ksT[:, ksl], qr, start=True, stop=last)
                    if sel_on:
                        k.mm(ps, eexp[:, ksl], B.nsT.re("p (o t) -> p o t", o=1).bc([32, 4, 128]),
                             start=False, stop=(kb != qb))
                    if kb == qb:
                        k.mm(ps, identb, causb.re("p (o t) -> p o t", o=1).bc([128, 4, 128]), start=False, stop=True)
                    k.act(B.Es[:, kb, :], ps, AF.Exp)
                    if kb % 2:
                        yield
                yield
                pv = PV.next()
                pv4 = pv.re("p b (x c) -> p (b x) c", c=256)
                for h in range(4):
                    for kb in range(qb + 1):
                        k.mm(pv4[:, h, 0:129], B.Es[:, kb, h * 128:(h + 1) * 128], Vs[:, kb, 0:129],
                             start=(kb == 0), stop=(kb == qb))
                    yield
                finish(B, qb, pv, 1, False)
                yield
                kb0 = max(0, qb - 4)
                for kb in range(kb0, qb + 1):
                    ps = ST.next()
                    ksl = slice(kb * 128, (kb + 1) * 128)
                    mk = causb if kb == qb else (winb if kb == qb - 4 else None)
                    k.mm(ps, kwT[:, ksl], qr, start=True, stop=(mk is None))
                    if mk is not None:
                        k.mm(ps, identb, mk.re("p (o t) -> p o t", o=1).bc([128, 4, 128]), start=False, stop=True)
                    k.act(B.Ew[:, kb - kb0, :], ps, AF.Exp)
                    if kb % 2:
                        yield
                yield
                pv = PV.next()
                pv4 = pv.re("p b (x c) -> p (b x) c", c=256)
                for h in range(4):
                    for kb in range(kb0, qb + 1):
                        k.mm(pv4[:, h, 0:129], B.Ew[:, kb - kb0, h * 128:(h + 1) * 128], Vw[:, kb, 0:129],
                             start=(kb == kb0), stop=(kb == qb))
                yield
                finish(B, qb, pv, 2, False)
                k.copy(B.accb, B.acc)
                yield
                for h in range(4):
                    k.tr(psT[:, h * 128:(h + 1) * 128], B.accb[:, h, :], identb)
                k.copy(B.ob, psT[:, 0:512].re("p (h t) -> p h t", h=4))
                k.dma("sp", d["obT"].re("(h p) t -> p h t", p=128)[:, g * 4:(g + 1) * 4, qsl], B.ob)

            run_window([qbgen(qb) for qb in range(16)], 2)
            k.barrier()


Prog.phase_nsa = phase_nsa


def dn_consts():
    c = {}
    a = np.arange(128)
    c["c_tri_le"] = (a[:, None] <= a[None, :]).astype(np.float32)
    c["c_tri_gt"] = (a[:, None] > a[None, :]).astype(np.float32)
    return c


CONST_SHAPES.update({"c_tri_le": (128, 128), "c_tri_gt": (128, 128)})


def phase_dnprep(self, l, s):
    k, d = self.k, self.d
    with ExitStack() as st:
        NB = 3
        cw = k.sb("d2_cw", [128, 24, 4], F32, st)
        ones = k.sb("d2_ones", [128, 128], F32, st)
        ident = k.sb("d2_ident", [128, 128], F32, st)
        up = Rot([k.sb(f"d2_up{i}", [128, T + 3], F32, st) for i in range(NB)])
        ys = Rot([k.sb(f"d2_y{i}", [128, T], F32, st) for i in range(NB)])
        sqs = Rot([k.sb(f"d2_sq{i}", [128, T], F32, st) for i in range(NB)])
        rs = Rot([k.sb(f"d2_rs{i}", [128, 512], F32, st) for i in range(4)])
        tms = Rot([k.sb(f"d2_tm{i}", [128, 4, 128], F32, st) for i in range(4)])
        pss = Rot([k.ps(f"d2_ps{i}", [128, 512], F32, st) for i in range(6)])
        k.dma("sp", cw, d["conv_wT"][l].re("(c p) j -> p c j", p=128))
        k.dma("sp", ones, d["c_ones"])
        k.dma("sp", ident, d["c_ident"])
        for u in up.items:
            k.memset(u[:, 0:3], 0.0)

        def chunk(which, h):
            c = which * 8 + h
            u = up.next()
            y = ys.next()
            sq = sqs.next()
            k.dma("sp" if c % 2 else "act", u[:, 3:3 + T], d["dnraw"][c * 128:(c + 1) * 128, :])
            yield
            k.ts(y, u[:, 3:3 + T], cw[:, c, 3:4], ALU.mult)
            for j in range(3):
                k.stt(y, u[:, j:j + T], cw[:, c, j:j + 1], y, ALU.mult, ALU.add)
            yield
            k.act(y, y, AF.Silu)
            if which < 2:
                k.act(sq, y, AF.Square)
                yield
                for tt in range(4):
                    tsl = slice(tt * 512, (tt + 1) * 512)
                    ps = pss.next()
                    k.mm(ps, ones, sq[:, tsl])
                    r = rs.next()
                    k.ts(r, ps, NORM_EPS, ALU.add)
                    k.act(r, r, AF.Sqrt)
                    k.recip(r, r)
                    if which == 0:
                        k.stt(y[:, tsl], y[:, tsl], 128.0 ** -0.5, r, ALU.mult, ALU.mult)
                    else:
                        k.tt(y[:, tsl], y[:, tsl], r, ALU.mult)
                    if tt % 2:
                        yield
                k.dma("sp", d["dqfm" if which == 0 else "dkfm"][h * 128:(h + 1) * 128, :], y)
            yield
            if which >= 1:
                dst = d["dktm" if which == 1 else "dvtm"]
                for t4 in range(4):
                    ps = pss.next()
                    for i in range(4):
                        tb = t4 * 4 + i
                        k.tr(ps[:, i * 128:(i + 1) * 128], y[:, tb * 128:(tb + 1) * 128], ident)
                    o = tms.next()
                    k.copy(o, ps.re("p (i c) -> p i c", i=4), e="act" if t4 % 2 else "dve")
                    k.dma("sp", dst.re("(tb p) c -> p tb c", p=128)[:, t4 * 4:(t4 + 1) * 4, h * 128:(h + 1) * 128], o)
                    if t4 % 2:
                        yield

        run_window([chunk(which, h) for which in range(3) for h in range(8)], NB)
        k.barrier()


def phase_dnscan(self, l, s):
    k, d = self.k, self.d
    with ExitStack() as st:
        SH = [128, 8, 128]
        GH = [128, 4, 128]
        ident = k.sb("d3_ident", [128, 128], F32, st)
        identb = k.sb("d3_identb", [128, 128], BF16, st)
        ones = k.sb("d3_ones", [128, 128], F32, st)
        trile = k.sb("d3_trile", [128, 128], F32, st)
        trigt = k.sb("d3_trigt", [128, 128], F32, st)
        nw = k.sb("d3_nw", [128, 128], F32, st)
        k.dma("sp", ident, d["c_ident"])
        k.dma("pool", identb, d["c_ident"])
        k.dma("sp", ones, d["c_ones"])
        k.dma("sp", trile, d["c_tri_le"])
        k.dma("sp", trigt, d["c_tri_gt"])
        k.dma("sp", nw, V(d["norm_w"].b, d["norm_w"].ap[l].partition_broadcast(128)))
        qT = Rot([k.sb(f"d3_qT{i}", SH, F32, st) for i in range(2)])
        kT = Rot([k.sb(f"d3_kT{i}", SH, F32, st) for i in range(2)])
        ktm = Rot([k.sb(f"d3_ktm{i}", SH, F32, st) for i in range(2)])
        vtm = Rot([k.sb(f"d3_vtm{i}", SH, F32, st) for i in range(2)])
        zt = Rot([k.sb(f"d3_z{i}", SH, F32, st) for i in range(2)])
        ba = Rot([k.sb(f"d3_ba{i}", [128, 16], F32, st) for i in range(2)])
        gbc = k.sb("d3_gbc", SH, F32, st)
        sm = k.sb("d3_sm", [128, 48], F32, st)
        Gs, eG, eGr, eGl, bg = (sm[:, i * 8:(i + 1) * 8] for i in range(5))
        pm = k.ps("d3_pm", [128, 512], F32, st)
        pT = k.ps("d3_pT", [128, 1024], BF16, st)

        class G:
            pass
        grp = []
        for hg in range(2):
            o = G()
            for nm in ("S", "S1", "E", "Dm", "DT", "RT", "qkT", "vb", "kbg", "kd", "uu", "wT", "vnew", "o1", "oo"):
                setattr(o, nm, k.sb(f"d3_{nm}{hg}", GH, F32, st))
            o.ob = k.sb(f"d3_ob{hg}", GH, BF16, st)
            o.X = Rot([k.sb(f"d3_X{hg}{i}", GH, F32, st) for i in range(2)])
            o.Y = Rot([k.sb(f"d3_Y{hg}{i}", GH, F32, st) for i in range(2)])
            o.oT = Rot([k.sb(f"d3_oT{hg}{i}", GH, BF16, st) for i in range(2)])
            o.rn = k.sb(f"d3_rn{hg}", [128, 4], F32, st)
            o.P = Rot([k.ps(f"d3_P{hg}{i}", GH, F32, st) for i in range(3)])
            k.memset(o.S, 0.0)
            grp.append(o)

        def bc4(v):
            return v.re("p (h o) -> p h o", o=1).bc(GH)

        def bc8(v):
            return v.re("p (h o) -> p h o", o=1).bc(SH)

        def bcm(v):
            return v.re("p (o c) -> p o c", o=1).bc(GH)

        def mm4(ps, lhs, rhs):
            for h in range(4):
                k.mm(ps[:, h, :], lhs[:, h, :], rhs[:, h, :])

        def chain(o, hg, q_, k_, kt_, vt_, z_, beta, csl):
            hs = slice(hg * 4, hg * 4 + 4)
            q4, k4, kt4, vt4, z4 = q_[:, hs, :], k_[:, hs, :], kt_[:, hs, :], vt_[:, hs, :], z_[:, hs, :]
            P = o.P
            pg = P.next()
            for h in range(4):
                k.mm(pg[:, h, :], gbc[:, hg * 4 + h, :], trile)
            yield
            for h in range(4):
                k.ts(o.E[:, h, :], pg[:, h, :], Gs[:, hg * 4 + h:hg * 4 + h + 1], ALU.subtract)
            k.ts(o.DT, o.E, 0.0, ALU.min)
            k.ts(o.Dm, o.E, -1.0, ALU.mult, 0.0, ALU.min)
            k.act(o.DT, o.DT, AF.Exp)
            k.act(o.Dm, o.Dm, AF.Exp)
            k.tt(o.DT, o.DT, bcm(trile), ALU.mult, e="pool")
            k.tt(o.Dm, o.Dm, bcm(trigt), ALU.mult, e="pool")
            pk = P.next()
            mm4(pk, k4, k4)
            yield
            X0 = o.X.next()
            for h in range(4):
                k.stt(X0[:, h, :], pk[:, h, :], beta[:, hg * 4 + h:hg * 4 + h + 1], o.Dm[:, h, :], ALU.mult, ALU.mult)
            pq = P.next()
            mm4(pq, k4, q4)
            py = P.next()
            for h in range(4):
                k.tr(py[:, h, :], X0[:, h, :], ident)
            yield
            k.tt(o.qkT, pq, o.DT, ALU.mult)
            Y0 = o.Y.next()
            k.copy(Y0, py, e="act")
            k.tt(o.RT, bcm(ident), Y0, ALU.subtract)
            Xp, Yp = X0, Y0
            for i in range(1, 7):
                px = P.next()
                mm4(px, Yp, Xp)
                if i < 6:
                    pyy = P.next()
                    mm4(pyy, Xp, Yp)
                yield
                Xn = o.X.next()
                k.copy(Xn, px, e="act")
                if i < 6:
                    Yn = o.Y.next()
                    k.copy(Yn, pyy)
                pr = P.next()
                mm4(pr, Xn, o.RT)
                yield
                k.tt(o.RT, o.RT, pr, ALU.add)
                Xp = Xn
                if i < 6:
                    Yp = Yn
            k.tt(o.vb, vt4, bc4(beta[:, hs]), ALU.mult, e="pool")
            k.tt(o.kbg, kt4, bc4(bg[:, hs]), ALU.mult, e="pool")
            k.tt(o.kd, kt4, bc4(eGr[:, hs]), ALU.mult, e="pool")
            pu = P.next()
            mm4(pu, o.RT, o.vb)
            pw = P.next()
            mm4(pw, o.kbg, o.RT)
            yield
            k.copy(o.uu, pu, e="act")
            k.copy(o.wT, pw)
            pv = P.next()
            mm4(pv, o.wT, o.S)
            po1 = P.next()
            mm4(po1, q4, o.S)
            yield
            k.tt(o.vnew, o.uu, pv, ALU.subtract)
            k.tt(o.o1, po1, bc4(eG[:, hs]), ALU.mult)
            po2 = P.next()
            mm4(po2, o.qkT, o.vnew)
            pS = P.next()
            mm4(pS, o.kd, o.vnew)
            k.tt(o.S1, o.S, bc4(eGl[:, hs]), ALU.mult, e="pool")
            yield
            k.tt(o.oo, o.o1, po2, ALU.add)
            k.tt(o.S, o.S1, pS, ALU.add)
            k.tt(o.o1, o.oo, o.oo, ALU.mult, e="pool")
            k.op("dve", lambda e: e.tensor_reduce(out=o.rn.ap, in_=o.o1.ap, axis=AX.X, op=ALU.add), R=[o.o1], W=[o.rn])
            k.ts(o.rn, o.rn, 1.0 / 128, ALU.mult, NORM_EPS, ALU.add)
            k.act(o.rn, o.rn, AF.Sqrt)
            k.recip(o.rn, o.rn)
            yield
            k.tt(o.oo, o.oo, bc4(o.rn), ALU.mult)
            k.tt(o.oo, o.oo, bcm(nw), ALU.mult, e="pool")
            k.tt(o.ob, o.oo, z4, ALU.mult)
            yield
            for h in range(4):
                k.tr(pT[:, (hg * 4 + h) * 128:(hg * 4 + h + 1) * 128], o.ob[:, h, :], identb)
            ot = o.oT.next()
            k.copy(ot, pT[:, hg * 512:(hg + 1) * 512].re("p (h t) -> p h t", h=4), e="act")
            k.dma("sp", d["oaT"].re("(h p) t -> p h t", p=128)[:, hs, csl], ot)

        for ci in range(16):
            csl = slice(ci * 128, (ci + 1) * 128)
            q_, k_, kt_, vt_, z_, ba_ = qT.next(), kT.next(), ktm.next(), vtm.next(), zt.next(), ba.next()
            k.dma("sp", q_, d["dqfm"].re("(h p) t -> p h t", p=128)[:, :, csl])
            k.dma("act", k_, d["dkfm"].re("(h p) t -> p h t", p=128)[:, :, csl])
            k.dma("sp", kt_, d["dktm"][csl, :].re("p (h c) -> p h c", h=8))
            k.dma("act", vt_, d["dvtm"][csl, :].re("p (h c) -> p h c", h=8))
            k.dma("sp", z_, d["ztm"][csl, :].re("p (h c) -> p h c", h=8))
            k.dma("act", ba_, d["batm"][csl, :])
            beta, g = ba_[:, 0:8], ba_[:, 8:16]
            k.mm(pm[:, 0:8], trile, g)
            k.mm(pm[:, 8:16], trigt, g)
            k.mm(pm[:, 16:24], ones, g)
            k.copy(Gs, pm[:, 0:8])
            k.act(sm[:, 8:32], pm[:, 0:24], AF.Exp)
            k.tt(bg, beta, eG, ALU.mult)
            k.copy(gbc, bc8(g))
            gens = [chain(grp[hg], hg, q_, k_, kt_, vt_, z_, beta, csl) for hg in range(2)]
            while gens:
                for gn in list(gens):
                    try:
                        next(gn)
                    except StopIteration:
                        gens.remove(gn)
        k.barrier()


Prog.phase_dnprep = phase_dnprep
Prog.phase_dnscan = phase_dnscan


def emit_all(p):
    for s in range(p.nseq):
        for l in p.layers:
            p.phase_inproj(l, s)
            p.phase_dnprep(l, s)
            p.phase_dnscan(l, s)
            p.phase_nsa(l, s)
            p.phase_outproj(l, s)
            p.phase_ln(l, "ln1_g", "ln1_b", p.d["xres"])
            p.phase_ffn(l, s)
            last = (l == p.layers[-1])
            p.phase_ln(l, "ln2_g", "ln2_b", p.d["out"][s] if last else p.d["xres"])
    p.k.barrier()


def all_consts():
    c = make_consts()
    c.update(nsa_consts())
    c.update(dn_consts())
    return c


def host_params(inp):
    f = lambda a: np.ascontiguousarray(np.asarray(a, dtype=np.float32))
    lnl = lambda a: f(np.asarray(a).reshape(DEPTH, 16, 128).transpose(0, 2, 1))
    m = {
        "w_in": f(inp["w_in"]), "conv_wT": f(np.asarray(inp["dn_conv_w"]).transpose(0, 2, 1)),
        "a_log": f(inp["dn_a_log"]), "dt_bias": f(inp["dn_dt_bias"]), "norm_w": f(inp["dn_norm_w"]),
        "peT_k": f(np.asarray(inp["cmp_pe_k"]).transpose(0, 2, 1)), "w1_k": f(inp["cmp_w1_k"]), "w2_k": f(inp["cmp_w2_k"]),
        "peT_v": f(np.asarray(inp["cmp_pe_v"]).transpose(0, 2, 1)), "w1_v": f(inp["cmp_w1_v"]), "w2_v": f(inp["cmp_w2_v"]),
        "w_a": f(inp["w_branch_a"]), "w_b": f(inp["w_branch_b"]), "w_out": f(inp["w_out"]),
        "w_gate": f(inp["w_ffn_gate"]), "w_up": f(inp["w_ffn_up"]), "w_down": f(inp["w_ffn_down"]),
        "w_ple": f(inp["w_ple"]), "w_pg": f(inp["w_ple_gate"]),
        "ln1_g": lnl(inp["ln1_g"]), "ln1_b": lnl(inp["ln1_b"]), "ln2_g": lnl(inp["ln2_g"]), "ln2_b": lnl(inp["ln2_b"]),
    }
    m.update(all_consts())
    return m


def run(inp, seq_ids_per_core, trace=False):
    nseq = len(seq_ids_per_core[0])
    p = Prog(nseq=nseq)
    emit_all(p)
    shared = host_params(inp)
    x = np.asarray(inp["x"], dtype=np.float32)
    pp = np.asarray(inp["p"], dtype=np.float32)
    in_maps = []
    for ids in seq_ids_per_core:
        m = dict(shared)
        m["xin"] = np.ascontiguousarray(x[ids].transpose(0, 2, 1))
        m["pin"] = np.ascontiguousarray(pp[:, ids].transpose(0, 1, 3, 2))
        in_maps.append({n: v for n, v in m.items() if n in p.d})
    res = run_bass_kernel_spmd(p.nc, in_maps, core_ids=list(range(len(in_maps))), trace=trace)
    outs = [np.ascontiguousarray(r["out"].transpose(0, 2, 1)) for r in res.results]
    return outs, res


def kernel(**inputs):
    B = np.asarray(inputs["x"]).shape[0]
    ids = [[2 * c, 2 * c + 1] for c in range(8)]
    outs, _ = run(inputs, ids)
    out = np.empty((B, T, D), np.float32)
    for c, o in enumerate(outs):
        out[ids[c]] = o
    return out
```
